# Optimizing a Trainium2 kernel written in Bass

```python
import math
import jax, jax.numpy as jnp
from jax import lax
import numpy as np

D_MODEL = 1024
BATCH = 8
SEQ = 4096
DEPTH = 2

N_A_LAYERS = (DEPTH + 1) // 2
N_B_LAYERS = DEPTH - N_A_LAYERS
HEAD_DIM = 64
N_HEADS = D_MODEL // (2 * HEAD_DIM)
DIL_GROUPS = ((128, 1), (512, 4), (2048, 16))
N_GROUPS = len(DIL_GROUPS)
ATTN_BLOCK = 128
DIFF_V_DIM = 2 * HEAD_DIM
D_FF = ((8 * D_MODEL // 3 + 127) // 128) * 128
CONV_WIDTH = 3
NUM_BUCKETS = 32
MAX_DISTANCE = 2048
RMS_EPS = 1e-6
SUBLN_EPS = 1e-5

kernel_name = 'hybrid_dilated_diffattn_yoco'


def rms_norm(x, g, eps=RMS_EPS):
    xf = x.astype(jnp.float32)
    y = xf * lax.rsqrt(jnp.mean(xf * xf, axis=-1, keepdims=True) + eps)
    return (y * g.astype(jnp.float32)).astype(x.dtype)


def rel_bias(table, dist):
    n = jnp.maximum(dist, 0)
    max_exact = NUM_BUCKETS // 2
    nf = jnp.maximum(n, 1).astype(jnp.float32)
    large = max_exact + (jnp.log(nf / max_exact) / math.log(MAX_DISTANCE / max_exact)
                         * (NUM_BUCKETS - max_exact)).astype(jnp.int32)
    large = jnp.minimum(large, NUM_BUCKETS - 1)
    bucket = jnp.where(n < max_exact, n, large)
    return jnp.take(table, bucket, axis=1).astype(jnp.float32)


def dilated_group_attention(q, k, v, table, window, dilation):
    b, s, h, dh = q.shape
    r = dilation
    n = s // r
    wu = window // dilation
    nb = -(-n // ATTN_BLOCK)
    n_p = nb * ATTN_BLOCK

    def to_blocks(t):
        t = t.reshape(b, n, r, h, dh)
        t = jnp.pad(t, ((0, 0), (0, n_p - n), (0, 0), (0, 0), (0, 0)))
        return t.reshape(b, nb, ATTN_BLOCK, r, h, dh)

    def with_prev(t):
        prev = jnp.pad(t, ((0, 0), (1, 0), (0, 0), (0, 0), (0, 0), (0, 0)))[:, :-1]
        return jnp.concatenate([prev, t], axis=2)

    qb = to_blocks(q)
    kc = with_prev(to_blocks(k))
    vc = with_prev(to_blocks(v))

    qi = jnp.arange(ATTN_BLOCK)[:, None]
    ki = jnp.arange(2 * ATTN_BLOCK)[None, :]
    dist_u = qi + ATTN_BLOCK - ki
    band = (dist_u >= 0) & (dist_u <= wu)
    has_prev = (jnp.arange(nb)[:, None, None] > 0) | (ki[None] >= ATTN_BLOCK)
    mask = band[None] & has_prev
    bias = rel_bias(table, dist_u * r)

    logits = jnp.einsum('bnqrhd,bnkrhd->bnrhqk', qb, kc,
                        preferred_element_type=jnp.float32) * (dh ** -0.5) + bias[None, None, None]
    logits = jnp.where(mask[None, :, None, None], logits, -jnp.inf)
    m = jnp.max(logits, axis=-1, keepdims=True)
    p = jnp.exp(logits - m)
    den = jnp.sum(p, axis=-1, keepdims=True)
    o = jnp.einsum('bnrhqk,bnkrhd->bnqrhd', p / den, vc.astype(jnp.float32))
    lse = (m + jnp.log(den))[..., 0]

    o = o.reshape(b, n_p, r, h, dh)[:, :n].reshape(b, s, h, dh)
    lse = lse.transpose(0, 1, 4, 2, 3).reshape(b, n_p, r, h)[:, :n].reshape(b, s, h)
    return o, lse


def dilated_mixer(x, w_in, w_out, table):
    b, s, _ = x.shape
    proj = (x @ w_in).reshape(b, s, N_GROUPS, 3, N_HEADS, HEAD_DIM)
    outs, lses = [], []
    for g, (window, dil) in enumerate(DIL_GROUPS):
        o, lse = dilated_group_attention(proj[:, :, g, 0], proj[:, :, g, 1], proj[:, :, g, 2],
                                         table, window, dil)
        outs.append(o)
        lses.append(lse)
    alpha = jax.nn.softmax(jnp.stack(lses, axis=-1), axis=-1)
    o = jnp.einsum('gbshd,bshg->bshd', jnp.stack(outs), alpha)
    return o.reshape(b, s, N_HEADS * HEAD_DIM).astype(x.dtype) @ w_out


def diff_mixer(x, k_sh, v_sh, w_q, lq1, lk1, lq2, lk2, subln_g, w_out, table, lambda_init):
    b, s, _ = x.shape
    q = (x @ w_q).reshape(b, s, N_HEADS, 2, HEAD_DIM)
    f32 = jnp.float32
    lam = (jnp.exp(jnp.sum(lq1.astype(f32) * lk1.astype(f32)))
           - jnp.exp(jnp.sum(lq2.astype(f32) * lk2.astype(f32))) + lambda_init)
    nq = s // ATTN_BLOCK
    qb = q.reshape(b, nq, ATTN_BLOCK, N_HEADS, 2, HEAD_DIM).transpose(1, 0, 2, 3, 4, 5)
    kpos = jnp.arange(s)
    scale = HEAD_DIM ** -0.5

    def block(args):
        i, qblk = args
        qpos = i * ATTN_BLOCK + jnp.arange(ATTN_BLOCK)
        dist = qpos[:, None] - kpos[None, :]
        bias = rel_bias(table, dist)
        logits = jnp.einsum('bqhcd,bkhcd->bhcqk', qblk, k_sh,
                            preferred_element_type=f32) * scale + bias[None, :, None]
        logits = jnp.where(dist >= 0, logits, -jnp.inf)
        p = jax.nn.softmax(logits, axis=-1)
        a = p[:, :, 0] - lam * p[:, :, 1]
        return jnp.einsum('bhqk,bkhe->bqhe', a, v_sh.astype(f32))

    o = lax.map(block, (jnp.arange(nq), qb))
    o = o.transpose(1, 0, 2, 3, 4).reshape(b, s, N_HEADS, DIFF_V_DIM)
    o = rms_norm(o, subln_g, SUBLN_EPS) * (1.0 - lambda_init)
    return o.reshape(b, s, N_HEADS * DIFF_V_DIM).astype(x.dtype) @ w_out


def conv_ffn(x, w_up, conv_w, conv_b, w_down):
    s = x.shape[1]
    u = x @ w_up
    up = jnp.pad(u, ((0, 0), (CONV_WIDTH - 1, 0), (0, 0)))
    c = conv_b
    for j in range(CONV_WIDTH):
        c = c + conv_w[j] * up[:, j:j + s]
    gate, val = jnp.split(c, 2, axis=-1)
    return (jax.nn.gelu(gate, approximate=False) * val) @ w_down


def setup_inputs(seed: int = 0) -> dict:
    key = jax.random.key(seed)
    ks = jax.random.split(key, 20)
    f32 = jnp.float32
    D = D_MODEL
    qkv_a = N_GROUPS * 3 * N_HEADS * HEAD_DIM
    a_width = N_HEADS * HEAD_DIM
    qk_b = N_HEADS * 2 * HEAD_DIM
    v_b = N_HEADS * DIFF_V_DIM
    nrm = lambda k, shp: jax.random.normal(k, shp, f32)
    return {
        'x': nrm(ks[0], (BATCH, SEQ, D)),
        'rel_bias_table': 0.2 * nrm(ks[1], (N_HEADS, NUM_BUCKETS)),
        'norm_g': 1.0 + 0.05 * nrm(ks[2], (DEPTH, 4, D)),
        'w_in_a': nrm(ks[3], (N_A_LAYERS, D, qkv_a)) * D ** -0.5,
        'w_out_a': nrm(ks[4], (N_A_LAYERS, a_width, D)) * a_width ** -0.5,
        'kv_norm_g': 1.0 + 0.05 * nrm(ks[5], (D,)),
        'w_k_shared': nrm(ks[6], (D, qk_b)) * D ** -0.5,
        'w_v_shared': nrm(ks[7], (D, v_b)) * D ** -0.5,
        'w_q_b': nrm(ks[8], (N_B_LAYERS, D, qk_b)) * D ** -0.5,
        'lam_q1': 0.1 * nrm(ks[9], (N_B_LAYERS, HEAD_DIM)),
        'lam_k1': 0.1 * nrm(ks[10], (N_B_LAYERS, HEAD_DIM)),
        'lam_q2': 0.1 * nrm(ks[11], (N_B_LAYERS, HEAD_DIM)),
        'lam_k2': 0.1 * nrm(ks[12], (N_B_LAYERS, HEAD_DIM)),
        'subln_g': 1.0 + 0.05 * nrm(ks[13], (N_B_LAYERS, DIFF_V_DIM)),
        'w_out_b': nrm(ks[14], (N_B_LAYERS, v_b, D)) * v_b ** -0.5,
        'w_up': nrm(ks[15], (DEPTH, D, 2 * D_FF)) * D ** -0.5,
        'conv_w': nrm(ks[16], (DEPTH, CONV_WIDTH, 2 * D_FF)) * CONV_WIDTH ** -0.5,
        'conv_b': 0.02 * nrm(ks[17], (DEPTH, 2 * D_FF)),
        'w_down': nrm(ks[18], (DEPTH, D_FF, D)) * D_FF ** -0.5,
    }


def reference(x, rel_bias_table, norm_g, w_in_a, w_out_a, kv_norm_g, w_k_shared, w_v_shared,
              w_q_b, lam_q1, lam_k1, lam_q2, lam_k2, subln_g, w_out_b, w_up, conv_w, conv_b,
              w_down):
    b, s, _ = x.shape
    h = x
    k_shared = None
    v_shared = None
    for layer in range(DEPTH):
        g = norm_g[layer]
        if layer < N_A_LAYERS:
            mix = dilated_mixer(rms_norm(h, g[0]), w_in_a[layer], w_out_a[layer], rel_bias_table)
        else:
            j = layer - N_A_LAYERS
            lambda_init = 0.8 - 0.6 * math.exp(-0.3 * layer)
            mix = diff_mixer(rms_norm(h, g[0]), k_shared, v_shared, w_q_b[j],
                             lam_q1[j], lam_k1[j], lam_q2[j], lam_k2[j], subln_g[j],
                             w_out_b[j], rel_bias_table, lambda_init)
        h = h + rms_norm(mix, g[1])
        ff = conv_ffn(rms_norm(h, g[2]), w_up[layer], conv_w[layer], conv_b[layer], w_down[layer])
        h = h + rms_norm(ff, g[3])
        if layer == N_A_LAYERS - 1:
            kv_src = rms_norm(h, kv_norm_g)
            k_shared = (kv_src @ w_k_shared).reshape(b, s, N_HEADS, 2, HEAD_DIM)
            v_shared = (kv_src @ w_v_shared).reshape(b, s, N_HEADS, DIFF_V_DIM)
    return h
```

```python
import math
from contextlib import ExitStack

import numpy as np
import concourse.bass as bass
import concourse.mybir as mybir
from concourse.bass_utils import run_bass_kernel_spmd

F32 = mybir.dt.float32
BF16 = mybir.dt.bfloat16
AF = mybir.ActivationFunctionType
ALU = mybir.AluOpType

SEQ = 4096
D = 1024
NT = SEQ // 128
NH = 8
DIL = (1, 4, 16)
DFF = 2816
NFC = DFF // 128
NEG = -30000.0
LAMBDA_INIT = 0.8 - 0.6 * math.exp(-0.3 * 1)
TBW = 2176
TB_CONST = 1664

ENGS = ("pe", "act", "dve", "pool", "sp")


class _Op:
    __slots__ = ("eng", "fn", "deps", "idx", "dma", "sig", "sigval", "dsem", "dval")

    def __init__(self, eng, fn, dma):
        self.eng = eng
        self.fn = fn
        self.deps = []
        self.dma = dma
        self.sig = False
        self.sigval = 0
        self.dsem = None
        self.dval = 0


class Sched:
    def __init__(self, nc, es):
        self.nc = nc
        self.es = es
        self.sems = {e: es.enter_context(nc.semaphore("s_" + e)) for e in ENGS}
        self.q = {e: [] for e in ENGS}
        self.lastw = {}
        self.readers = {}
        self.sigcount = {e: 0 for e in ENGS}
        self.dma_slot = {}
        self.nsem = 0

    def _slot(self, key):
        if key not in self.dma_slot:
            self.nsem += 1
            self.dma_slot[key] = [self.es.enter_context(self.nc.semaphore("d%d" % self.nsem)), 0]
        return self.dma_slot[key]

    def op(self, eng, fn, reads=(), writes=(), dma=None):
        o = _Op(eng, fn, dma is not None)
        o.idx = len(self.q[eng])
        if dma is not None:
            s = self._slot(dma)
            s[1] += 16
            o.dsem, o.dval = s[0], s[1]
        dep = {}

        def add(d):
            if d is None or d is o:
                return
            k = ("d", id(d.dsem)) if d.dma else ("e", d.eng)
            cur = dep.get(k)
            if cur is None or (cur.dval < d.dval if d.dma else cur.idx < d.idx):
                dep[k] = d

        for b in reads:
            add(self.lastw.get(b))
        for b in writes:
            add(self.lastw.get(b))
            for r in self.readers.get(b, {}).values():
                add(r)
        o.deps = list(dep.values())
        me = ("d", id(o.dsem)) if o.dma else ("e", eng)
        for b in reads:
            self.readers.setdefault(b, {})[me] = o
        for b in writes:
            self.lastw[b] = o
            self.readers[b] = {}
        self.q[eng].append(o)
        return o

    def finalize(self):
        for e in ENGS:
            for o in self.q[e]:
                for d in o.deps:
                    if d.dma:
                        continue
                    if d.eng == "pe" and o.eng == "pe":
                        continue
                    d.sig = True
        for e in ENGS:
            c = self.sigcount[e]
            for o in self.q[e]:
                if o.sig and not o.dma:
                    c += 1
                    o.sigval = c
            self.sigcount[e] = c

    def emit_engine(self, e, h):
        known = {}
        for o in self.q[e]:
            need = {}
            for d in o.deps:
                if d.dma:
                    k = ("d", id(d.dsem))
                    if need.get(k, (None, 0))[1] < d.dval:
                        need[k] = (d.dsem, d.dval)
                else:
                    if d.eng == "pe" and e == "pe":
                        continue
                    k = ("e", d.eng)
                    if need.get(k, (None, 0))[1] < d.sigval:
                        need[k] = (self.sems[d.eng], d.sigval)
            for k, (sem, val) in need.items():
                if known.get(k, 0) >= val:
                    continue
                h.wait_ge(sem, val)
                known[k] = val
            ins = o.fn(h)
            if o.dma:
                ins.then_inc(o.dsem, 16)
            elif o.sig:
                ins.then_inc(self.sems[e], 1)

    def final_dma_waits(self, h):
        for key, (sem, cnt) in self.dma_slot.items():
            if cnt:
                h.wait_ge(sem, cnt)

    def reset_phase(self):
        self.q = {e: [] for e in ENGS}
        self.lastw = {}
        self.readers = {}


class K:
    ntag = 0

    def newtag(self):
        self.ntag += 1
        return "_p%d" % self.ntag


def emit_block(k):
    S, nc = k.S, k.nc
    S.finalize()
    with nc.Block() as block:
        @block.sync
        def _(h):
            S.emit_engine("sp", h)
            S.final_dma_waits(h)

        @block.tensor
        def _(h):
            S.emit_engine("pe", h)

        @block.scalar
        def _(h):
            S.emit_engine("act", h)

        @block.vector
        def _(h):
            S.emit_engine("dve", h)

        @block.gpsimd
        def _(h):
            S.emit_engine("pool", h)
    S.reset_phase()


def bank_bf16(bank, c):
    return bank[:, :].bitcast(BF16).rearrange("p (c t) -> p c t", c=c)


def load_cast_w(k, dst, dst_key, w2d, nchunk, ncols, slot):
    S = k.S
    wv = w2d.rearrange("(c p) n -> p c n", p=128)
    keys = []
    i = 0
    for c in range(nchunk):
        for c0 in range(0, ncols, 2048):
            c1 = min(ncols, c0 + 2048)
            key = (dst_key, c, c0)
            keys.append(key)
            S.op("pool", lambda h, c=c, c0=c0, c1=c1: h.dma_start(out=dst[:, c, c0:c1], in_=wv[:, c, c0:c1]),
                 writes=[key], dma=(slot, i % 4))
            i += 1
    return keys


def rstd_from_ss(k, ss_ap, ms_ap, rstd_ap, nh_ap, keys_ss, key_ms, key_rstd, inv_n, eps):
    S = k.S
    S.op("dve", lambda h: h.tensor_scalar(out=ms_ap, in0=ss_ap, scalar1=inv_n, scalar2=eps, op0=ALU.mult, op1=ALU.add),
         reads=keys_ss, writes=[key_ms])
    S.op("pool", lambda h: h.tensor_tensor(out=rstd_ap, in0=ms_ap, in1=nh_ap, op=ALU.pow),
         reads=[key_ms], writes=[key_rstd])


def phase1(k, groups=(0, 1, 2)):
    nc, S = k.nc, k.S
    with ExitStack() as es:
        tg = k.newtag()
        sb = lambda n, s, d: es.enter_context(nc.sbuf_tensor(n + tg, s, d))
        banks = [es.enter_context(nc.psum_tensor("bk%d" % i + tg, [128, 512], F32)) for i in range(8)]
        g0b = sb("g0b", [128, 1024], F32)
        Wg = sb("Wg", [128, 8, 1536], BF16)
        xt = [sb("xt%d" % i, [128, 1024], F32) for i in range(3)]
        junk = sb("junk", [128, 1024], BF16)
        xn = [sb("xn%d" % i, [128, 1024], BF16) for i in range(2)]
        xnT = [sb("xnT%d" % i, [128, 8, 512], BF16) for i in range(2)]
        QT = sb("QT", [128, 4, SEQ], BF16)
        KT = sb("KT", [128, 4, SEQ], BF16)
        Va = sb("Va", [128, NT, 8, 65], BF16)
        bA = sb("bA", [128, 8, 256], F32)
        mA = sb("mA", [128, 256], F32)
        Sb = [sb("Sb%d" % i, [128, 256], F32) for i in range(4)]
        PT = [sb("PT%d" % i, [128, 8, 256], BF16) for i in range(3)]
        Ost = [sb("Ost%d" % i, [128, 520], F32) for i in range(2)]
        st_ss = sb("st_ss", [128, 128], F32)
        st_ms = sb("st_ms", [128, 128], F32)
        st_rs = sb("st_rs", [128, 128], F32)

        S.op("sp", lambda h: h.dma_start(out=g0b[:], in_=k.d["norm_g"][0, 0, :].partition_broadcast(128)), writes=["g0b"], dma="g0b")
        S.op("sp", lambda h: h.dma_start(out=mA[:], in_=k.d["maskA"]), writes=["mA"], dma="mA")
        S.op("pool", lambda h: h.memset(Va[:, :, :, 64:65], 1.0), writes=["Va1"])

        x = k.d["x"]
        tcount = 0
        for g in groups:
            r = DIL[g]
            nb = NT // r
            xr = x.rearrange("(m r) d -> r m d", r=r)
            Ur = k.d["U%d" % g].rearrange("(m r) d -> r m d", r=r)
            wkeys = load_cast_w(k, Wg, "Wg", k.d["w_in_a"][0, :, g * 1536:(g + 1) * 1536], 8, 1536, "Wg")
            S.op("sp", lambda h, g=g: h.dma_start(out=bA[:], in_=k.d["biasA"][g].rearrange("h p q -> p h q")),
                 writes=["bA"], dma="bA")
            for hh in range(8):
                S.op("pool", lambda h, hh=hh: h.tensor_tensor(out=bA[:, hh, :], in0=bA[:, hh, :], in1=mA[:], op=ALU.add),
                     reads=["bA", "mA"], writes=["bA"])

            def stageA_load_norm(st):
                nonlocal tcount
                for t4 in range(4):
                    T = st * 4 + t4
                    c, n = divmod(T, nb)
                    sl = tcount % 3
                    col = tcount % 128
                    tcount += 1
                    S.op("sp", lambda h, c=c, n=n, sl=sl, xr=xr: h.dma_start(out=xt[sl][:], in_=xr[c, n * 128:(n + 1) * 128, :]),
                         writes=[("xt", sl)], dma=("xt", sl))
                    S.op("act", lambda h, sl=sl, col=col: h.activation(out=junk[:], in_=xt[sl][:], func=AF.Square,
                                                                    accum_out=st_ss[:, col:col + 1]),
                         reads=[("xt", sl)], writes=[("ss", col)])
                    rstd_from_ss(k, st_ss[:, col:col + 1], st_ms[:, col:col + 1], st_rs[:, col:col + 1], k.nhalf[:, 0:1],
                                 [("ss", col)], ("ms", col), ("rs", col), 1.0 / D, 1e-6)
                    xs = T % 2
                    S.op("dve", lambda h, sl=sl, col=col, xs=xs: h.scalar_tensor_tensor(
                        out=xn[xs][:], in0=xt[sl][:], scalar=st_rs[:, col:col + 1], in1=g0b[:], op0=ALU.mult, op1=ALU.mult),
                        reads=[("xt", sl), ("rs", col), "g0b"], writes=[("xn", xs)])
                    pb = banks[T % 2]
                    for dc in range(8):
                        S.op("pe", lambda h, dc=dc, xs=xs, pb=pb: h.transpose(
                            out=bank_bf16(pb, 8)[:, dc, :], in_=xn[xs][:, dc * 128:(dc + 1) * 128], identity=k.identb[:]),
                            reads=[("xn", xs), "identb"], writes=[("bank", T % 2)])
                    S.op("act", lambda h, st=st, t4=t4, pb=pb: h.copy(out=xnT[st % 2][:, :, t4 * 128:(t4 + 1) * 128], in_=bank_bf16(pb, 8)),
                         reads=[], writes=[("bank", T % 2), ("xnT", st % 2, t4)])

            pcount = [0]

            def stageA_proj(st):
                xk = [("xnT", st % 2, t4) for t4 in range(4)]
                for fc in range(8):
                    bi = 2 + pcount[0] % 3
                    pcount[0] += 1
                    for dc in range(8):
                        S.op("pe", lambda h, fc=fc, dc=dc, bi=bi: h.matmul(
                            banks[bi][:, :], lhsT=Wg[:, dc, fc * 128:(fc + 1) * 128], rhs=xnT[st % 2][:, dc, :],
                            start=(dc == 0), stop=(dc == 7)),
                            reads=xk + wkeys, writes=[("bank", bi)])
                    if fc < 4:
                        S.op("act", lambda h, fc=fc, bi=bi: h.activation(out=QT[:, fc, st * 512:(st + 1) * 512], in_=banks[bi][:, :],
                                                                   func=AF.Copy, scale=0.125),
                             writes=[("bank", bi), ("QT", st)])
                    else:
                        S.op("dve", lambda h, fc=fc, bi=bi: h.tensor_copy(out=KT[:, fc - 4, st * 512:(st + 1) * 512], in_=banks[bi][:, :]),
                             writes=[("bank", bi), ("KT", st)])
                for t4 in range(4):
                    T = st * 4 + t4
                    bi = 2 + pcount[0] % 3
                    pcount[0] += 1
                    for dc in range(8):
                        S.op("pe", lambda h, t4=t4, dc=dc, bi=bi: h.matmul(
                            banks[bi][:, :], lhsT=xnT[st % 2][:, dc, t4 * 128:(t4 + 1) * 128], rhs=Wg[:, dc, 1024:1536],
                            start=(dc == 0), stop=(dc == 7)),
                            reads=[("xnT", st % 2, t4)] + wkeys, writes=[("bank", bi)])
                    eng = "act" if t4 % 2 == 0 else "dve"
                    if eng == "act":
                        S.op("act", lambda h, T=T, bi=bi: h.copy(out=Va[:, T, :, 0:64], in_=banks[bi][:, :].rearrange("p (h e) -> p h e", h=8)),
                             writes=[("bank", bi), ("Va", T)])
                    else:
                        S.op("dve", lambda h, T=T, bi=bi: h.tensor_copy(out=Va[:, T, :, 0:64], in_=banks[bi][:, :].rearrange("p (h e) -> p h e", h=8)),
                             writes=[("bank", bi), ("Va", T)])

            stageA_load_norm(0)
            for st in range(8):
                if st + 1 < 8:
                    stageA_load_norm(st + 1)
                stageA_proj(st)

            scount = [0]

            def stageB_scores(T):
                c, j = divmod(T, nb)
                ncols = 256 if j < nb - 1 else 128
                st_list = sorted(set([(T * 128) // 512, (T * 128 + ncols - 1) // 512]))
                for hd in range(8):
                    hr = slice(64 * (hd % 2), 64 * (hd % 2) + 64)
                    hp = hd // 2
                    bi = 2 + scount[0] % 3
                    si = scount[0] % 4
                    scount[0] += 1
                    S.op("pe", lambda h, T=T, hr=hr, hp=hp, bi=bi, ncols=ncols: h.matmul(
                        banks[bi][:, 0:ncols], lhsT=KT[hr, hp, T * 128:(T + 1) * 128], rhs=QT[hr, hp, T * 128:T * 128 + ncols],
                        start=True, stop=True),
                        reads=[("KT", T // 4)] + [("QT", s_) for s_ in st_list], writes=[("bank", bi)])
                    S.op("dve", lambda h, hd=hd, bi=bi, si=si, ncols=ncols: h.tensor_tensor(
                        out=Sb[si][:, 0:ncols], in0=banks[bi][:, 0:ncols], in1=bA[:, hd, 0:ncols], op=ALU.add),
                        reads=["bA"], writes=[("bank", bi), ("Sb", si)])
                    S.op("act", lambda h, T=T, hd=hd, si=si, ncols=ncols: h.activation(
                        out=PT[T % 3][:, hd, 0:ncols], in_=Sb[si][:, 0:ncols], func=AF.Exp),
                        reads=[("Sb", si)], writes=[("PT", T % 3, hd)])

            def stageB_pv(T):
                c, j = divmod(T, nb)
                o0 = 0 if T % 2 == 0 else 1
                obanks = (0, 1) if o0 == 0 else (5, 6)
                for hd in range(8):
                    bi = obanks[hd // 4]
                    oap = banks[bi][:, :].rearrange("p (h e) -> p h e", h=4)[:, hd % 4, 0:65]
                    if j > 0:
                        S.op("pe", lambda h, T=T, hd=hd, oap=oap: h.matmul(
                            oap, lhsT=PT[(T - 1) % 3][:, hd, 128:256], rhs=Va[:, T - 1, hd, :], start=True, stop=False),
                            reads=[("PT", (T - 1) % 3, hd), ("Va", T - 1), "Va1"], writes=[("bank", bi)])
                    S.op("pe", lambda h, T=T, hd=hd, oap=oap, j=j: h.matmul(
                        oap, lhsT=PT[T % 3][:, hd, 0:128], rhs=Va[:, T, hd, :], start=(j == 0), stop=True),
                        reads=[("PT", T % 3, hd), ("Va", T), "Va1"], writes=[("bank", bi)])
                osl = T % 2
                for half in range(2):
                    bi = obanks[half]
                    S.op("act", lambda h, half=half, bi=bi, osl=osl: h.copy(
                        out=Ost[osl][:, half * 260:(half + 1) * 260].rearrange("p (h e) -> p h e", h=4),
                        in_=banks[bi][:, :].rearrange("p (h e) -> p h e", h=4)[:, :, 0:65]),
                        writes=[("bank", bi), ("Ost", osl, half)])
                S.op("sp", lambda h, c=c, j=j, osl=osl, Ur=Ur: h.dma_start(out=Ur[c, j * 128:(j + 1) * 128, :], in_=Ost[osl][:]),
                     reads=[("Ost", osl, 0), ("Ost", osl, 1)], dma=("Ost", osl))

            for T in range(NT):
                stageB_scores(T)
                if T >= 1:
                    stageB_pv(T - 1)
            stageB_pv(NT - 1)
        emit_block(k)


def postnorm_residual(k, pbanks, pkeys, gb, gkey, hres, hkey, tmp, tmpkey, st, col):
    S = k.S
    ss, ms, rs = st
    for half in range(2):
        S.op("act", lambda h, half=half: h.activation(out=k.junk[:, 0:512], in_=pbanks[half][:, :], func=AF.Square,
                                                      accum_out=ss[:, 2 * col + half:2 * col + half + 1]),
             writes=[pkeys[half], ("ss", 2 * col + half)])
    S.op("dve", lambda h: h.tensor_tensor(out=ms[:, 2 * col:2 * col + 1], in0=ss[:, 2 * col:2 * col + 1], in1=ss[:, 2 * col + 1:2 * col + 2], op=ALU.add),
         reads=[("ss", 2 * col), ("ss", 2 * col + 1)], writes=[("ms", 2 * col)])
    rstd_from_ss(k, ms[:, 2 * col:2 * col + 1], ms[:, 2 * col + 1:2 * col + 2], rs[:, col:col + 1], k.nhalf[:, 0:1],
                 [("ms", 2 * col)], ("ms", 2 * col + 1), ("rs", col), 1.0 / D, 1e-6)
    for half in range(2):
        S.op("dve", lambda h, half=half: h.scalar_tensor_tensor(
            out=tmp[:, half * 512:(half + 1) * 512], in0=pbanks[half][:, :], scalar=rs[:, col:col + 1],
            in1=gb[:, half * 512:(half + 1) * 512], op0=ALU.mult, op1=ALU.mult),
            reads=[("rs", col), gkey], writes=[pkeys[half], (tmpkey, half)])
    S.op("pool", lambda h: h.tensor_tensor(out=hres, in0=hres, in1=tmp[:, :], op=ALU.add),
         reads=[(tmpkey, 0), (tmpkey, 1)], writes=[hkey])


def phase_outproj(k, layer):
    nc, S = k.nc, k.S
    with ExitStack() as es:
        tg = k.newtag()
        sb = lambda n, s, d: es.enter_context(nc.sbuf_tensor(n + tg, s, d))
        banks = [es.enter_context(nc.psum_tensor("bk%d" % i + tg, [128, 512], F32)) for i in range(8)]
        nkc = 4 if layer == 0 else 8
        Wo = sb("Wo", [128, nkc, 1024], BF16)
        g1b = sb("g1b", [128, 1024], F32)
        hres = [sb("hres%d" % i, [128, 1024], F32) for i in range(3)]
        tmp = [sb("tmp%d" % i, [128, 1024], F32) for i in range(2)]
        oT = [sb("oT%d" % i, [128, nkc, 128], BF16) for i in range(2)]
        ss = sb("ss", [128, 128], F32)
        ms = sb("ms", [128, 128], F32)
        rs = sb("rs", [128, 64], F32)
        if layer == 0:
            Ul = [[sb("Ul%d_%d" % (i, g), [128, 520], F32) for g in range(3)] for i in range(2)]
            Us = [sb("Us%d" % i, [128, 520], F32) for i in range(2)]
            rden = [sb("rden%d" % i, [128, 8], F32) for i in range(2)]
            obf = [sb("obf%d" % i, [128, 512], BF16) for i in range(2)]
            wsrc = k.d["w_out_a"][0]
            hin = k.d["x"]
            hout = k.d["H1"]
        else:
            obf = [sb("obf%d" % i, [128, 1024], BF16) for i in range(3)]
            wsrc = k.d["w_out_b"][0]
            hin = k.d["H2"]
            hout = k.d["H1"]
        wkeys = load_cast_w(k, Wo, "Wo", wsrc, nkc, 1024, "Wo")
        S.op("sp", lambda h: h.dma_start(out=g1b[:], in_=k.d["norm_g"][layer, 1, :].partition_broadcast(128)), writes=["g1b"], dma="g1b")

        def load(T):
            hs = T % 3
            S.op("sp", lambda h: h.dma_start(out=hres[hs][:], in_=hin[T * 128:(T + 1) * 128, :]), writes=[("hres", hs)], dma=("hres", hs))
            if layer == 0:
                us = T % 2
                for g in range(3):
                    S.op("sp", lambda h, g=g: h.dma_start(out=Ul[us][g][:], in_=k.d["U%d" % g][T * 128:(T + 1) * 128, :]),
                         writes=[("Ul", us, g)], dma=("Ul", us, g))
            else:
                os_ = T % 3
                S.op("sp", lambda h: h.dma_start(out=obf[os_][:], in_=k.d["O1"][T * 128:(T + 1) * 128, :]), writes=[("obf", os_)], dma=("obf", os_))

        def front(T):
            if layer == 0:
                us = T % 2
                S.op("dve", lambda h: h.tensor_tensor(out=Us[us][:], in0=Ul[us][0][:], in1=Ul[us][1][:], op=ALU.add),
                     reads=[("Ul", us, 0), ("Ul", us, 1)], writes=[("Us", us)])
                S.op("dve", lambda h: h.tensor_tensor(out=Us[us][:], in0=Us[us][:], in1=Ul[us][2][:], op=ALU.add),
                     reads=[("Ul", us, 2)], writes=[("Us", us)])
                usv = Us[us][:, :].rearrange("p (h e) -> p h e", h=8)
                S.op("dve", lambda h: h.reciprocal(out=rden[us][:, :], in_=usv[:, :, 64]), reads=[("Us", us)], writes=[("rden", us)])
                S.op("dve", lambda h: h.tensor_tensor(out=obf[us][:, :].rearrange("p (h e) -> p h e", h=8), in0=usv[:, :, 0:64],
                                                      in1=rden[us][:, :].unsqueeze(2).to_broadcast([128, 8, 64]), op=ALU.mult),
                     reads=[("Us", us), ("rden", us)], writes=[("obf", us)])
                ok = ("obf", us)
                osrc = obf[us]
            else:
                ok = ("obf", T % 3)
                osrc = obf[T % 3]
            bi = T % 2
            for kc in range(nkc):
                S.op("pe", lambda h, kc=kc: h.transpose(out=bank_bf16(banks[bi], 8)[:, kc, :], in_=osrc[:, kc * 128:(kc + 1) * 128], identity=k.identb[:]),
                     reads=[ok, "identb"], writes=[("bank", bi)])
            S.op("act", lambda h: h.copy(out=oT[T % 2][:, :, :], in_=bank_bf16(banks[bi], 8)[:, 0:nkc, :]),
                 writes=[("bank", bi), ("oT", T % 2)])

        def back(T):
            pb = (2, 3) if T % 2 == 0 else (4, 5)
            for half in range(2):
                for kc in range(nkc):
                    S.op("pe", lambda h, half=half, kc=kc: h.matmul(
                        banks[pb[half]][:, :], lhsT=oT[T % 2][:, kc, :], rhs=Wo[:, kc, half * 512:(half + 1) * 512],
                        start=(kc == 0), stop=(kc == nkc - 1)),
                        reads=[("oT", T % 2)] + wkeys, writes=[("bank", pb[half])])
            hs = T % 3
            postnorm_residual(k, [banks[pb[0]], banks[pb[1]]], [("bank", pb[0]), ("bank", pb[1])], g1b, "g1b",
                              hres[hs][:, :], ("hres", hs), tmp[T % 2], ("tmp", T % 2), (ss, ms, rs), T % 64)
            S.op("sp", lambda h: h.dma_start(out=hout[T * 128:(T + 1) * 128, :], in_=hres[hs][:]), reads=[("hres", hs)], dma=("hout", hs))

        load(0)
        load(1)
        front(0)
        for T in range(NT):
            if T + 2 < NT:
                load(T + 2)
            if T + 1 < NT:
                front(T + 1)
            back(T)
        emit_block(k)


def phase_ffn(k, layer, hin, hout):
    nc, S = k.nc, k.S
    NS = SEQ // 256
    with ExitStack() as es:
        tg = k.newtag()
        sb = lambda n, s, d: es.enter_context(nc.sbuf_tensor(n + tg, s, d))
        banks = [es.enter_context(nc.psum_tensor("bk%d" % i + tg, [128, 512], F32)) for i in range(8)]
        Wup = sb("Wup", [128, 8, 2 * DFF], BF16)
        Wdn = sb("Wdn", [128, NFC, 1024], BF16)
        g2b = sb("g2b", [128, 1024], F32)
        g3b = sb("g3b", [128, 1024], F32)
        cp = sb("cp", [128, 2 * NFC, 4], F32)
        carry = sb("carry", [128, 2 * NFC, 2], F32)
        hres = [sb("hres%d" % i, [128, 2, 1024], F32) for i in range(2)]
        xn = [sb("xn%d" % i, [128, 1024], BF16) for i in range(2)]
        xnT = sb("xnT", [128, 8, 256], BF16)
        Ub = [sb("Ub%d" % i, [128, 258], F32) for i in range(4)]
        Cb = [sb("Cb%d" % i, [128, 256], F32) for i in range(4)]
        Gb = [sb("Gb%d" % i, [128, 256], F32) for i in range(2)]
        gT = sb("gT", [128, NFC, 256], BF16)
        tmp = [sb("tmp%d" % i, [128, 1024], F32) for i in range(2)]
        ss = sb("ss", [128, 128], F32)
        ms = sb("ms", [128, 128], F32)
        rs = sb("rs", [128, 64], F32)
        ss2 = sb("ss2", [128, 64], F32)
        ms2 = sb("ms2", [128, 64], F32)
        rs2 = sb("rs2", [128, 64], F32)

        wup_keys = load_cast_w(k, Wup, "Wup", k.d["w_up"][layer], 8, 2 * DFF, "Wup")
        wdn_keys = load_cast_w(k, Wdn, "Wdn", k.d["w_down"][layer], NFC, 1024, "Wdn")
        S.op("sp", lambda h: h.dma_start(out=g2b[:], in_=k.d["norm_g"][layer, 2, :].partition_broadcast(128)), writes=["g2b"], dma="g2b")
        S.op("sp", lambda h: h.dma_start(out=g3b[:], in_=k.d["norm_g"][layer, 3, :].partition_broadcast(128)), writes=["g3b"], dma="g3b")
        S.op("sp", lambda h: h.dma_start(out=cp[:], in_=k.d["convp"][layer]), writes=["cp"], dma="cp")
        S.op("pool", lambda h: h.memset(carry[:], 0.0), writes=[("carry", fc) for fc in range(2 * NFC)])

        def load(s):
            hs = s % 2
            for tt in range(2):
                T = s * 2 + tt
                S.op("sp", lambda h, tt=tt, T=T: h.dma_start(out=hres[hs][:, tt, :], in_=hin[T * 128:(T + 1) * 128, :]),
                     writes=[("hres", hs, tt)], dma=("hres", hs, tt))

        def prenorm(s):
            hs = s % 2
            for tt in range(2):
                T = s * 2 + tt
                col = T % 64
                S.op("act", lambda h, tt=tt, col=col: h.activation(out=k.junk[:], in_=hres[hs][:, tt, :], func=AF.Square,
                                                                   accum_out=ss2[:, col:col + 1]),
                     reads=[("hres", hs, tt)], writes=[("ss2", col)])
                rstd_from_ss(k, ss2[:, col:col + 1], ms2[:, col:col + 1], rs2[:, col:col + 1], k.nhalf[:, 0:1],
                             [("ss2", col)], ("ms2", col), ("rs2", col), 1.0 / D, 1e-6)
                S.op("dve", lambda h, tt=tt, col=col: h.scalar_tensor_tensor(
                    out=xn[tt][:], in0=hres[hs][:, tt, :], scalar=rs2[:, col:col + 1], in1=g2b[:], op0=ALU.mult, op1=ALU.mult),
                    reads=[("hres", hs, tt), ("rs2", col), "g2b"], writes=[("xn", tt)])
                for dc in range(8):
                    S.op("pe", lambda h, tt=tt, dc=dc: h.transpose(out=bank_bf16(banks[0], 8)[:, dc, :], in_=xn[tt][:, dc * 128:(dc + 1) * 128],
                                                                  identity=k.identb[:]),
                         reads=[("xn", tt), "identb"], writes=[("bank", 0)])
                S.op("act", lambda h, tt=tt: h.copy(out=xnT[:, :, tt * 128:(tt + 1) * 128], in_=bank_bf16(banks[0], 8)),
                     writes=[("bank", 0), ("xnT", tt)])

        ucount = [0]

        def up(s):
            for i in range(NFC):
                for which in range(2):
                    fc = i + which * NFC
                    bi = 1 + ucount[0] % 3
                    ui = ucount[0] % 4
                    ucount[0] += 1
                    for dc in range(8):
                        S.op("pe", lambda h, fc=fc, dc=dc, bi=bi: h.matmul(
                            banks[bi][:, 0:256], lhsT=Wup[:, dc, fc * 128:(fc + 1) * 128], rhs=xnT[:, dc, :], start=(dc == 0), stop=(dc == 7)),
                            reads=[("xnT", 0), ("xnT", 1)] + wup_keys, writes=[("bank", bi)])
                    S.op("pool", lambda h, fc=fc, ui=ui: h.tensor_copy(out=Ub[ui][:, 0:2], in_=carry[:, fc, :]),
                         reads=[("carry", fc)], writes=[("Ub", ui, "c")])
                    S.op("act", lambda h, bi=bi, ui=ui: h.copy(out=Ub[ui][:, 2:258], in_=banks[bi][:, 0:256]),
                         writes=[("bank", bi), ("Ub", ui, "m")])
                    S.op("pool", lambda h, fc=fc, ui=ui: h.tensor_copy(out=carry[:, fc, :], in_=Ub[ui][:, 256:258]),
                         reads=[("Ub", ui, "m")], writes=[("carry", fc)])
                    S.op("act", lambda h, fc=fc, ui=ui: h.activation(out=Cb[ui][:, :], in_=Ub[ui][:, 2:258], func=AF.Identity,
                                                                 scale=cp[:, fc, 2:3], bias=cp[:, fc, 3:4]),
                         reads=[("Ub", ui, "m"), "cp"], writes=[("Cb", ui)])
                    S.op("dve", lambda h, fc=fc, ui=ui: h.scalar_tensor_tensor(
                        out=Cb[ui][:, :], in0=Ub[ui][:, 1:257], scalar=cp[:, fc, 1:2], in1=Cb[ui][:, :], op0=ALU.mult, op1=ALU.add),
                        reads=[("Ub", ui, "m"), ("Ub", ui, "c"), "cp"], writes=[("Cb", ui)])
                    S.op("dve", lambda h, fc=fc, ui=ui: h.scalar_tensor_tensor(
                        out=Cb[ui][:, :], in0=Ub[ui][:, 0:256], scalar=cp[:, fc, 0:1], in1=Cb[ui][:, :], op0=ALU.mult, op1=ALU.add),
                        reads=[("Ub", ui, "m"), ("Ub", ui, "c"), "cp"], writes=[("Cb", ui)])
                    if which == 0:
                        gi = i % 2
                        ug = ui
                        S.op("act", lambda h, ui=ui, gi=gi: h.activation(out=Gb[gi][:, :], in_=Cb[ui][:, :], func=AF.Gelu),
                             reads=[("Cb", ui)], writes=[("Gb", gi)])
                    else:
                        S.op("dve", lambda h, i=i, ui=ui, gi=gi: h.tensor_tensor(out=gT[:, i, :], in0=Gb[gi][:, :], in1=Cb[ui][:, :], op=ALU.mult),
                             reads=[("Gb", gi), ("Cb", ui)], writes=[("gT", i)])

        def down(s):
            hs = s % 2
            gkeys = [("gT", i) for i in range(NFC)]
            for tt in range(2):
                T = s * 2 + tt
                pb = (4, 5) if tt == 0 else (6, 7)
                for half in range(2):
                    for i in range(NFC):
                        S.op("pe", lambda h, tt=tt, half=half, i=i, pb=pb: h.matmul(
                            banks[pb[half]][:, :], lhsT=gT[:, i, tt * 128:(tt + 1) * 128], rhs=Wdn[:, i, half * 512:(half + 1) * 512],
                            start=(i == 0), stop=(i == NFC - 1)),
                            reads=gkeys + wdn_keys, writes=[("bank", pb[half])])
                postnorm_residual(k, [banks[pb[0]], banks[pb[1]]], [("bank", pb[0]), ("bank", pb[1])], g3b, "g3b",
                                  hres[hs][:, tt, :], ("hres", hs, tt), tmp[tt], ("tmp", tt), (ss, ms, rs), T % 64)
                S.op("sp", lambda h, tt=tt, T=T: h.dma_start(out=hout[T * 128:(T + 1) * 128, :], in_=hres[hs][:, tt, :]),
                     reads=[("hres", hs, tt)], dma=("hout", hs, tt))

        load(0)
        for s in range(NS):
            if s + 1 < NS:
                load(s + 1)
            prenorm(s)
            up(s)
            down(s)
        emit_block(k)


def phase3(k):
    nc, S = k.nc, k.S
    with ExitStack() as es:
        tg = k.newtag()
        sb = lambda n, s, d: es.enter_context(nc.sbuf_tensor(n + tg, s, d))
        banks = [es.enter_context(nc.psum_tensor("bk%d" % i + tg, [128, 512], F32)) for i in range(8)]
        Wq = sb("Wq", [128, 8, 1024], BF16)
        Wk = sb("Wk", [128, 8, 1024], BF16)
        Wv = sb("Wv", [128, 8, 1024], BF16)
        gqb = sb("gqb", [128, 1024], F32)
        gkb = sb("gkb", [128, 1024], F32)
        ht = [sb("ht%d" % i, [128, 1024], F32) for i in range(3)]
        xq = [sb("xq%d" % i, [128, 1024], BF16) for i in range(2)]
        xk = [sb("xk%d" % i, [128, 1024], BF16) for i in range(2)]
        xqT = [sb("xqT%d" % i, [128, 8, 512], BF16) for i in range(2)]
        xkT = [sb("xkT%d" % i, [128, 8, 512], BF16) for i in range(2)]
        Qs = [sb("Qs%d" % i, [128, 512], BF16) for i in range(4)]
        Vs = [sb("Vs%d" % i, [128, 1024], BF16) for i in range(2)]
        ss = sb("ss", [128, 128], F32)
        ms = sb("ms", [128, 128], F32)
        rs = sb("rs", [128, 128], F32)
        wq_keys = load_cast_w(k, Wq, "Wq", k.d["w_q_b"][0], 8, 1024, "Wq")
        wk_keys = load_cast_w(k, Wk, "Wk", k.d["w_k_shared"], 8, 1024, "Wk")
        wv_keys = load_cast_w(k, Wv, "Wv", k.d["w_v_shared"], 8, 1024, "Wv")
        S.op("sp", lambda h: h.dma_start(out=gqb[:], in_=k.d["norm_g"][1, 0, :].partition_broadcast(128)), writes=["gqb"], dma="gqb")
        S.op("sp", lambda h: h.dma_start(out=gkb[:], in_=k.d["kv_norm_g"].partition_broadcast(128)), writes=["gkb"], dma="gkb")
        hin = k.d["H2"]

        def front(st):
            for t4 in range(4):
                T = st * 4 + t4
                sl = T % 3
                col = T % 128
                S.op("sp", lambda h, T=T, sl=sl: h.dma_start(out=ht[sl][:], in_=hin[T * 128:(T + 1) * 128, :]), writes=[("ht", sl)], dma=("ht", sl))
                S.op("act", lambda h, sl=sl, col=col: h.activation(out=k.junk[:], in_=ht[sl][:], func=AF.Square, accum_out=ss[:, col:col + 1]),
                     reads=[("ht", sl)], writes=[("ss", col)])
                rstd_from_ss(k, ss[:, col:col + 1], ms[:, col:col + 1], rs[:, col:col + 1], k.nhalf[:, 0:1],
                             [("ss", col)], ("ms", col), ("rs", col), 1.0 / D, 1e-6)
                xs = T % 2
                for (xb, gb, gk, nm) in ((xq, gqb, "gqb", "xq"), (xk, gkb, "gkb", "xk")):
                    S.op("dve", lambda h, sl=sl, col=col, xb=xb, gb=gb, xs=xs: h.scalar_tensor_tensor(
                        out=xb[xs][:], in0=ht[sl][:], scalar=rs[:, col:col + 1], in1=gb[:], op0=ALU.mult, op1=ALU.mult),
                        reads=[("ht", sl), ("rs", col), gk], writes=[(nm, xs)])
                for (xb, xT, nm, bi) in ((xq, xqT, "xq", 0), (xk, xkT, "xk", 1)):
                    for dc in range(8):
                        S.op("pe", lambda h, dc=dc, xb=xb, bi=bi, xs=xs: h.transpose(out=bank_bf16(banks[bi], 8)[:, dc, :], in_=xb[xs][:, dc * 128:(dc + 1) * 128],
                                                                            identity=k.identb[:]),
                             reads=[(nm, xs), "identb"], writes=[("bank", bi)])
                    S.op("act", lambda h, t4=t4, xT=xT, bi=bi: h.copy(out=xT[st % 2][:, :, t4 * 128:(t4 + 1) * 128], in_=bank_bf16(banks[bi], 8)),
                         writes=[("bank", bi), (nm + "T", st % 2, t4)])

        pc = [0]

        def proj(st):
            for (W, wkeys, xT, nm, dst, scale) in ((Wq, wq_keys, xqT, "xq", k.d["QT1"], 0.125), (Wk, wk_keys, xkT, "xk", k.d["KT1"], None)):
                xkeys = [(nm + "T", st % 2, t4) for t4 in range(4)]
                for hd in range(8):
                    bi = 2 + pc[0] % 4
                    qi = pc[0] % 4
                    pc[0] += 1
                    for dc in range(8):
                        S.op("pe", lambda h, hd=hd, dc=dc, bi=bi, W=W, xT=xT: h.matmul(
                            banks[bi][:, :], lhsT=W[:, dc, hd * 128:(hd + 1) * 128], rhs=xT[st % 2][:, dc, :], start=(dc == 0), stop=(dc == 7)),
                            reads=xkeys + wkeys, writes=[("bank", bi)])
                    if scale is not None:
                        S.op("act", lambda h, bi=bi, qi=qi: h.activation(out=Qs[qi][:, :], in_=banks[bi][:, :], func=AF.Copy, scale=0.125),
                             writes=[("bank", bi), ("Qs", qi)])
                    else:
                        S.op("dve", lambda h, bi=bi, qi=qi: h.tensor_copy(out=Qs[qi][:, :], in_=banks[bi][:, :]),
                             writes=[("bank", bi), ("Qs", qi)])
                    S.op("sp", lambda h, hd=hd, qi=qi, dst=dst: h.dma_start(out=dst[hd, :, st * 512:(st + 1) * 512], in_=Qs[qi][:, :]),
                         reads=[("Qs", qi)], dma=("Qs", qi))
            for t4 in range(4):
                T = st * 4 + t4
                vs = T % 2
                for half in range(2):
                    bi = 2 + pc[0] % 4
                    pc[0] += 1
                    for dc in range(8):
                        S.op("pe", lambda h, t4=t4, half=half, dc=dc, bi=bi: h.matmul(
                            banks[bi][:, :], lhsT=xkT[st % 2][:, dc, t4 * 128:(t4 + 1) * 128], rhs=Wv[:, dc, half * 512:(half + 1) * 512],
                            start=(dc == 0), stop=(dc == 7)),
                            reads=[("xkT", st % 2, t4)] + wv_keys, writes=[("bank", bi)])
                    if half == 0:
                        S.op("act", lambda h, bi=bi, vs=vs: h.copy(out=Vs[vs][:, 0:512], in_=banks[bi][:, :]), writes=[("bank", bi), ("Vs", vs, 0)])
                    else:
                        S.op("dve", lambda h, bi=bi, vs=vs: h.tensor_copy(out=Vs[vs][:, 512:1024], in_=banks[bi][:, :]), writes=[("bank", bi), ("Vs", vs, 1)])
                S.op("sp", lambda h, T=T, vs=vs: h.dma_start(out=k.d["V1"][T * 128:(T + 1) * 128, :], in_=Vs[vs][:, :]),
                     reads=[("Vs", vs, 0), ("Vs", vs, 1)], dma=("Vs", vs))

        front(0)
        for st in range(8):
            if st + 1 < 8:
                front(st + 1)
            proj(st)
        emit_block(k)


def phase4(k, heads=range(8)):
    nc, S = k.nc, k.S
    with ExitStack() as es:
        tg = k.newtag()
        sb = lambda n, s, d: es.enter_context(nc.sbuf_tensor(n + tg, s, d))
        banks = [es.enter_context(nc.psum_tensor("bk%d" % i + tg, [128, 512], F32)) for i in range(8)]
        KTh = [sb("KTh%d" % i, [128, SEQ], BF16) for i in range(2)]
        QTh = [sb("QTh%d" % i, [128, SEQ], BF16) for i in range(2)]
        Vh = [sb("Vh%d" % i, [128, NT, 129], BF16) for i in range(2)]
        Tb = [sb("Tb%d" % i, [128, TBW], F32) for i in range(2)]
        mB = sb("mB", [128, 128], F32)
        Sb = [sb("Sb%d" % i, [128, 512], F32) for i in range(3)]
        PT = [sb("PT%d" % i, [128, 512], BF16) for i in range(3)]
        Ost = [[sb("Ost%d_%d" % (i, c), [128, 4, 129], F32) for c in range(2)] for i in range(2)]
        Oh = [sb("Oh%d" % i, [128, NT, 128], BF16) for i in range(2)]
        lv = [sb("lv%d" % i, [128, 64], F32) for i in range(4)]
        lsum = sb("lsum", [128, 4], F32)
        lam = sb("lam", [128, 1], F32)
        sgb = sb("sgb", [128, 128], F32)
        rd = sb("rd", [128, 64, 8], F32)
        av = [sb("av%d" % i, [128, 128], F32) for i in range(2)]
        tv = [sb("tv%d" % i, [128, 128], F32) for i in range(2)]
        st1 = sb("st1", [128, 128], F32)
        st2 = sb("st2", [128, 128], F32)
        st3 = sb("st3", [128, 128], F32)

        for i, nm in enumerate(("lam_q1", "lam_k1", "lam_q2", "lam_k2")):
            S.op("sp", lambda h, i=i, nm=nm: h.dma_start(out=lv[i][:], in_=k.d[nm][0, :].partition_broadcast(128)), writes=[("lv", i)], dma=("lv", i))
        S.op("sp", lambda h: h.dma_start(out=sgb[:], in_=k.d["subln_g"][0, :].partition_broadcast(128)), writes=["sgb"], dma="sgb")
        S.op("sp", lambda h: h.dma_start(out=mB[:], in_=k.d["maskB"]), writes=["mB"], dma="mB")
        S.op("dve", lambda h: h.tensor_scalar(out=sgb[:], in0=sgb[:], scalar1=1.0 - LAMBDA_INIT, scalar2=None, op0=ALU.mult),
             reads=["sgb"], writes=["sgb"])
        for i in range(2):
            S.op("dve", lambda h, i=i: h.scalar_tensor_tensor(out=k.junk[:, 0:64], in0=lv[2 * i][:], scalar=1.0, in1=lv[2 * i + 1][:],
                                                            op0=ALU.mult, op1=ALU.mult, accum_out=lsum[:, i:i + 1]),
                 reads=[("lv", 2 * i), ("lv", 2 * i + 1)], writes=[("lsum", i)])
            S.op("act", lambda h, i=i: h.activation(out=lsum[:, 2 + i:3 + i], in_=lsum[:, i:i + 1], func=AF.Exp),
                 reads=[("lsum", i)], writes=[("lsum", 2 + i)])
        S.op("dve", lambda h: h.tensor_tensor(out=lam[:], in0=lsum[:, 2:3], in1=lsum[:, 3:4], op=ALU.subtract),
             reads=[("lsum", 2), ("lsum", 3)], writes=["lam"])
        S.op("dve", lambda h: h.tensor_scalar(out=lam[:], in0=lam[:], scalar1=LAMBDA_INIT, scalar2=None, op0=ALU.add),
             reads=["lam"], writes=["lam"])
        for i in range(2):
            S.op("pool", lambda h, i=i: h.memset(Vh[i][:, :, 128:129], 1.0), writes=[("Vh1", i)])

        def load_head(hi, hd):
            b = hi % 2
            S.op("sp", lambda h: h.dma_start(out=KTh[b][:], in_=k.d["KT1"][hd]), writes=[("KTh", b)], dma=("KTh", b))
            S.op("sp", lambda h: h.dma_start(out=QTh[b][:], in_=k.d["QT1"][hd]), writes=[("QTh", b)], dma=("QTh", b))
            S.op("sp", lambda h: h.dma_start(out=Vh[b][:, :, 0:128], in_=k.d["V1"][:, hd * 128:(hd + 1) * 128].rearrange("(t p) e -> p t e", p=128)),
                 writes=[("Vh", b)], dma=("Vh", b))
            S.op("sp", lambda h: h.dma_start(out=Tb[b][:], in_=k.d["biasB"][hd]), writes=[("Tb", b)], dma=("Tb", b))
            S.op("dve", lambda h: h.tensor_tensor(out=Tb[b][:, 0:128], in0=Tb[b][:, 0:128], in1=mB[:], op=ALU.add),
                 reads=["mB", ("Tb", b)], writes=[("Tb", b)])

        sc = [0]
        cc = [0]

        def attend(hi, hd):
            b = hi % 2
            for qs in range(8):
                osl = qs % 2
                for c in range(2):
                    cr = slice(64 * c, 64 * c + 64)
                    obank = (4, 5, 6, 7)
                    nk = 4 * qs + 4
                    for j in range(nk):
                        qt0 = max(0, j - 4 * qs)
                        col0 = qt0 * 128
                        ncols = 512 - col0
                        delta = qs * 512 - j * 128 + col0
                        off = min(delta, TB_CONST) if delta + 0 >= 0 else None
                        assert off is not None and off >= 0 and off + ncols <= TBW
                        bi = sc[0] % 4
                        si = sc[0] % 3
                        sc[0] += 1
                        S.op("pe", lambda h, j=j, col0=col0, bi=bi, cr=cr, qs=qs: h.matmul(
                            banks[bi][:, col0:512], lhsT=KTh[b][cr, j * 128:(j + 1) * 128], rhs=QTh[b][cr, qs * 512 + col0:(qs + 1) * 512],
                            start=True, stop=True),
                            reads=[("KTh", b), ("QTh", b)], writes=[("bank", bi)])
                        S.op("dve", lambda h, bi=bi, si=si, col0=col0, off=off, ncols=ncols: h.tensor_tensor(
                            out=Sb[si][:, col0:512], in0=banks[bi][:, col0:512], in1=Tb[b][:, off:off + ncols], op=ALU.add),
                            reads=[("Tb", b)], writes=[("bank", bi), ("Sb", si)])
                        S.op("act", lambda h, si=si, col0=col0: h.activation(out=PT[si][:, col0:512], in_=Sb[si][:, col0:512], func=AF.Exp),
                             reads=[("Sb", si)], writes=[("PT", si)])
                        for qt in range(qt0, 4):
                            S.op("pe", lambda h, j=j, qt=qt, si=si, qs=qs: h.matmul(
                                banks[obank[qt]][:, 0:129], lhsT=PT[si][:, qt * 128:(qt + 1) * 128], rhs=Vh[b][:, j, :],
                                start=(j == 0), stop=(j == 4 * qs + qt)),
                                reads=[("PT", si), ("Vh", b), ("Vh1", b)], writes=[("bank", obank[qt])])
                    for qt in range(4):
                        eng = "act" if qt % 2 == 0 else "dve"
                        if eng == "act":
                            S.op("act", lambda h, qt=qt, c=c, osl=osl: h.copy(out=Ost[osl][c][:, qt, :], in_=banks[obank[qt]][:, 0:129]),
                                 writes=[("bank", obank[qt]), ("Ost", osl, c, qt)])
                        else:
                            S.op("dve", lambda h, qt=qt, c=c, osl=osl: h.tensor_copy(out=Ost[osl][c][:, qt, :], in_=banks[obank[qt]][:, 0:129]),
                                 writes=[("bank", obank[qt]), ("Ost", osl, c, qt)])
                ci = cc[0] % 64
                cc[0] += 1
                okeys = [("Ost", osl, c_, qt) for c_ in range(2) for qt in range(4)]
                S.op("dve", lambda h, ci=ci, osl=osl: h.reciprocal(out=rd[:, ci, 0:4], in_=Ost[osl][0][:, :, 128]), reads=okeys, writes=[("rd", ci, 0)])
                S.op("dve", lambda h, ci=ci, osl=osl: h.reciprocal(out=rd[:, ci, 4:8], in_=Ost[osl][1][:, :, 128]), reads=okeys, writes=[("rd", ci, 1)])
                S.op("dve", lambda h, ci=ci: h.tensor_scalar(out=rd[:, ci, 4:8], in0=rd[:, ci, 4:8], scalar1=lam[:, 0:1], scalar2=None, op0=ALU.mult),
                     reads=[("rd", ci, 1), "lam"], writes=[("rd", ci, 1)])
                for qt in range(4):
                    T = qs * 4 + qt
                    col = T % 128
                    a_ = av[qt % 2]
                    t_ = tv[qt % 2]
                    S.op("dve", lambda h, qt=qt, ci=ci, t_=t_, osl=osl: h.tensor_scalar(out=t_[:], in0=Ost[osl][1][:, qt, 0:128], scalar1=rd[:, ci, 4 + qt:5 + qt],
                                                                              scalar2=None, op0=ALU.mult),
                         reads=okeys + [("rd", ci, 1)], writes=[("tv", qt % 2)])
                    S.op("dve", lambda h, qt=qt, ci=ci, a_=a_, t_=t_, osl=osl: h.scalar_tensor_tensor(
                        out=a_[:], in0=Ost[osl][0][:, qt, 0:128], scalar=rd[:, ci, qt:qt + 1], in1=t_[:], op0=ALU.mult, op1=ALU.subtract),
                        reads=okeys + [("rd", ci, 0), ("tv", qt % 2)], writes=[("av", qt % 2)])
                    S.op("dve", lambda h, a_=a_, col=col: h.scalar_tensor_tensor(
                        out=k.junk[:, 0:128], in0=a_[:], scalar=1.0, in1=a_[:], op0=ALU.mult, op1=ALU.mult, accum_out=st1[:, col:col + 1]),
                        reads=[("av", qt % 2)], writes=[("st1", col)])
                    rstd_from_ss(k, st1[:, col:col + 1], st2[:, col:col + 1], st3[:, col:col + 1], k.nhalf[:, 0:1],
                                 [("st1", col)], ("st2", col), ("st3", col), 1.0 / 128, 1e-5)
                    S.op("dve", lambda h, a_=a_, col=col, T=T: h.scalar_tensor_tensor(
                        out=Oh[b][:, T, :], in0=a_[:], scalar=st3[:, col:col + 1], in1=sgb[:], op0=ALU.mult, op1=ALU.mult),
                        reads=[("av", qt % 2), ("st3", col), "sgb"], writes=[("Oh", b, T)])
            S.op("sp", lambda h: h.dma_start(out=k.d["O1"][:, hd * 128:(hd + 1) * 128].rearrange("(t p) e -> p t e", p=128), in_=Oh[b][:]),
                 reads=[("Oh", b, T) for T in range(NT)], dma=("Oh", b))

        heads = list(heads)
        load_head(0, heads[0])
        for hi, hd in enumerate(heads):
            if hi + 1 < len(heads):
                load_head(hi + 1, heads[hi + 1])
            attend(hi, hd)
        emit_block(k)


ALL_PHASES = ("p1", "p2a", "ffn0", "p3", "p4", "p5a", "ffn1")


def build(phases=ALL_PHASES, kinds=None):
    kinds = kinds or {}
    nc = bass.Bass("TRN2", target_bir_lowering=False)
    k = K()
    k.nc = nc
    d = {}

    def dt(name, shape, dtype, kind):
        d[name] = nc.dram_tensor(name, list(shape), dtype, kind=kinds.get(name, kind)).ap()

    EI = "ExternalInput"
    dt("x", [SEQ, D], F32, EI)
    dt("norm_g", [2, 4, D], F32, EI)
    dt("w_in_a", [1, D, 4608], F32, EI)
    dt("w_out_a", [1, 512, D], F32, EI)
    dt("kv_norm_g", [D], F32, EI)
    dt("w_k_shared", [D, D], F32, EI)
    dt("w_v_shared", [D, D], F32, EI)
    dt("w_q_b", [1, D, D], F32, EI)
    for nm in ("lam_q1", "lam_k1", "lam_q2", "lam_k2"):
        dt(nm, [1, 64], F32, EI)
    dt("subln_g", [1, 128], F32, EI)
    dt("w_out_b", [1, D, D], F32, EI)
    dt("w_up", [2, D, 2 * DFF], F32, EI)
    dt("w_down", [2, DFF, D], F32, EI)
    dt("convp", [2, 128, 2 * NFC, 4], F32, EI)
    dt("ident", [128, 128], F32, EI)
    dt("biasA", [3, NH, 128, 256], F32, EI)
    dt("maskA", [128, 256], F32, EI)
    dt("biasB", [NH, 128, TBW], F32, EI)
    dt("maskB", [128, 128], F32, EI)
    for g in range(3):
        dt("U%d" % g, [SEQ, 520], F32, "Internal")
    dt("H1", [SEQ, D], F32, "Internal")
    dt("H2", [SEQ, D], F32, "Internal")
    dt("QT1", [NH, 128, SEQ], BF16, "Internal")
    dt("KT1", [NH, 128, SEQ], BF16, "Internal")
    dt("V1", [SEQ, D], BF16, "Internal")
    dt("O1", [SEQ, D], BF16, "Internal")
    dt("out", [SEQ, D], F32, "ExternalOutput")
    k.d = d

    with ExitStack() as es:
        k.S = Sched(nc, es)
        S = k.S
        identf = es.enter_context(nc.sbuf_tensor("identf", [128, 128], F32))
        k.identb = es.enter_context(nc.sbuf_tensor("identb", [128, 128], BF16))
        k.nhalf = es.enter_context(nc.sbuf_tensor("nhalf", [128, 8], F32))
        k.junk = es.enter_context(nc.sbuf_tensor("junkg", [128, 1024], BF16))
        S.op("sp", lambda h: h.dma_start(out=identf[:], in_=d["ident"]), writes=["identf"], dma="identf")
        S.op("dve", lambda h: h.tensor_copy(out=k.identb[:], in_=identf[:]), reads=["identf"], writes=["identb"])
        S.op("dve", lambda h: h.memset(k.nhalf[:], -0.5), writes=["nhalf"])
        emit_block(k)
        for ph in phases:
            if ph == "p1":
                phase1(k)
            elif ph.startswith("p1g"):
                phase1(k, groups=(int(ph[3]),))
            elif ph == "p2a":
                phase_outproj(k, 0)
            elif ph == "ffn0":
                phase_ffn(k, 0, d["H1"], d["H2"])
            elif ph == "p3":
                phase3(k)
            elif ph == "p4":
                phase4(k)
            elif ph.startswith("p4h"):
                phase4(k, heads=[int(ph[3])])
            elif ph == "p5a":
                phase_outproj(k, 1)
            elif ph == "ffn1":
                phase_ffn(k, 1, d["H1"], d["out"])
    return nc


def _bucket(n):
    n = np.maximum(n, 0)
    nf = np.maximum(n, 1).astype(np.float32)
    large = 16 + (np.log(nf / np.float32(16)) / np.float32(math.log(2048 / 16)) * np.float32(16)).astype(np.int32)
    large = np.minimum(large, 31)
    return np.where(n < 16, n, large).astype(np.int64)


def host_consts(rel_bias_table, conv_w, conv_b):
    tab = np.asarray(rel_bias_table, dtype=np.float32)
    kl = np.arange(128)[:, None]
    qc = np.arange(256)[None, :]
    du = qc - kl
    biasA = np.stack([np.take(tab, _bucket(r * np.clip(du, 0, 128)), axis=1) for r in DIL]).astype(np.float32)
    maskA = np.where((du >= 0) & (du <= 128), np.float32(0.0), np.float32(NEG)).astype(np.float32)
    jj = np.arange(TBW)[None, :]
    pp = np.arange(128)[:, None]
    biasB = np.take(tab, _bucket(np.maximum(jj - pp, 0)), axis=1).astype(np.float32)
    maskB = np.where(jj[:, :128] - pp >= 0, np.float32(0.0), np.float32(NEG)).astype(np.float32)
    cw = np.asarray(conv_w, dtype=np.float32)
    cb = np.asarray(conv_b, dtype=np.float32)
    convp = np.concatenate([cw, cb[:, None, :]], axis=1)
    convp = np.ascontiguousarray(convp.reshape(2, 4, 2 * NFC, 128).transpose(0, 3, 2, 1))
    return dict(biasA=np.ascontiguousarray(biasA), maskA=maskA, biasB=np.ascontiguousarray(biasB), maskB=np.ascontiguousarray(maskB),
                convp=convp, ident=np.eye(128, dtype=np.float32))


_NC_CACHE = {}


def kernel(x, rel_bias_table, norm_g, w_in_a, w_out_a, kv_norm_g, w_k_shared, w_v_shared, w_q_b,
           lam_q1, lam_k1, lam_q2, lam_k2, subln_g, w_out_b, w_up, conv_w, conv_b, w_down):
    f = lambda a: np.ascontiguousarray(np.asarray(a, dtype=np.float32))
    hc = host_consts(rel_bias_table, conv_w, conv_b)
    shared = dict(norm_g=f(norm_g), w_in_a=f(w_in_a), w_out_a=f(w_out_a), kv_norm_g=f(kv_norm_g), w_k_shared=f(w_k_shared),
                  w_v_shared=f(w_v_shared), w_q_b=f(w_q_b), lam_q1=f(lam_q1), lam_k1=f(lam_k1), lam_q2=f(lam_q2), lam_k2=f(lam_k2),
                  subln_g=f(subln_g), w_out_b=f(w_out_b), w_up=f(w_up), w_down=f(w_down), **hc)
    x = f(x)
    if "nc" not in _NC_CACHE:
        _NC_CACHE["nc"] = build()
    nc = _NC_CACHE["nc"]
    in_maps = [dict(shared, x=x[b]) for b in range(8)]
    res = run_bass_kernel_spmd(nc, in_maps, core_ids=list(range(8)))
    return np.stack([np.asarray(res.results[b]["out"], dtype=np.float32) for b in range(8)], axis=0)
```

```python
import math
from contextlib import ExitStack

import numpy as np
import concourse.bass as bass
import concourse.mybir as mybir
from concourse.bass_utils import run_bass_kernel_spmd

F32 = mybir.dt.float32
BF16 = mybir.dt.bfloat16
AF = mybir.ActivationFunctionType
ALU = mybir.AluOpType

SEQ = 4096
D = 1024
NT = SEQ // 128
NH = 8
DIL = (1, 4, 16)
DFF = 2816
NFC = DFF // 128
NEG = -30000.0
LAMBDA_INIT = 0.8 - 0.6 * math.exp(-0.3 * 1)
TBW = 2176
TB_CONST = 1664

ENGS = ("pe", "act", "dve", "pool", "sp")


class _Op:
    __slots__ = ("eng", "fn", "deps", "idx", "dma", "sig", "sigval", "dsem", "dval")

    def __init__(self, eng, fn, dma):
        self.eng = eng
        self.fn = fn
        self.deps = []
        self.dma = dma
        self.sig = False
        self.sigval = 0
        self.dsem = None
        self.dval = 0


class Sched:
    def __init__(self, nc, es):
        self.nc = nc
        self.es = es
        self.sems = {e: es.enter_context(nc.semaphore("s_" + e)) for e in ENGS}
        self.q = {e: [] for e in ENGS}
        self.lastw = {}
        self.readers = {}
        self.sigcount = {e: 0 for e in ENGS}
        self.dma_slot = {}
        self.nsem = 0

    def _slot(self, key):
        if key not in self.dma_slot:
            self.nsem += 1
            self.dma_slot[key] = [self.es.enter_context(self.nc.semaphore("d%d" % self.nsem)), 0]
        return self.dma_slot[key]

    def op(self, eng, fn, reads=(), writes=(), dma=None):
        o = _Op(eng, fn, dma is not None)
        o.idx = len(self.q[eng])
        if dma is not None:
            s = self._slot(dma)
            s[1] += 16
            o.dsem, o.dval = s[0], s[1]
        dep = {}

        def add(d):
            if d is None or d is o:
                return
            k = ("d", id(d.dsem)) if d.dma else ("e", d.eng)
            cur = dep.get(k)
            if cur is None or (cur.dval < d.dval if d.dma else cur.idx < d.idx):
                dep[k] = d

        for b in reads:
            add(self.lastw.get(b))
        for b in writes:
            add(self.lastw.get(b))
            for r in self.readers.get(b, {}).values():
                add(r)
        o.deps = list(dep.values())
        me = ("d", id(o.dsem)) if o.dma else ("e", eng)
        for b in reads:
            self.readers.setdefault(b, {})[me] = o
        for b in writes:
            self.lastw[b] = o
            self.readers[b] = {}
        self.q[eng].append(o)
        return o

    def finalize(self):
        for e in ENGS:
            for o in self.q[e]:
                for d in o.deps:
                    if d.dma:
                        continue
                    if d.eng == "pe" and o.eng == "pe":
                        continue
                    d.sig = True
        for e in ENGS:
            c = self.sigcount[e]
            for o in self.q[e]:
                if o.sig and not o.dma:
                    c += 1
                    o.sigval = c
            self.sigcount[e] = c

    def emit_engine(self, e, h):
        known = {}
        for o in self.q[e]:
            need = {}
            for d in o.deps:
                if d.dma:
                    k = ("d", id(d.dsem))
                    if need.get(k, (None, 0))[1] < d.dval:
                        need[k] = (d.dsem, d.dval)
                else:
                    if d.eng == "pe" and e == "pe":
                        continue
                    k = ("e", d.eng)
                    if need.get(k, (None, 0))[1] < d.sigval:
                        need[k] = (self.sems[d.eng], d.sigval)
            for k, (sem, val) in need.items():
                if known.get(k, 0) >= val:
                    continue
                h.wait_ge(sem, val)
                known[k] = val
            ins = o.fn(h)
            if o.dma:
                ins.then_inc(o.dsem, 16)
            elif o.sig:
                ins.then_inc(self.sems[e], 1)

    def final_dma_waits(self, h):
        for key, (sem, cnt) in self.dma_slot.items():
            if cnt:
                h.wait_ge(sem, cnt)

    def reset_phase(self):
        self.q = {e: [] for e in ENGS}
        self.lastw = {}
        self.readers = {}


class K:
    ntag = 0

    def newtag(self):
        self.ntag += 1
        return "_p%d" % self.ntag


def emit_block(k):
    S, nc = k.S, k.nc
    S.finalize()
    with nc.Block() as block:
        @block.sync
        def _(h):
            S.emit_engine("sp", h)
            S.final_dma_waits(h)

        @block.tensor
        def _(h):
            S.emit_engine("pe", h)

        @block.scalar
        def _(h):
            S.emit_engine("act", h)

        @block.vector
        def _(h):
            S.emit_engine("dve", h)

        @block.gpsimd
        def _(h):
            S.emit_engine("pool", h)
    S.reset_phase()


def bank_bf16(bank, c):
    return bank[:, :].bitcast(BF16).rearrange("p (c t) -> p c t", c=c)


def load_cast_w(k, dst, dst_key, w2d, nchunk, ncols, slot):
    S = k.S
    wv = w2d.rearrange("(c p) n -> p c n", p=128)
    keys = []
    i = 0
    for c in range(nchunk):
        for c0 in range(0, ncols, 2048):
            c1 = min(ncols, c0 + 2048)
            key = (dst_key, c, c0)
            keys.append(key)
            S.op("pool", lambda h, c=c, c0=c0, c1=c1: h.dma_start(out=dst[:, c, c0:c1], in_=wv[:, c, c0:c1]),
                 writes=[key], dma=(slot, i % 4))
            i += 1
    return keys


def rstd_from_ss(k, ss_ap, ms_ap, rstd_ap, nh_ap, keys_ss, key_ms, key_rstd, inv_n, eps):
    S = k.S
    S.op("dve", lambda h: h.tensor_scalar(out=ms_ap, in0=ss_ap, scalar1=inv_n, scalar2=eps, op0=ALU.mult, op1=ALU.add),
         reads=keys_ss, writes=[key_ms])
    S.op("pool", lambda h: h.tensor_tensor(out=rstd_ap, in0=ms_ap, in1=nh_ap, op=ALU.pow),
         reads=[key_ms], writes=[key_rstd])


def phase1(k, groups=(0, 1, 2)):
    nc, S = k.nc, k.S
    with ExitStack() as es:
        tg = k.newtag()
        sb = lambda n, s, d: es.enter_context(nc.sbuf_tensor(n + tg, s, d))
        banks = [es.enter_context(nc.psum_tensor("bk%d" % i + tg, [128, 512], F32)) for i in range(8)]
        g0b = sb("g0b", [128, 1024], F32)
        Wg = sb("Wg", [128, 8, 1536], BF16)
        xt = [sb("xt%d" % i, [128, 1024], F32) for i in range(3)]
        junk = sb("junk", [128, 1024], BF16)
        xn = [sb("xn%d" % i, [128, 1024], BF16) for i in range(2)]
        xnT = [sb("xnT%d" % i, [128, 8, 512], BF16) for i in range(2)]
        QT = sb("QT", [128, 4, SEQ], BF16)
        KT = sb("KT", [128, 4, SEQ], BF16)
        Va = sb("Va", [128, NT, 8, 65], BF16)
        bA = sb("bA", [128, 8, 256], F32)
        mA = sb("mA", [128, 256], F32)
        Sb = [sb("Sb%d" % i, [128, 256], F32) for i in range(4)]
        PT = [sb("PT%d" % i, [128, 8, 256], BF16) for i in range(3)]
        Ost = [sb("Ost%d" % i, [128, 520], F32) for i in range(2)]
        st_ss = sb("st_ss", [128, 128], F32)
        st_ms = sb("st_ms", [128, 128], F32)
        st_rs = sb("st_rs", [128, 128], F32)

        S.op("sp", lambda h: h.dma_start(out=g0b[:], in_=k.d["norm_g"][0, 0, :].partition_broadcast(128)), writes=["g0b"], dma="g0b")
        S.op("sp", lambda h: h.dma_start(out=mA[:], in_=k.d["maskA"]), writes=["mA"], dma="mA")
        S.op("pool", lambda h: h.memset(Va[:, :, :, 64:65], 1.0), writes=["Va1"])

        x = k.d["x"]
        tcount = 0
        for g in groups:
            r = DIL[g]
            nb = NT // r
            xr = x.rearrange("(m r) d -> r m d", r=r)
            Ur = k.d["U%d" % g].rearrange("(m r) d -> r m d", r=r)
            wkeys = load_cast_w(k, Wg, "Wg", k.d["w_in_a"][0, :, g * 1536:(g + 1) * 1536], 8, 1536, "Wg")
            S.op("sp", lambda h, g=g: h.dma_start(out=bA[:], in_=k.d["biasA"][g].rearrange("h p q -> p h q")),
                 writes=["bA"], dma="bA")
            for hh in range(8):
                S.op("pool", lambda h, hh=hh: h.tensor_tensor(out=bA[:, hh, :], in0=bA[:, hh, :], in1=mA[:], op=ALU.add),
                     reads=["bA", "mA"], writes=["bA"])

            def stageA_load_norm(st):
                nonlocal tcount
                for t4 in range(4):
                    T = st * 4 + t4
                    c, n = divmod(T, nb)
                    sl = tcount % 3
                    col = tcount % 128
                    tcount += 1
                    S.op("sp", lambda h, c=c, n=n, sl=sl, xr=xr: h.dma_start(out=xt[sl][:], in_=xr[c, n * 128:(n + 1) * 128, :]),
                         writes=[("xt", sl)], dma=("xt", sl))
                    S.op("act", lambda h, sl=sl, col=col: h.activation(out=junk[:], in_=xt[sl][:], func=AF.Square,
                                                                    accum_out=st_ss[:, col:col + 1]),
                         reads=[("xt", sl)], writes=[("ss", col)])
                    rstd_from_ss(k, st_ss[:, col:col + 1], st_ms[:, col:col + 1], st_rs[:, col:col + 1], k.nhalf[:, 0:1],
                                 [("ss", col)], ("ms", col), ("rs", col), 1.0 / D, 1e-6)
                    xs = T % 2
                    S.op("dve", lambda h, sl=sl, col=col, xs=xs: h.scalar_tensor_tensor(
                        out=xn[xs][:], in0=xt[sl][:], scalar=st_rs[:, col:col + 1], in1=g0b[:], op0=ALU.mult, op1=ALU.mult),
                        reads=[("xt", sl), ("rs", col), "g0b"], writes=[("xn", xs)])
                    pb = banks[T % 2]
                    for dc in range(8):
                        S.op("pe", lambda h, dc=dc, xs=xs, pb=pb: h.transpose(
                            out=bank_bf16(pb, 8)[:, dc, :], in_=xn[xs][:, dc * 128:(dc + 1) * 128], identity=k.identb[:]),
                            reads=[("xn", xs), "identb"], writes=[("bank", T % 2)])
                    S.op("act", lambda h, st=st, t4=t4, pb=pb: h.copy(out=xnT[st % 2][:, :, t4 * 128:(t4 + 1) * 128], in_=bank_bf16(pb, 8)),
                         reads=[], writes=[("bank", T % 2), ("xnT", st % 2, t4)])

            pcount = [0]

            def stageA_proj(st):
                xk = [("xnT", st % 2, t4) for t4 in range(4)]
                for fc in range(8):
                    bi = 2 + pcount[0] % 3
                    pcount[0] += 1
                    for dc in range(8):
                        S.op("pe", lambda h, fc=fc, dc=dc, bi=bi: h.matmul(
                            banks[bi][:, :], lhsT=Wg[:, dc, fc * 128:(fc + 1) * 128], rhs=xnT[st % 2][:, dc, :],
                            start=(dc == 0), stop=(dc == 7)),
                            reads=xk + wkeys, writes=[("bank", bi)])
                    if fc < 4:
                        S.op("act", lambda h, fc=fc, bi=bi: h.activation(out=QT[:, fc, st * 512:(st + 1) * 512], in_=banks[bi][:, :],
                                                                   func=AF.Copy, scale=0.125),
                             writes=[("bank", bi), ("QT", st)])
                    else:
                        S.op("dve", lambda h, fc=fc, bi=bi: h.tensor_copy(out=KT[:, fc - 4, st * 512:(st + 1) * 512], in_=banks[bi][:, :]),
                             writes=[("bank", bi), ("KT", st)])
                for t4 in range(4):
                    T = st * 4 + t4
                    bi = 2 + pcount[0] % 3
                    pcount[0] += 1
                    for dc in range(8):
                        S.op("pe", lambda h, t4=t4, dc=dc, bi=bi: h.matmul(
                            banks[bi][:, :], lhsT=xnT[st % 2][:, dc, t4 * 128:(t4 + 1) * 128], rhs=Wg[:, dc, 1024:1536],
                            start=(dc == 0), stop=(dc == 7)),
                            reads=[("xnT", st % 2, t4)] + wkeys, writes=[("bank", bi)])
                    eng = "act" if t4 % 2 == 0 else "dve"
                    if eng == "act":
                        S.op("act", lambda h, T=T, bi=bi: h.copy(out=Va[:, T, :, 0:64], in_=banks[bi][:, :].rearrange("p (h e) -> p h e", h=8)),
                             writes=[("bank", bi), ("Va", T)])
                    else:
                        S.op("dve", lambda h, T=T, bi=bi: h.tensor_copy(out=Va[:, T, :, 0:64], in_=banks[bi][:, :].rearrange("p (h e) -> p h e", h=8)),
                             writes=[("bank", bi), ("Va", T)])

            stageA_load_norm(0)
            for st in range(8):
                if st + 1 < 8:
                    stageA_load_norm(st + 1)
                stageA_proj(st)

            scount = [0]

            def stageB_scores(T):
                c, j = divmod(T, nb)
                ncols = 256 if j < nb - 1 else 128
                st_list = sorted(set([(T * 128) // 512, (T * 128 + ncols - 1) // 512]))
                for hd in range(8):
                    hr = slice(64 * (hd % 2), 64 * (hd % 2) + 64)
                    hp = hd // 2
                    bi = 2 + scount[0] % 3
                    si = scount[0] % 4
                    scount[0] += 1
                    S.op("pe", lambda h, T=T, hr=hr, hp=hp, bi=bi, ncols=ncols: h.matmul(
                        banks[bi][:, 0:ncols], lhsT=KT[hr, hp, T * 128:(T + 1) * 128], rhs=QT[hr, hp, T * 128:T * 128 + ncols],
                        start=True, stop=True),
                        reads=[("KT", T // 4)] + [("QT", s_) for s_ in st_list], writes=[("bank", bi)])
                    S.op("dve", lambda h, hd=hd, bi=bi, si=si, ncols=ncols: h.tensor_tensor(
                        out=Sb[si][:, 0:ncols], in0=banks[bi][:, 0:ncols], in1=bA[:, hd, 0:ncols], op=ALU.add),
                        reads=["bA"], writes=[("bank", bi), ("Sb", si)])
                    S.op("act", lambda h, T=T, hd=hd, si=si, ncols=ncols: h.activation(
                        out=PT[T % 3][:, hd, 0:ncols], in_=Sb[si][:, 0:ncols], func=AF.Exp),
                        reads=[("Sb", si)], writes=[("PT", T % 3, hd)])

            def stageB_pv(T):
                c, j = divmod(T, nb)
                o0 = 0 if T % 2 == 0 else 1
                obanks = (0, 1) if o0 == 0 else (5, 6)
                for hd in range(8):
                    bi = obanks[hd // 4]
                    oap = banks[bi][:, :].rearrange("p (h e) -> p h e", h=4)[:, hd % 4, 0:65]
                    if j > 0:
                        S.op("pe", lambda h, T=T, hd=hd, oap=oap: h.matmul(
                            oap, lhsT=PT[(T - 1) % 3][:, hd, 128:256], rhs=Va[:, T - 1, hd, :], start=True, stop=False),
                            reads=[("PT", (T - 1) % 3, hd), ("Va", T - 1), "Va1"], writes=[("bank", bi)])
                    S.op("pe", lambda h, T=T, hd=hd, oap=oap, j=j: h.matmul(
                        oap, lhsT=PT[T % 3][:, hd, 0:128], rhs=Va[:, T, hd, :], start=(j == 0), stop=True),
                        reads=[("PT", T % 3, hd), ("Va", T), "Va1"], writes=[("bank", bi)])
                osl = T % 2
                for half in range(2):
                    bi = obanks[half]
                    S.op("act", lambda h, half=half, bi=bi, osl=osl: h.copy(
                        out=Ost[osl][:, half * 260:(half + 1) * 260].rearrange("p (h e) -> p h e", h=4),
                        in_=banks[bi][:, :].rearrange("p (h e) -> p h e", h=4)[:, :, 0:65]),
                        writes=[("bank", bi), ("Ost", osl, half)])
                S.op("sp", lambda h, c=c, j=j, osl=osl, Ur=Ur: h.dma_start(out=Ur[c, j * 128:(j + 1) * 128, :], in_=Ost[osl][:]),
                     reads=[("Ost", osl, 0), ("Ost", osl, 1)], dma=("Ost", osl))

            for T in range(NT):
                stageB_scores(T)
                if T >= 1:
                    stageB_pv(T - 1)
            stageB_pv(NT - 1)
        emit_block(k)


def postnorm_residual(k, pbanks, pkeys, gb, gkey, hres, hkey, tmp, tmpkey, st, col):
    S = k.S
    ss, ms, rs = st
    for half in range(2):
        S.op("act", lambda h, half=half: h.activation(out=k.junk[:, 0:512], in_=pbanks[half][:, :], func=AF.Square,
                                                      accum_out=ss[:, 2 * col + half:2 * col + half + 1]),
             writes=[pkeys[half], ("ss", 2 * col + half)])
    S.op("dve", lambda h: h.tensor_tensor(out=ms[:, 2 * col:2 * col + 1], in0=ss[:, 2 * col:2 * col + 1], in1=ss[:, 2 * col + 1:2 * col + 2], op=ALU.add),
         reads=[("ss", 2 * col), ("ss", 2 * col + 1)], writes=[("ms", 2 * col)])
    rstd_from_ss(k, ms[:, 2 * col:2 * col + 1], ms[:, 2 * col + 1:2 * col + 2], rs[:, col:col + 1], k.nhalf[:, 0:1],
                 [("ms", 2 * col)], ("ms", 2 * col + 1), ("rs", col), 1.0 / D, 1e-6)
    for half in range(2):
        S.op("dve", lambda h, half=half: h.scalar_tensor_tensor(
            out=tmp[:, half * 512:(half + 1) * 512], in0=pbanks[half][:, :], scalar=rs[:, col:col + 1],
            in1=gb[:, half * 512:(half + 1) * 512], op0=ALU.mult, op1=ALU.mult),
            reads=[("rs", col), gkey], writes=[pkeys[half], (tmpkey, half)])
    S.op("pool", lambda h: h.tensor_tensor(out=hres, in0=hres, in1=tmp[:, :], op=ALU.add),
         reads=[(tmpkey, 0), (tmpkey, 1)], writes=[hkey])


def phase_outproj(k, layer):
    nc, S = k.nc, k.S
    with ExitStack() as es:
        tg = k.newtag()
        sb = lambda n, s, d: es.enter_context(nc.sbuf_tensor(n + tg, s, d))
        banks = [es.enter_context(nc.psum_tensor("bk%d" % i + tg, [128, 512], F32)) for i in range(8)]
        nkc = 4 if layer == 0 else 8
        Wo = sb("Wo", [128, nkc, 1024], BF16)
        g1b = sb("g1b", [128, 1024], F32)
        hres = [sb("hres%d" % i, [128, 1024], F32) for i in range(3)]
        tmp = [sb("tmp%d" % i, [128, 1024], F32) for i in range(2)]
        oT = [sb("oT%d" % i, [128, nkc, 128], BF16) for i in range(2)]
        ss = sb("ss", [128, 128], F32)
        ms = sb("ms", [128, 128], F32)
        rs = sb("rs", [128, 64], F32)
        if layer == 0:
            Ul = [[sb("Ul%d_%d" % (i, g), [128, 520], F32) for g in range(3)] for i in range(2)]
            Us = [sb("Us%d" % i, [128, 520], F32) for i in range(2)]
            rden = [sb("rden%d" % i, [128, 8], F32) for i in range(2)]
            obf = [sb("obf%d" % i, [128, 512], BF16) for i in range(2)]
            wsrc = k.d["w_out_a"][0]
            hin = k.d["x"]
            hout = k.d["H1"]
        else:
            obf = [sb("obf%d" % i, [128, 1024], BF16) for i in range(3)]
            wsrc = k.d["w_out_b"][0]
            hin = k.d["H2"]
            hout = k.d["H1"]
        wkeys = load_cast_w(k, Wo, "Wo", wsrc, nkc, 1024, "Wo")
        S.op("sp", lambda h: h.dma_start(out=g1b[:], in_=k.d["norm_g"][layer, 1, :].partition_broadcast(128)), writes=["g1b"], dma="g1b")

        def load(T):
            hs = T % 3
            S.op("sp", lambda h: h.dma_start(out=hres[hs][:], in_=hin[T * 128:(T + 1) * 128, :]), writes=[("hres", hs)], dma=("hres", hs))
            if layer == 0:
                us = T % 2
                for g in range(3):
                    S.op("sp", lambda h, g=g: h.dma_start(out=Ul[us][g][:], in_=k.d["U%d" % g][T * 128:(T + 1) * 128, :]),
                         writes=[("Ul", us, g)], dma=("Ul", us, g))
            else:
                os_ = T % 3
                S.op("sp", lambda h: h.dma_start(out=obf[os_][:], in_=k.d["O1"][T * 128:(T + 1) * 128, :]), writes=[("obf", os_)], dma=("obf", os_))

        def front(T):
            if layer == 0:
                us = T % 2
                S.op("dve", lambda h: h.tensor_tensor(out=Us[us][:], in0=Ul[us][0][:], in1=Ul[us][1][:], op=ALU.add),
                     reads=[("Ul", us, 0), ("Ul", us, 1)], writes=[("Us", us)])
                S.op("dve", lambda h: h.tensor_tensor(out=Us[us][:], in0=Us[us][:], in1=Ul[us][2][:], op=ALU.add),
                     reads=[("Ul", us, 2)], writes=[("Us", us)])
                usv = Us[us][:, :].rearrange("p (h e) -> p h e", h=8)
                S.op("dve", lambda h: h.reciprocal(out=rden[us][:, :], in_=usv[:, :, 64]), reads=[("Us", us)], writes=[("rden", us)])
                S.op("dve", lambda h: h.tensor_tensor(out=obf[us][:, :].rearrange("p (h e) -> p h e", h=8), in0=usv[:, :, 0:64],
                                                      in1=rden[us][:, :].unsqueeze(2).to_broadcast([128, 8, 64]), op=ALU.mult),
                     reads=[("Us", us), ("rden", us)], writes=[("obf", us)])
                ok = ("obf", us)
                osrc = obf[us]
            else:
                ok = ("obf", T % 3)
                osrc = obf[T % 3]
            bi = T % 2
            for kc in range(nkc):
                S.op("pe", lambda h, kc=kc: h.transpose(out=bank_bf16(banks[bi], 8)[:, kc, :], in_=osrc[:, kc * 128:(kc + 1) * 128], identity=k.identb[:]),
                     reads=[ok, "identb"], writes=[("bank", bi)])
            S.op("act", lambda h: h.copy(out=oT[T % 2][:, :, :], in_=bank_bf16(banks[bi], 8)[:, 0:nkc, :]),
                 writes=[("bank", bi), ("oT", T % 2)])

        def back(T):
            pb = (2, 3) if T % 2 == 0 else (4, 5)
            for half in range(2):
                for kc in range(nkc):
                    S.op("pe", lambda h, half=half, kc=kc: h.matmul(
                        banks[pb[half]][:, :], lhsT=oT[T % 2][:, kc, :], rhs=Wo[:, kc, half * 512:(half + 1) * 512],
                        start=(kc == 0), stop=(kc == nkc - 1)),
                        reads=[("oT", T % 2)] + wkeys, writes=[("bank", pb[half])])
            hs = T % 3
            postnorm_residual(k, [banks[pb[0]], banks[pb[1]]], [("bank", pb[0]), ("bank", pb[1])], g1b, "g1b",
                              hres[hs][:, :], ("hres", hs), tmp[T % 2], ("tmp", T % 2), (ss, ms, rs), T % 64)
            S.op("sp", lambda h: h.dma_start(out=hout[T * 128:(T + 1) * 128, :], in_=hres[hs][:]), reads=[("hres", hs)], dma=("hout", hs))

        load(0)
        load(1)
        front(0)
        for T in range(NT):
            if T + 2 < NT:
                load(T + 2)
            if T + 1 < NT:
                front(T + 1)
            back(T)
        emit_block(k)


def phase_ffn(k, layer, hin, hout):
    nc, S = k.nc, k.S
    NS = SEQ // 256
    NCH = 2 * NFC
    NCB = 6
    with ExitStack() as es:
        tg = k.newtag()
        sb = lambda n, s, d: es.enter_context(nc.sbuf_tensor(n + tg, s, d))
        banks = [es.enter_context(nc.psum_tensor("bk%d" % i + tg, [128, 512], F32)) for i in range(8)]
        Wup = sb("Wup", [128, 8, 2 * DFF], BF16)
        Wdn = sb("Wdn", [128, NFC, 1024], BF16)
        g2b = sb("g2b", [128, 1024], F32)
        g3b = sb("g3b", [128, 1024], F32)
        cp = sb("cp", [128, NCH, 4], F32)
        hres = [sb("hres%d" % i, [128, 2, 1024], F32) for i in range(3)]
        xn = [sb("xn%d" % i, [128, 1024], BF16) for i in range(2)]
        xnT = [sb("xnT%d" % i, [128, 8, 258], BF16) for i in range(2)]
        Cb = [sb("Cb%d" % i, [128, 256], F32) for i in range(NCB)]
        Gb = [sb("Gb%d" % i, [128, 256], F32) for i in range(3)]
        gT = sb("gT", [128, NFC, 256], BF16)
        tmp = sb("tmp", [128, 1024], F32)
        ss = sb("ss", [128, 128], F32)
        ms = sb("ms", [128, 128], F32)
        rs = sb("rs", [128, 64], F32)
        ss2 = sb("ss2", [128, 64], F32)
        ms2 = sb("ms2", [128, 64], F32)
        rs2 = sb("rs2", [128, 64], F32)

        wup_keys = load_cast_w(k, Wup, "Wup", k.d["w_up"][layer], 8, 2 * DFF, "Wup")
        wdn_keys = load_cast_w(k, Wdn, "Wdn", k.d["w_down"][layer], NFC, 1024, "Wdn")
        S.op("sp", lambda h: h.dma_start(out=g2b[:], in_=k.d["norm_g"][layer, 2, :].partition_broadcast(128)), writes=["g2b"], dma="g2b")
        S.op("sp", lambda h: h.dma_start(out=g3b[:], in_=k.d["norm_g"][layer, 3, :].partition_broadcast(128)), writes=["g3b"], dma="g3b")
        S.op("sp", lambda h: h.dma_start(out=cp[:], in_=k.d["convp"][layer]), writes=["cp"], dma="cp")
        S.op("pool", lambda h: h.memset(xnT[0][:, :, 0:2], 0.0), writes=[("xnT", 0, "h")])

        def load(s):
            hs = s % 3
            for tt in range(2):
                T = s * 2 + tt
                S.op("sp", lambda h, tt=tt, T=T: h.dma_start(out=hres[hs][:, tt, :], in_=hin[T * 128:(T + 1) * 128, :]),
                     writes=[("hres", hs, tt)], dma=("hres", hs, tt))

        def prenorm_a(s):
            hs = s % 3
            for tt in range(2):
                T = s * 2 + tt
                col = T % 64
                S.op("act", lambda h, tt=tt, col=col: h.activation(out=k.junk[:], in_=hres[hs][:, tt, :], func=AF.Square,
                                                                   accum_out=ss2[:, col:col + 1]),
                     reads=[("hres", hs, tt)], writes=[("ss2", col)])
                rstd_from_ss(k, ss2[:, col:col + 1], ms2[:, col:col + 1], rs2[:, col:col + 1], k.nhalf[:, 0:1],
                             [("ss2", col)], ("ms2", col), ("rs2", col), 1.0 / D, 1e-6)
                S.op("dve", lambda h, tt=tt, col=col: h.scalar_tensor_tensor(
                    out=xn[tt][:], in0=hres[hs][:, tt, :], scalar=rs2[:, col:col + 1], in1=g2b[:], op0=ALU.mult, op1=ALU.mult),
                    reads=[("hres", hs, tt), ("rs2", col), "g2b"], writes=[("xn", tt)])

        def prenorm_b(s):
            xb = s % 2
            if s > 0:
                S.op("pool", lambda h: h.tensor_copy(out=xnT[xb][:, :, 0:2], in_=xnT[1 - xb][:, :, 256:258]),
                     reads=[("xnT", 1 - xb, 1)], writes=[("xnT", xb, "h")])
            for tt in range(2):
                for dc in range(8):
                    S.op("pe", lambda h, tt=tt, dc=dc: h.transpose(out=bank_bf16(banks[0], 8)[:, dc, :], in_=xn[tt][:, dc * 128:(dc + 1) * 128],
                                                                  identity=k.identb[:]),
                         reads=[("xn", tt), "identb"], writes=[("bank", 0)])
                S.op("act", lambda h, tt=tt: h.copy(out=xnT[xb][:, :, 2 + tt * 128:2 + (tt + 1) * 128], in_=bank_bf16(banks[0], 8)),
                     writes=[("bank", 0), ("xnT", xb, tt)])

        def stA(s, n):
            i, which = divmod(n, 2)
            fc = i + which * NFC
            bi = 1 + n % 3
            xb = s % 2
            for dc in range(8):
                S.op("pe", lambda h, fc=fc, dc=dc, bi=bi: h.matmul(
                    banks[bi][:, 0:258], lhsT=Wup[:, dc, fc * 128:(fc + 1) * 128], rhs=xnT[xb][:, dc, :], start=(dc == 0), stop=(dc == 7)),
                    reads=[("xnT", xb, 0), ("xnT", xb, 1), ("xnT", xb, "h")] + wup_keys, writes=[("bank", bi)])

        def stB(s, n):
            i, which = divmod(n, 2)
            fc = i + which * NFC
            bi = 1 + n % 3
            ci = n % NCB
            S.op("act", lambda h: h.activation(out=Cb[ci][:, :], in_=banks[bi][:, 2:258], func=AF.Identity,
                                               scale=cp[:, fc, 2:3], bias=cp[:, fc, 3:4]),
                 reads=["cp"], writes=[("bank", bi), ("Cb", ci)])

        def stC(s, n):
            i, which = divmod(n, 2)
            fc = i + which * NFC
            bi = 1 + n % 3
            ci = n % NCB
            S.op("dve", lambda h: h.scalar_tensor_tensor(
                out=Cb[ci][:, :], in0=banks[bi][:, 1:257], scalar=cp[:, fc, 1:2], in1=Cb[ci][:, :], op0=ALU.mult, op1=ALU.add),
                reads=["cp"], writes=[("bank", bi), ("Cb", ci)])
            S.op("dve", lambda h: h.scalar_tensor_tensor(
                out=Cb[ci][:, :], in0=banks[bi][:, 0:256], scalar=cp[:, fc, 0:1], in1=Cb[ci][:, :], op0=ALU.mult, op1=ALU.add),
                reads=["cp"], writes=[("bank", bi), ("Cb", ci)])

        def stD(s, n):
            i, which = divmod(n, 2)
            ci = n % NCB
            gi = i % 3
            if which == 0:
                S.op("act", lambda h: h.activation(out=Gb[gi][:, :], in_=Cb[ci][:, :], func=AF.Gelu),
                     reads=[("Cb", ci)], writes=[("Gb", gi)])
            else:
                S.op("pool", lambda h: h.tensor_tensor(out=gT[:, i, :], in0=Gb[gi][:, :], in1=Cb[ci][:, :], op=ALU.mult),
                     reads=[("Gb", gi), ("Cb", ci)], writes=[("gT", i)])

        def down_mm(s):
            for tt in range(2):
                pb = (4, 5) if tt == 0 else (6, 7)
                for half in range(2):
                    for i in range(NFC):
                        S.op("pe", lambda h, tt=tt, half=half, i=i, pb=pb: h.matmul(
                            banks[pb[half]][:, :], lhsT=gT[:, i, tt * 128:(tt + 1) * 128], rhs=Wdn[:, i, half * 512:(half + 1) * 512],
                            start=(i == 0), stop=(i == NFC - 1)),
                            reads=[("gT", i)] + wdn_keys, writes=[("bank", pb[half])])

        def down_post(s):
            hs = s % 3
            for tt in range(2):
                T = s * 2 + tt
                pb = (4, 5) if tt == 0 else (6, 7)
                postnorm_residual(k, [banks[pb[0]], banks[pb[1]]], [("bank", pb[0]), ("bank", pb[1])], g3b, "g3b",
                                  hres[hs][:, tt, :], ("hres", hs, tt), tmp, "tmp", (ss, ms, rs), T % 64)
                S.op("sp", lambda h, tt=tt, T=T: h.dma_start(out=hout[T * 128:(T + 1) * 128, :], in_=hres[hs][:, tt, :]),
                     reads=[("hres", hs, tt)], dma=("hout", hs, tt))

        load(0)
        load(1)
        prenorm_a(0)
        prenorm_b(0)
        for s in range(NS):
            for t in range(NCH + 3):
                if t < NCH:
                    stA(s, t)
                if 0 <= t - 1 < NCH:
                    stB(s, t - 1)
                if 0 <= t - 2 < NCH:
                    stC(s, t - 2)
                if 0 <= t - 3 < NCH:
                    stD(s, t - 3)
                if t == 8 and s > 0:
                    down_post(s - 1)
                if t == 9 and s + 2 < NS:
                    load(s + 2)
                if t == 18 and s + 1 < NS:
                    prenorm_a(s + 1)
                if t == 30 and s + 1 < NS:
                    prenorm_b(s + 1)
            down_mm(s)
        down_post(NS - 1)
        emit_block(k)


def phase3(k):
    nc, S = k.nc, k.S
    with ExitStack() as es:
        tg = k.newtag()
        sb = lambda n, s, d: es.enter_context(nc.sbuf_tensor(n + tg, s, d))
        banks = [es.enter_context(nc.psum_tensor("bk%d" % i + tg, [128, 512], F32)) for i in range(8)]
        Wq = sb("Wq", [128, 8, 1024], BF16)
        Wk = sb("Wk", [128, 8, 1024], BF16)
        Wv = sb("Wv", [128, 8, 1024], BF16)
        gqb = sb("gqb", [128, 1024], F32)
        gkb = sb("gkb", [128, 1024], F32)
        ht = [sb("ht%d" % i, [128, 1024], F32) for i in range(3)]
        xq = [sb("xq%d" % i, [128, 1024], BF16) for i in range(2)]
        xk = [sb("xk%d" % i, [128, 1024], BF16) for i in range(2)]
        xqT = [sb("xqT%d" % i, [128, 8, 512], BF16) for i in range(2)]
        xkT = [sb("xkT%d" % i, [128, 8, 512], BF16) for i in range(2)]
        Qs = [sb("Qs%d" % i, [128, 512], BF16) for i in range(4)]
        Vs = [sb("Vs%d" % i, [128, 1024], BF16) for i in range(2)]
        ss = sb("ss", [128, 128], F32)
        ms = sb("ms", [128, 128], F32)
        rs = sb("rs", [128, 128], F32)
        wq_keys = load_cast_w(k, Wq, "Wq", k.d["w_q_b"][0], 8, 1024, "Wq")
        wk_keys = load_cast_w(k, Wk, "Wk", k.d["w_k_shared"], 8, 1024, "Wk")
        wv_keys = load_cast_w(k, Wv, "Wv", k.d["w_v_shared"], 8, 1024, "Wv")
        S.op("sp", lambda h: h.dma_start(out=gqb[:], in_=k.d["norm_g"][1, 0, :].partition_broadcast(128)), writes=["gqb"], dma="gqb")
        S.op("sp", lambda h: h.dma_start(out=gkb[:], in_=k.d["kv_norm_g"].partition_broadcast(128)), writes=["gkb"], dma="gkb")
        hin = k.d["H2"]

        def front(st):
            for t4 in range(4):
                T = st * 4 + t4
                sl = T % 3
                col = T % 128
                S.op("sp", lambda h, T=T, sl=sl: h.dma_start(out=ht[sl][:], in_=hin[T * 128:(T + 1) * 128, :]), writes=[("ht", sl)], dma=("ht", sl))
                S.op("act", lambda h, sl=sl, col=col: h.activation(out=k.junk[:], in_=ht[sl][:], func=AF.Square, accum_out=ss[:, col:col + 1]),
                     reads=[("ht", sl)], writes=[("ss", col)])
                rstd_from_ss(k, ss[:, col:col + 1], ms[:, col:col + 1], rs[:, col:col + 1], k.nhalf[:, 0:1],
                             [("ss", col)], ("ms", col), ("rs", col), 1.0 / D, 1e-6)
                xs = T % 2
                for (xb, gb, gk, nm) in ((xq, gqb, "gqb", "xq"), (xk, gkb, "gkb", "xk")):
                    S.op("dve", lambda h, sl=sl, col=col, xb=xb, gb=gb, xs=xs: h.scalar_tensor_tensor(
                        out=xb[xs][:], in0=ht[sl][:], scalar=rs[:, col:col + 1], in1=gb[:], op0=ALU.mult, op1=ALU.mult),
                        reads=[("ht", sl), ("rs", col), gk], writes=[(nm, xs)])
                for (xb, xT, nm, bi) in ((xq, xqT, "xq", 0), (xk, xkT, "xk", 1)):
                    for dc in range(8):
                        S.op("pe", lambda h, dc=dc, xb=xb, bi=bi, xs=xs: h.transpose(out=bank_bf16(banks[bi], 8)[:, dc, :], in_=xb[xs][:, dc * 128:(dc + 1) * 128],
                                                                            identity=k.identb[:]),
                             reads=[(nm, xs), "identb"], writes=[("bank", bi)])
                    S.op("act", lambda h, t4=t4, xT=xT, bi=bi: h.copy(out=xT[st % 2][:, :, t4 * 128:(t4 + 1) * 128], in_=bank_bf16(banks[bi], 8)),
                         writes=[("bank", bi), (nm + "T", st % 2, t4)])

        pc = [0]

        def proj(st):
            for (W, wkeys, xT, nm, dst, scale) in ((Wq, wq_keys, xqT, "xq", k.d["QT1"], 0.125), (Wk, wk_keys, xkT, "xk", k.d["KT1"], None)):
                xkeys = [(nm + "T", st % 2, t4) for t4 in range(4)]
                for hd in range(8):
                    bi = 2 + pc[0] % 4
                    qi = pc[0] % 4
                    pc[0] += 1
                    for dc in range(8):
                        S.op("pe", lambda h, hd=hd, dc=dc, bi=bi, W=W, xT=xT: h.matmul(
                            banks[bi][:, :], lhsT=W[:, dc, hd * 128:(hd + 1) * 128], rhs=xT[st % 2][:, dc, :], start=(dc == 0), stop=(dc == 7)),
                            reads=xkeys + wkeys, writes=[("bank", bi)])
                    if scale is not None:
                        S.op("act", lambda h, bi=bi, qi=qi: h.activation(out=Qs[qi][:, :], in_=banks[bi][:, :], func=AF.Copy, scale=0.125),
                             writes=[("bank", bi), ("Qs", qi)])
                    else:
                        S.op("dve", lambda h, bi=bi, qi=qi: h.tensor_copy(out=Qs[qi][:, :], in_=banks[bi][:, :]),
                             writes=[("bank", bi), ("Qs", qi)])
                    S.op("sp", lambda h, hd=hd, qi=qi, dst=dst: h.dma_start(out=dst[hd, :, st * 512:(st + 1) * 512], in_=Qs[qi][:, :]),
                         reads=[("Qs", qi)], dma=("Qs", qi))
            for t4 in range(4):
                T = st * 4 + t4
                vs = T % 2
                for half in range(2):
                    bi = 2 + pc[0] % 4
                    pc[0] += 1
                    for dc in range(8):
                        S.op("pe", lambda h, t4=t4, half=half, dc=dc, bi=bi: h.matmul(
                            banks[bi][:, :], lhsT=xkT[st % 2][:, dc, t4 * 128:(t4 + 1) * 128], rhs=Wv[:, dc, half * 512:(half + 1) * 512],
                            start=(dc == 0), stop=(dc == 7)),
                            reads=[("xkT", st % 2, t4)] + wv_keys, writes=[("bank", bi)])
                    if half == 0:
                        S.op("act", lambda h, bi=bi, vs=vs: h.copy(out=Vs[vs][:, 0:512], in_=banks[bi][:, :]), writes=[("bank", bi), ("Vs", vs, 0)])
                    else:
                        S.op("dve", lambda h, bi=bi, vs=vs: h.tensor_copy(out=Vs[vs][:, 512:1024], in_=banks[bi][:, :]), writes=[("bank", bi), ("Vs", vs, 1)])
                S.op("sp", lambda h, T=T, vs=vs: h.dma_start(out=k.d["V1"][T * 128:(T + 1) * 128, :], in_=Vs[vs][:, :]),
                     reads=[("Vs", vs, 0), ("Vs", vs, 1)], dma=("Vs", vs))

        front(0)
        for st in range(8):
            if st + 1 < 8:
                front(st + 1)
            proj(st)
        emit_block(k)


def phase4(k, heads=range(8)):
    nc, S = k.nc, k.S
    NSB = 4
    SK = 3
    with ExitStack() as es:
        tg = k.newtag()
        sb = lambda n, s, d: es.enter_context(nc.sbuf_tensor(n + tg, s, d))
        banks = [es.enter_context(nc.psum_tensor("bk%d" % i + tg, [128, 512], F32)) for i in range(8)]
        KTh = [sb("KTh%d" % i, [128, SEQ], BF16) for i in range(2)]
        QTh = [sb("QTh%d" % i, [128, SEQ], BF16) for i in range(2)]
        Vh = [sb("Vh%d" % i, [128, NT, 129], BF16) for i in range(2)]
        Tb = [sb("Tb%d" % i, [128, TBW], F32) for i in range(2)]
        mB = sb("mB", [128, 128], F32)
        Sb = [sb("Sb%d" % i, [128, 512], F32) for i in range(NSB)]
        PT = [sb("PT%d" % i, [128, 512], BF16) for i in range(NSB)]
        Ost = [[sb("Ost%d_%d" % (i, c), [128, 4, 129], F32) for c in range(2)] for i in range(2)]
        Oh = [sb("Oh%d" % i, [128, NT, 128], BF16) for i in range(2)]
        lv = [sb("lv%d" % i, [128, 64], F32) for i in range(4)]
        lsum = sb("lsum", [128, 4], F32)
        lam = sb("lam", [128, 1], F32)
        sgb = sb("sgb", [128, 128], F32)
        rd = sb("rd", [128, 64, 8], F32)
        av = [sb("av%d" % i, [128, 128], F32) for i in range(2)]
        tv = [sb("tv%d" % i, [128, 128], F32) for i in range(2)]
        st1 = sb("st1", [128, 128], F32)
        st2 = sb("st2", [128, 128], F32)
        st3 = sb("st3", [128, 128], F32)

        for i, nm in enumerate(("lam_q1", "lam_k1", "lam_q2", "lam_k2")):
            S.op("sp", lambda h, i=i, nm=nm: h.dma_start(out=lv[i][:], in_=k.d[nm][0, :].partition_broadcast(128)), writes=[("lv", i)], dma=("lv", i))
        S.op("sp", lambda h: h.dma_start(out=sgb[:], in_=k.d["subln_g"][0, :].partition_broadcast(128)), writes=["sgb"], dma="sgb")
        S.op("sp", lambda h: h.dma_start(out=mB[:], in_=k.d["maskB"]), writes=["mB"], dma="mB")
        S.op("dve", lambda h: h.tensor_scalar(out=sgb[:], in0=sgb[:], scalar1=1.0 - LAMBDA_INIT, scalar2=None, op0=ALU.mult),
             reads=["sgb"], writes=["sgb"])
        for i in range(2):
            S.op("dve", lambda h, i=i: h.scalar_tensor_tensor(out=k.junk[:, 0:64], in0=lv[2 * i][:], scalar=1.0, in1=lv[2 * i + 1][:],
                                                            op0=ALU.mult, op1=ALU.mult, accum_out=lsum[:, i:i + 1]),
                 reads=[("lv", 2 * i), ("lv", 2 * i + 1)], writes=[("lsum", i)])
            S.op("act", lambda h, i=i: h.activation(out=lsum[:, 2 + i:3 + i], in_=lsum[:, i:i + 1], func=AF.Exp),
                 reads=[("lsum", i)], writes=[("lsum", 2 + i)])
        S.op("dve", lambda h: h.tensor_tensor(out=lam[:], in0=lsum[:, 2:3], in1=lsum[:, 3:4], op=ALU.subtract),
             reads=[("lsum", 2), ("lsum", 3)], writes=["lam"])
        S.op("dve", lambda h: h.tensor_scalar(out=lam[:], in0=lam[:], scalar1=LAMBDA_INIT, scalar2=None, op0=ALU.add),
             reads=["lam"], writes=["lam"])
        for i in range(2):
            S.op("pool", lambda h, i=i: h.memset(Vh[i][:, :, 128:129], 1.0), writes=[("Vh1", i)])

        def load_head(hi, hd):
            b = hi % 2
            S.op("sp", lambda h: h.dma_start(out=KTh[b][:], in_=k.d["KT1"][hd]), writes=[("KTh", b)], dma=("KTh", b))
            S.op("sp", lambda h: h.dma_start(out=QTh[b][:], in_=k.d["QT1"][hd]), writes=[("QTh", b)], dma=("QTh", b))
            S.op("sp", lambda h: h.dma_start(out=Vh[b][:, :, 0:128], in_=k.d["V1"][:, hd * 128:(hd + 1) * 128].rearrange("(t p) e -> p t e", p=128)),
                 writes=[("Vh", b)], dma=("Vh", b))
            S.op("sp", lambda h: h.dma_start(out=Tb[b][:], in_=k.d["biasB"][hd]), writes=[("Tb", b)], dma=("Tb", b))
            S.op("dve", lambda h: h.tensor_tensor(out=Tb[b][:, 0:128], in0=Tb[b][:, 0:128], in1=mB[:], op=ALU.add),
                 reads=["mB", ("Tb", b)], writes=[("Tb", b)])

        obank = (4, 5, 6, 7)
        heads = list(heads)
        items = []
        for hi, hd in enumerate(heads):
            for qs in range(8):
                for c in range(2):
                    for j in range(4 * qs + 4):
                        items.append((hi, hd, qs, c, j))
        N = len(items)
        deferred = {}

        def defer(t, fn):
            deferred.setdefault(t, []).append(fn)

        def geom(it):
            hi, hd, qs, c, j = it
            qt0 = max(0, j - 4 * qs)
            col0 = qt0 * 128
            ncols = 512 - col0
            delta = qs * 512 - j * 128 + col0
            off = min(delta, TB_CONST)
            assert 0 <= off and off + ncols <= TBW
            return qt0, col0, ncols, off

        def stA(n):
            hi, hd, qs, c, j = items[n]
            b = hi % 2
            qt0, col0, ncols, off = geom(items[n])
            cr = slice(64 * c, 64 * c + 64)
            bi = n % 4
            S.op("pe", lambda h: h.matmul(
                banks[bi][:, col0:512], lhsT=KTh[b][cr, j * 128:(j + 1) * 128], rhs=QTh[b][cr, qs * 512 + col0:(qs + 1) * 512],
                start=True, stop=True),
                reads=[("KTh", b), ("QTh", b)], writes=[("bank", bi)])

        def stB(n):
            hi, hd, qs, c, j = items[n]
            b = hi % 2
            qt0, col0, ncols, off = geom(items[n])
            bi = n % 4
            si = n % NSB
            S.op("dve", lambda h: h.tensor_tensor(
                out=Sb[si][:, col0:512], in0=banks[bi][:, col0:512], in1=Tb[b][:, off:off + ncols], op=ALU.add),
                reads=[("Tb", b)], writes=[("bank", bi), ("Sb", si)])

        def stC(n):
            qt0, col0, ncols, off = geom(items[n])
            si = n % NSB
            S.op("act", lambda h: h.activation(out=PT[si][:, col0:512], in_=Sb[si][:, col0:512], func=AF.Exp),
                 reads=[("Sb", si)], writes=[("PT", si)])

        def evac(hi, qs, c, qt):
            osl = qs % 2
            if qt % 2 == 0:
                S.op("act", lambda h: h.copy(out=Ost[osl][c][:, qt, :], in_=banks[obank[qt]][:, 0:129]),
                     writes=[("bank", obank[qt]), ("Ost", osl, c, qt)])
            else:
                S.op("dve", lambda h: h.tensor_copy(out=Ost[osl][c][:, qt, :], in_=banks[obank[qt]][:, 0:129]),
                     writes=[("bank", obank[qt]), ("Ost", osl, c, qt)])

        cc = [0]

        def combine(hi, hd, qs):
            b = hi % 2
            osl = qs % 2
            ci = cc[0] % 64
            cc[0] += 1
            okeys = [("Ost", osl, c_, qt) for c_ in range(2) for qt in range(4)]
            S.op("dve", lambda h: h.reciprocal(out=rd[:, ci, 0:4], in_=Ost[osl][0][:, :, 128]), reads=okeys, writes=[("rd", ci, 0)])
            S.op("dve", lambda h: h.reciprocal(out=rd[:, ci, 4:8], in_=Ost[osl][1][:, :, 128]), reads=okeys, writes=[("rd", ci, 1)])
            S.op("dve", lambda h: h.tensor_scalar(out=rd[:, ci, 4:8], in0=rd[:, ci, 4:8], scalar1=lam[:, 0:1], scalar2=None, op0=ALU.mult),
                 reads=[("rd", ci, 1), "lam"], writes=[("rd", ci, 1)])
            for qt in range(4):
                T = qs * 4 + qt
                col = T % 128
                a_ = av[qt % 2]
                t_ = tv[qt % 2]
                S.op("pool", lambda h, qt=qt, t_=t_: h.tensor_scalar(out=t_[:], in0=Ost[osl][1][:, qt, 0:128], scalar1=rd[:, ci, 4 + qt:5 + qt],
                                                                    scalar2=0.0, op0=ALU.mult, op1=ALU.add),
                     reads=okeys + [("rd", ci, 1)], writes=[("tv", qt % 2)])
                S.op("dve", lambda h, qt=qt, a_=a_, t_=t_: h.scalar_tensor_tensor(
                    out=a_[:], in0=Ost[osl][0][:, qt, 0:128], scalar=rd[:, ci, qt:qt + 1], in1=t_[:], op0=ALU.mult, op1=ALU.subtract),
                    reads=okeys + [("rd", ci, 0), ("tv", qt % 2)], writes=[("av", qt % 2)])
                S.op("dve", lambda h, a_=a_, col=col: h.scalar_tensor_tensor(
                    out=k.junk[:, 0:128], in0=a_[:], scalar=1.0, in1=a_[:], op0=ALU.mult, op1=ALU.mult, accum_out=st1[:, col:col + 1]),
                    reads=[("av", qt % 2)], writes=[("st1", col)])
                rstd_from_ss(k, st1[:, col:col + 1], st2[:, col:col + 1], st3[:, col:col + 1], k.nhalf[:, 0:1],
                             [("st1", col)], ("st2", col), ("st3", col), 1.0 / 128, 1e-5)
                S.op("pool", lambda h, a_=a_, col=col, T=T: h.tensor_scalar(
                    out=a_[:], in0=a_[:], scalar1=st3[:, col:col + 1], scalar2=0.0, op0=ALU.mult, op1=ALU.add),
                    reads=[("av", qt % 2), ("st3", col)], writes=[("av", qt % 2)])
                S.op("pool", lambda h, a_=a_, T=T: h.tensor_tensor(out=Oh[b][:, T, :], in0=a_[:], in1=sgb[:], op=ALU.mult),
                     reads=[("av", qt % 2), "sgb"], writes=[("Oh", b, T)])
            if qs == 7:
                S.op("sp", lambda h: h.dma_start(out=k.d["O1"][:, hd * 128:(hd + 1) * 128].rearrange("(t p) e -> p t e", p=128), in_=Oh[b][:]),
                     reads=[("Oh", b, T) for T in range(NT)], dma=("Oh", b))

        def stD(n, t):
            hi, hd, qs, c, j = items[n]
            b = hi % 2
            qt0, col0, ncols, off = geom(items[n])
            si = n % NSB
            for qt in range(qt0, 4):
                last = (j == 4 * qs + qt)
                S.op("pe", lambda h, qt=qt, last=last: h.matmul(
                    banks[obank[qt]][:, 0:129], lhsT=PT[si][:, qt * 128:(qt + 1) * 128], rhs=Vh[b][:, j, :],
                    start=(j == 0), stop=last),
                    reads=[("PT", si), ("Vh", b), ("Vh1", b)], writes=[("bank", obank[qt])])
                if last:
                    defer(t + 1, lambda hi=hi, qs=qs, c=c, qt=qt: evac(hi, qs, c, qt))
            if c == 1 and j == 4 * qs + 3:
                defer(t + 6, lambda hi=hi, hd=hd, qs=qs: combine(hi, hd, qs))

        load_head(0, heads[0])
        for t in range(N + SK + 8):
            for fn in deferred.pop(t, []):
                fn()
            if t < N:
                it = items[t]
                if it[2] == 0 and it[3] == 1 and it[4] == 0 and it[0] + 1 < len(heads):
                    load_head(it[0] + 1, heads[it[0] + 1])
                stA(t)
            if 0 <= t - 1 < N:
                stB(t - 1)
            if 0 <= t - 2 < N:
                stC(t - 2)
            if 0 <= t - SK < N:
                stD(t - SK, t)
        assert not deferred
        emit_block(k)


ALL_PHASES = ("p1", "p2a", "ffn0", "p3", "p4", "p5a", "ffn1")


def build(phases=ALL_PHASES, kinds=None):
    kinds = kinds or {}
    nc = bass.Bass("TRN2", target_bir_lowering=False)
    k = K()
    k.nc = nc
    d = {}

    def dt(name, shape, dtype, kind):
        d[name] = nc.dram_tensor(name, list(shape), dtype, kind=kinds.get(name, kind)).ap()

    EI = "ExternalInput"
    dt("x", [SEQ, D], F32, EI)
    dt("norm_g", [2, 4, D], F32, EI)
    dt("w_in_a", [1, D, 4608], F32, EI)
    dt("w_out_a", [1, 512, D], F32, EI)
    dt("kv_norm_g", [D], F32, EI)
    dt("w_k_shared", [D, D], F32, EI)
    dt("w_v_shared", [D, D], F32, EI)
    dt("w_q_b", [1, D, D], F32, EI)
    for nm in ("lam_q1", "lam_k1", "lam_q2", "lam_k2"):
        dt(nm, [1, 64], F32, EI)
    dt("subln_g", [1, 128], F32, EI)
    dt("w_out_b", [1, D, D], F32, EI)
    dt("w_up", [2, D, 2 * DFF], F32, EI)
    dt("w_down", [2, DFF, D], F32, EI)
    dt("convp", [2, 128, 2 * NFC, 4], F32, EI)
    dt("ident", [128, 128], F32, EI)
    dt("biasA", [3, NH, 128, 256], F32, EI)
    dt("maskA", [128, 256], F32, EI)
    dt("biasB", [NH, 128, TBW], F32, EI)
    dt("maskB", [128, 128], F32, EI)
    for g in range(3):
        dt("U%d" % g, [SEQ, 520], F32, "Internal")
    dt("H1", [SEQ, D], F32, "Internal")
    dt("H2", [SEQ, D], F32, "Internal")
    dt("QT1", [NH, 128, SEQ], BF16, "Internal")
    dt("KT1", [NH, 128, SEQ], BF16, "Internal")
    dt("V1", [SEQ, D], BF16, "Internal")
    dt("O1", [SEQ, D], BF16, "Internal")
    dt("out", [SEQ, D], F32, "ExternalOutput")
    k.d = d

    with ExitStack() as es:
        k.S = Sched(nc, es)
        S = k.S
        identf = es.enter_context(nc.sbuf_tensor("identf", [128, 128], F32))
        k.identb = es.enter_context(nc.sbuf_tensor("identb", [128, 128], BF16))
        k.nhalf = es.enter_context(nc.sbuf_tensor("nhalf", [128, 8], F32))
        k.junk = es.enter_context(nc.sbuf_tensor("junkg", [128, 1024], BF16))
        S.op("sp", lambda h: h.dma_start(out=identf[:], in_=d["ident"]), writes=["identf"], dma="identf")
        S.op("dve", lambda h: h.tensor_copy(out=k.identb[:], in_=identf[:]), reads=["identf"], writes=["identb"])
        S.op("dve", lambda h: h.memset(k.nhalf[:], -0.5), writes=["nhalf"])
        emit_block(k)
        for ph in phases:
            if ph == "p1":
                phase1(k)
            elif ph.startswith("p1g"):
                phase1(k, groups=(int(ph[3]),))
            elif ph == "p2a":
                phase_outproj(k, 0)
            elif ph == "ffn0":
                phase_ffn(k, 0, d["H1"], d["H2"])
            elif ph == "p3":
                phase3(k)
            elif ph == "p4":
                phase4(k)
            elif ph.startswith("p4h"):
                phase4(k, heads=[int(ph[3])])
            elif ph == "p5a":
                phase_outproj(k, 1)
            elif ph == "ffn1":
                phase_ffn(k, 1, d["H1"], d["out"])
    return nc


def _bucket(n):
    n = np.maximum(n, 0)
    nf = np.maximum(n, 1).astype(np.float32)
    large = 16 + (np.log(nf / np.float32(16)) / np.float32(math.log(2048 / 16)) * np.float32(16)).astype(np.int32)
    large = np.minimum(large, 31)
    return np.where(n < 16, n, large).astype(np.int64)


def host_consts(rel_bias_table, conv_w, conv_b):
    tab = np.asarray(rel_bias_table, dtype=np.float32)
    kl = np.arange(128)[:, None]
    qc = np.arange(256)[None, :]
    du = qc - kl
    biasA = np.stack([np.take(tab, _bucket(r * np.clip(du, 0, 128)), axis=1) for r in DIL]).astype(np.float32)
    maskA = np.where((du >= 0) & (du <= 128), np.float32(0.0), np.float32(NEG)).astype(np.float32)
    jj = np.arange(TBW)[None, :]
    pp = np.arange(128)[:, None]
    biasB = np.take(tab, _bucket(np.maximum(jj - pp, 0)), axis=1).astype(np.float32)
    maskB = np.where(jj[:, :128] - pp >= 0, np.float32(0.0), np.float32(NEG)).astype(np.float32)
    cw = np.asarray(conv_w, dtype=np.float32)
    cb = np.asarray(conv_b, dtype=np.float32)
    convp = np.concatenate([cw, cb[:, None, :]], axis=1)
    convp = np.ascontiguousarray(convp.reshape(2, 4, 2 * NFC, 128).transpose(0, 3, 2, 1))
    return dict(biasA=np.ascontiguousarray(biasA), maskA=maskA, biasB=np.ascontiguousarray(biasB), maskB=np.ascontiguousarray(maskB),
                convp=convp, ident=np.eye(128, dtype=np.float32))


_NC_CACHE = {}


def kernel(x, rel_bias_table, norm_g, w_in_a, w_out_a, kv_norm_g, w_k_shared, w_v_shared, w_q_b,
           lam_q1, lam_k1, lam_q2, lam_k2, subln_g, w_out_b, w_up, conv_w, conv_b, w_down):
    f = lambda a: np.ascontiguousarray(np.asarray(a, dtype=np.float32))
    hc = host_consts(rel_bias_table, conv_w, conv_b)
    shared = dict(norm_g=f(norm_g), w_in_a=f(w_in_a), w_out_a=f(w_out_a), kv_norm_g=f(kv_norm_g), w_k_shared=f(w_k_shared),
                  w_v_shared=f(w_v_shared), w_q_b=f(w_q_b), lam_q1=f(lam_q1), lam_k1=f(lam_k1), lam_q2=f(lam_q2), lam_k2=f(lam_k2),
                  subln_g=f(subln_g), w_out_b=f(w_out_b), w_up=f(w_up), w_down=f(w_down), **hc)
    x = f(x)
    if "nc" not in _NC_CACHE:
        _NC_CACHE["nc"] = build()
    nc = _NC_CACHE["nc"]
    in_maps = [dict(shared, x=x[b]) for b in range(8)]
    res = run_bass_kernel_spmd(nc, in_maps, core_ids=list(range(8)))
    return np.stack([np.asarray(res.results[b]["out"], dtype=np.float32) for b in range(8)], axis=0)
```

```python
import math
from contextlib import ExitStack

import numpy as np
import concourse.bass as bass
import concourse.mybir as mybir
from concourse.bass_utils import run_bass_kernel_spmd

F32 = mybir.dt.float32
BF16 = mybir.dt.bfloat16
AF = mybir.ActivationFunctionType
ALU = mybir.AluOpType

SEQ = 4096
D = 1024
NT = SEQ // 128
NH = 8
DIL = (1, 4, 16)
DFF = 2816
NFC = DFF // 128
NEG = -30000.0
LAMBDA_INIT = 0.8 - 0.6 * math.exp(-0.3 * 1)
TBW = 2176
TB_CONST = 1664

ENGS = ("pe", "act", "dve", "pool", "sp")


class _Op:
    __slots__ = ("eng", "fn", "deps", "idx", "dma", "sig", "sigval", "dsem", "dval")

    def __init__(self, eng, fn, dma):
        self.eng = eng
        self.fn = fn
        self.deps = []
        self.dma = dma
        self.sig = False
        self.sigval = 0
        self.dsem = None
        self.dval = 0


class Sched:
    def __init__(self, nc, es):
        self.nc = nc
        self.es = es
        self.sems = {e: es.enter_context(nc.semaphore("s_" + e)) for e in ENGS}
        self.q = {e: [] for e in ENGS}
        self.lastw = {}
        self.readers = {}
        self.sigcount = {e: 0 for e in ENGS}
        self.dma_slot = {}
        self.free_slots = {}
        self.nsem = 0

    def _slot(self, key, eng):
        if key not in self.dma_slot:
            fs = self.free_slots.setdefault(eng, [])
            if fs:
                self.dma_slot[key] = fs.pop()
            else:
                self.nsem += 1
                self.dma_slot[key] = [self.es.enter_context(self.nc.semaphore("d%d" % self.nsem)), 0, eng]
        assert self.dma_slot[key][2] == eng
        return self.dma_slot[key]

    def op(self, eng, fn, reads=(), writes=(), dma=None):
        o = _Op(eng, fn, dma is not None)
        o.idx = len(self.q[eng])
        if dma is not None:
            s = self._slot(dma, eng)
            s[1] += 16
            o.dsem, o.dval = s[0], s[1]
        dep = {}

        def add(d):
            if d is None or d is o:
                return
            k = ("d", id(d.dsem)) if d.dma else ("e", d.eng)
            cur = dep.get(k)
            if cur is None or (cur.dval < d.dval if d.dma else cur.idx < d.idx):
                dep[k] = d

        for b in reads:
            add(self.lastw.get(b))
        for b in writes:
            add(self.lastw.get(b))
            for r in self.readers.get(b, {}).values():
                add(r)
        o.deps = list(dep.values())
        me = ("d", id(o.dsem)) if o.dma else ("e", eng)
        for b in reads:
            self.readers.setdefault(b, {})[me] = o
        for b in writes:
            self.lastw[b] = o
            self.readers[b] = {}
        self.q[eng].append(o)
        return o

    def finalize(self):
        for e in ENGS:
            for o in self.q[e]:
                for d in o.deps:
                    if d.dma:
                        continue
                    if d.eng == "pe" and o.eng == "pe":
                        continue
                    d.sig = True
        for e in ENGS:
            c = self.sigcount[e]
            for o in self.q[e]:
                if o.sig and not o.dma:
                    c += 1
                    o.sigval = c
            self.sigcount[e] = c

    def emit_engine(self, e, h):
        known = {}
        for o in self.q[e]:
            need = {}
            for d in o.deps:
                if d.dma:
                    k = ("d", id(d.dsem))
                    if need.get(k, (None, 0))[1] < d.dval:
                        need[k] = (d.dsem, d.dval)
                else:
                    if d.eng == "pe" and e == "pe":
                        continue
                    k = ("e", d.eng)
                    if need.get(k, (None, 0))[1] < d.sigval:
                        need[k] = (self.sems[d.eng], d.sigval)
            for k, (sem, val) in need.items():
                if known.get(k, 0) >= val:
                    continue
                h.wait_ge(sem, val)
                known[k] = val
            ins = o.fn(h)
            if o.dma:
                ins.then_inc(o.dsem, 16)
            elif o.sig:
                ins.then_inc(self.sems[e], 1)

    def final_dma_waits(self, h):
        for key, sl in self.dma_slot.items():
            if sl[1]:
                h.wait_ge(sl[0], sl[1])

    def reset_phase(self):
        for sl in self.dma_slot.values():
            self.free_slots.setdefault(sl[2], []).append(sl)
        self.dma_slot = {}
        self.q = {e: [] for e in ENGS}
        self.lastw = {}
        self.readers = {}


class K:
    ntag = 0

    def newtag(self):
        self.ntag += 1
        return "_p%d" % self.ntag


def emit_block(k):
    S, nc = k.S, k.nc
    S.finalize()
    with nc.Block() as block:
        @block.sync
        def _(h):
            S.emit_engine("sp", h)
            S.final_dma_waits(h)

        @block.tensor
        def _(h):
            S.emit_engine("pe", h)

        @block.scalar
        def _(h):
            S.emit_engine("act", h)

        @block.vector
        def _(h):
            S.emit_engine("dve", h)

        @block.gpsimd
        def _(h):
            S.emit_engine("pool", h)
    S.reset_phase()


def bank_bf16(bank, c):
    return bank[:, :].bitcast(BF16).rearrange("p (c t) -> p c t", c=c)


def load_cast_w(k, dst, dst_key, w2d, nchunk, ncols, slot):
    S = k.S
    wv = w2d.rearrange("(c p) n -> p c n", p=128)
    keys = []
    i = 0
    for c in range(nchunk):
        for c0 in range(0, ncols, 2048):
            c1 = min(ncols, c0 + 2048)
            key = (dst_key, c, c0)
            keys.append(key)
            S.op("pool", lambda h, c=c, c0=c0, c1=c1: h.dma_start(out=dst[:, c, c0:c1], in_=wv[:, c, c0:c1]),
                 writes=[key], dma=(slot, i % 4))
            i += 1
    return keys


def rstd_from_ss(k, ss_ap, ms_ap, rstd_ap, nh_ap, keys_ss, key_ms, key_rstd, inv_n, eps):
    S = k.S
    S.op("dve", lambda h: h.tensor_scalar(out=ms_ap, in0=ss_ap, scalar1=inv_n, scalar2=eps, op0=ALU.mult, op1=ALU.add),
         reads=keys_ss, writes=[key_ms])
    S.op("pool", lambda h: h.tensor_tensor(out=rstd_ap, in0=ms_ap, in1=nh_ap, op=ALU.pow),
         reads=[key_ms], writes=[key_rstd])


def phase1(k, groups=(0, 1, 2)):
    nc, S = k.nc, k.S
    with ExitStack() as es:
        tg = k.newtag()
        sb = lambda n, s, d: es.enter_context(nc.sbuf_tensor(n + tg, s, d))
        banks = [es.enter_context(nc.psum_tensor("bk%d" % i + tg, [128, 512], F32)) for i in range(8)]
        g0b = sb("g0b", [128, 1024], F32)
        Wg = sb("Wg", [128, 8, 1536], BF16)
        xt = [sb("xt%d" % i, [128, 1024], F32) for i in range(5)]
        junk = k.junk
        xn = [sb("xn%d" % i, [128, 1024], BF16) for i in range(2)]
        xnT = [sb("xnT%d" % i, [128, 8, 512], BF16) for i in range(2)]
        QT = sb("QT", [128, 4, SEQ], BF16)
        KT = sb("KT", [128, 4, SEQ], BF16)
        Va = sb("Va", [128, NT, 8, 65], BF16)
        bA = sb("bA", [128, 8, 256], F32)
        mA = sb("mA", [128, 256], F32)
        Sb = [sb("Sb%d" % i, [128, 256], F32) for i in range(4)]
        PT = [sb("PT%d" % i, [128, 8, 256], BF16) for i in range(3)]
        Ost = [sb("Ost%d" % i, [128, 520], F32) for i in range(2)]
        st_ss = sb("st_ss", [128, 128], F32)
        st_ms = sb("st_ms", [128, 128], F32)
        st_rs = sb("st_rs", [128, 128], F32)

        S.op("sp", lambda h: h.dma_start(out=g0b[:], in_=k.d["norm_g"][0, 0, :].partition_broadcast(128)), writes=["g0b"], dma="g0b")
        S.op("sp", lambda h: h.dma_start(out=mA[:], in_=k.d["maskA"]), writes=["mA"], dma="mA")
        S.op("pool", lambda h: h.memset(Va[:, :, :, 64:65], 1.0), writes=["Va1"])

        x = k.d["x"]
        tcount = 0
        for g in groups:
            r = DIL[g]
            nb = NT // r
            xr = x.rearrange("(m r) d -> r m d", r=r)
            Ur = k.d["U%d" % g].rearrange("(m r) d -> r m d", r=r)
            wkeys = load_cast_w(k, Wg, "Wg", k.d["w_in_a"][0, :, g * 1536:(g + 1) * 1536], 8, 1536, "Wg")
            S.op("sp", lambda h, g=g: h.dma_start(out=bA[:], in_=k.d["biasA"][g].rearrange("h p q -> p h q")),
                 writes=["bA"], dma="bA")
            for hh in range(8):
                S.op("pool", lambda h, hh=hh: h.tensor_tensor(out=bA[:, hh, :], in0=bA[:, hh, :], in1=mA[:], op=ALU.add),
                     reads=["bA", "mA"], writes=["bA"])

            tbase = tcount
            tcount += NT

            def a_load(T):
                c, n = divmod(T, nb)
                sl = (tbase + T) % 5
                S.op("sp", lambda h, c=c, n=n, sl=sl, xr=xr: h.dma_start(out=xt[sl][:], in_=xr[c, n * 128:(n + 1) * 128, :]),
                     writes=[("xt", sl)], dma=("xt", sl))

            def a_sq(T):
                sl = (tbase + T) % 5
                col = (tbase + T) % 128
                S.op("act", lambda h, sl=sl, col=col: h.activation(out=junk[:], in_=xt[sl][:], func=AF.Square,
                                                                accum_out=st_ss[:, col:col + 1]),
                     reads=[("xt", sl)], writes=[("ss", col)])

            def a_rs(T):
                col = (tbase + T) % 128
                rstd_from_ss(k, st_ss[:, col:col + 1], st_ms[:, col:col + 1], st_rs[:, col:col + 1], k.nhalf[:, 0:1],
                             [("ss", col)], ("ms", col), ("rs", col), 1.0 / D, 1e-6)

            def a_xn(T):
                sl = (tbase + T) % 5
                col = (tbase + T) % 128
                xs = T % 2
                S.op("dve", lambda h, sl=sl, col=col, xs=xs: h.scalar_tensor_tensor(
                    out=xn[xs][:], in0=xt[sl][:], scalar=st_rs[:, col:col + 1], in1=g0b[:], op0=ALU.mult, op1=ALU.mult),
                    reads=[("xt", sl), ("rs", col), "g0b"], writes=[("xn", xs)])

            def a_tr(T):
                st, t4 = divmod(T, 4)
                xs = T % 2
                pb = banks[T % 2]
                for dc in range(8):
                    S.op("pe", lambda h, dc=dc, xs=xs, pb=pb: h.transpose(
                        out=bank_bf16(pb, 8)[:, dc, :], in_=xn[xs][:, dc * 128:(dc + 1) * 128], identity=k.identb[:]),
                        reads=[("xn", xs), "identb"], writes=[("bank", T % 2)])
                S.op("act", lambda h, st=st, t4=t4, pb=pb: h.copy(out=xnT[st % 2][:, :, t4 * 128:(t4 + 1) * 128], in_=bank_bf16(pb, 8)),
                     reads=[], writes=[("bank", T % 2), ("xnT", st % 2, t4)])

            pcount = [0]

            def a_proj(st, q):
                xk = [("xnT", st % 2, t4) for t4 in range(4)]
                for gi in range(3 * q, 3 * q + 3):
                    bi = 2 + pcount[0] % 3
                    pcount[0] += 1
                    if gi < 8:
                        fc = gi
                        for dc in range(8):
                            S.op("pe", lambda h, fc=fc, dc=dc, bi=bi: h.matmul(
                                banks[bi][:, :], lhsT=Wg[:, dc, fc * 128:(fc + 1) * 128], rhs=xnT[st % 2][:, dc, :],
                                start=(dc == 0), stop=(dc == 7)),
                                reads=xk + wkeys, writes=[("bank", bi)])
                        if fc < 4:
                            S.op("act", lambda h, fc=fc, bi=bi: h.activation(out=QT[:, fc, st * 512:(st + 1) * 512], in_=banks[bi][:, :],
                                                                       func=AF.Copy, scale=0.125),
                                 writes=[("bank", bi), ("QT", st)])
                        else:
                            S.op("dve", lambda h, fc=fc, bi=bi: h.tensor_copy(out=KT[:, fc - 4, st * 512:(st + 1) * 512], in_=banks[bi][:, :]),
                                 writes=[("bank", bi), ("KT", st)])
                    else:
                        t4 = gi - 8
                        T = st * 4 + t4
                        for dc in range(8):
                            S.op("pe", lambda h, t4=t4, dc=dc, bi=bi: h.matmul(
                                banks[bi][:, :], lhsT=xnT[st % 2][:, dc, t4 * 128:(t4 + 1) * 128], rhs=Wg[:, dc, 1024:1536],
                                start=(dc == 0), stop=(dc == 7)),
                                reads=[("xnT", st % 2, t4)] + wkeys, writes=[("bank", bi)])
                        if t4 % 2 == 0:
                            S.op("act", lambda h, T=T, bi=bi: h.copy(out=Va[:, T, :, 0:64], in_=banks[bi][:, :].rearrange("p (h e) -> p h e", h=8)),
                                 writes=[("bank", bi), ("Va", T)])
                        else:
                            S.op("dve", lambda h, T=T, bi=bi: h.tensor_copy(out=Va[:, T, :, 0:64], in_=banks[bi][:, :].rearrange("p (h e) -> p h e", h=8)),
                                 writes=[("bank", bi), ("Va", T)])

            for u in range(-4, NT + 4):
                if 0 <= u + 4 < NT:
                    a_load(u + 4)
                if 0 <= u + 3 < NT:
                    a_sq(u + 3)
                if 0 <= u + 2 < NT:
                    a_rs(u + 2)
                if 0 <= u + 1 < NT:
                    a_xn(u + 1)
                if 0 <= u < NT:
                    a_tr(u)
                if u >= 4:
                    a_proj((u - 4) // 4, (u - 4) % 4)

            scount = [0]

            def stageB_scores(T):
                c, j = divmod(T, nb)
                ncols = 256 if j < nb - 1 else 128
                st_list = sorted(set([(T * 128) // 512, (T * 128 + ncols - 1) // 512]))
                for hd in range(8):
                    hr = slice(64 * (hd % 2), 64 * (hd % 2) + 64)
                    hp = hd // 2
                    bi = 2 + scount[0] % 3
                    si = scount[0] % 4
                    scount[0] += 1
                    S.op("pe", lambda h, T=T, hr=hr, hp=hp, bi=bi, ncols=ncols: h.matmul(
                        banks[bi][:, 0:ncols], lhsT=KT[hr, hp, T * 128:(T + 1) * 128], rhs=QT[hr, hp, T * 128:T * 128 + ncols],
                        start=True, stop=True),
                        reads=[("KT", T // 4)] + [("QT", s_) for s_ in st_list], writes=[("bank", bi)])
                    S.op("dve", lambda h, hd=hd, bi=bi, si=si, ncols=ncols: h.tensor_tensor(
                        out=Sb[si][:, 0:ncols], in0=banks[bi][:, 0:ncols], in1=bA[:, hd, 0:ncols], op=ALU.add),
                        reads=["bA"], writes=[("bank", bi), ("Sb", si)])
                    S.op("act", lambda h, T=T, hd=hd, si=si, ncols=ncols: h.activation(
                        out=PT[T % 3][:, hd, 0:ncols], in_=Sb[si][:, 0:ncols], func=AF.Exp),
                        reads=[("Sb", si)], writes=[("PT", T % 3, hd)])

            def stageB_pv(T):
                c, j = divmod(T, nb)
                o0 = 0 if T % 2 == 0 else 1
                obanks = (0, 1) if o0 == 0 else (5, 6)
                for hd in range(8):
                    bi = obanks[hd // 4]
                    oap = banks[bi][:, :].rearrange("p (h e) -> p h e", h=4)[:, hd % 4, 0:65]
                    if j > 0:
                        S.op("pe", lambda h, T=T, hd=hd, oap=oap: h.matmul(
                            oap, lhsT=PT[(T - 1) % 3][:, hd, 128:256], rhs=Va[:, T - 1, hd, :], start=True, stop=False),
                            reads=[("PT", (T - 1) % 3, hd), ("Va", T - 1), "Va1"], writes=[("bank", bi)])
                    S.op("pe", lambda h, T=T, hd=hd, oap=oap, j=j: h.matmul(
                        oap, lhsT=PT[T % 3][:, hd, 0:128], rhs=Va[:, T, hd, :], start=(j == 0), stop=True),
                        reads=[("PT", T % 3, hd), ("Va", T), "Va1"], writes=[("bank", bi)])
                osl = T % 2
                for half in range(2):
                    bi = obanks[half]
                    S.op("act", lambda h, half=half, bi=bi, osl=osl: h.copy(
                        out=Ost[osl][:, half * 260:(half + 1) * 260].rearrange("p (h e) -> p h e", h=4),
                        in_=banks[bi][:, :].rearrange("p (h e) -> p h e", h=4)[:, :, 0:65]),
                        writes=[("bank", bi), ("Ost", osl, half)])
                S.op("sp", lambda h, c=c, j=j, osl=osl, Ur=Ur: h.dma_start(out=Ur[c, j * 128:(j + 1) * 128, :], in_=Ost[osl][:]),
                     reads=[("Ost", osl, 0), ("Ost", osl, 1)], dma=("Ost", osl))

            for T in range(NT):
                stageB_scores(T)
                if T >= 1:
                    stageB_pv(T - 1)
            stageB_pv(NT - 1)
        emit_block(k)


def postnorm_residual(k, pbanks, pkeys, gb, gkey, hres, hkey, tmp, tmpkey, st, col):
    S = k.S
    ss, ms, rs = st
    for half in range(2):
        S.op("act", lambda h, half=half: h.activation(out=k.junk[:, 0:512], in_=pbanks[half][:, :], func=AF.Square,
                                                      accum_out=ss[:, 2 * col + half:2 * col + half + 1]),
             writes=[pkeys[half], ("ss", 2 * col + half)])
    S.op("dve", lambda h: h.tensor_tensor(out=ms[:, 2 * col:2 * col + 1], in0=ss[:, 2 * col:2 * col + 1], in1=ss[:, 2 * col + 1:2 * col + 2], op=ALU.add),
         reads=[("ss", 2 * col), ("ss", 2 * col + 1)], writes=[("ms", 2 * col)])
    rstd_from_ss(k, ms[:, 2 * col:2 * col + 1], ms[:, 2 * col + 1:2 * col + 2], rs[:, col:col + 1], k.nhalf[:, 0:1],
                 [("ms", 2 * col)], ("ms", 2 * col + 1), ("rs", col), 1.0 / D, 1e-6)
    for half in range(2):
        S.op("dve", lambda h, half=half: h.scalar_tensor_tensor(
            out=tmp[:, half * 512:(half + 1) * 512], in0=pbanks[half][:, :], scalar=rs[:, col:col + 1],
            in1=gb[:, half * 512:(half + 1) * 512], op0=ALU.mult, op1=ALU.mult),
            reads=[("rs", col), gkey], writes=[pkeys[half], (tmpkey, half)])
    S.op("pool", lambda h: h.tensor_tensor(out=hres, in0=hres, in1=tmp[:, :], op=ALU.add),
         reads=[(tmpkey, 0), (tmpkey, 1)], writes=[hkey])


def phase_outproj(k, layer):
    nc, S = k.nc, k.S
    NHB = 7
    with ExitStack() as es:
        tg = k.newtag()
        sb = lambda n, s, d: es.enter_context(nc.sbuf_tensor(n + tg, s, d))
        banks = [es.enter_context(nc.psum_tensor("bk%d" % i + tg, [128, 512], F32)) for i in range(8)]
        nkc = 4 if layer == 0 else 8
        Wo = sb("Wo", [128, nkc, 1024], BF16)
        g1b = sb("g1b", [128, 1024], F32)
        hres = [sb("hres%d" % i, [128, 1024], F32) for i in range(NHB)]
        tmp = [sb("tmp%d" % i, [128, 1024], F32) for i in range(2)]
        oT = [sb("oT%d" % i, [128, nkc, 128], BF16) for i in range(2)]
        ss = sb("ss", [128, 128], F32)
        ms = sb("ms", [128, 128], F32)
        rs = sb("rs", [128, 64], F32)
        if layer == 0:
            Ul = [[sb("Ul%d_%d" % (i, g), [128, 520], F32) for g in range(3)] for i in range(3)]
            Us = [sb("Us%d" % i, [128, 520], F32) for i in range(2)]
            rden = [sb("rden%d" % i, [128, 8], F32) for i in range(2)]
            obf = [sb("obf%d" % i, [128, 512], BF16) for i in range(2)]
            wsrc = k.d["w_out_a"][0]
            hin = k.d["x"]
            hout = k.d["H1"]
        else:
            obf = [sb("obf%d" % i, [128, 1024], BF16) for i in range(3)]
            wsrc = k.d["w_out_b"][0]
            hin = k.d["H2"]
            hout = k.d["H1"]
        wkeys = load_cast_w(k, Wo, "Wo", wsrc, nkc, 1024, "Wo")
        S.op("sp", lambda h: h.dma_start(out=g1b[:], in_=k.d["norm_g"][layer, 1, :].partition_broadcast(128)), writes=["g1b"], dma="g1b")
        mbanks = ((2, 3), (4, 5), (6, 7))

        def s0(T):
            hs = T % NHB
            S.op("sp", lambda h: h.dma_start(out=hres[hs][:], in_=hin[T * 128:(T + 1) * 128, :]), writes=[("hres", hs)], dma=("hres", hs))
            if layer == 0:
                us = T % 3
                for g in range(3):
                    S.op("sp", lambda h, g=g: h.dma_start(out=Ul[us][g][:], in_=k.d["U%d" % g][T * 128:(T + 1) * 128, :]),
                         writes=[("Ul", us, g)], dma=("Ul", us, g))
            else:
                os_ = T % 3
                S.op("sp", lambda h: h.dma_start(out=obf[os_][:], in_=k.d["O1"][T * 128:(T + 1) * 128, :]), writes=[("obf", os_)], dma=("obf", os_))

        def s1(T):
            if layer != 0:
                return
            ul = T % 3
            us = T % 2
            S.op("dve", lambda h: h.tensor_tensor(out=Us[us][:], in0=Ul[ul][0][:], in1=Ul[ul][1][:], op=ALU.add),
                 reads=[("Ul", ul, 0), ("Ul", ul, 1)], writes=[("Us", us)])
            S.op("dve", lambda h: h.tensor_tensor(out=Us[us][:], in0=Us[us][:], in1=Ul[ul][2][:], op=ALU.add),
                 reads=[("Ul", ul, 2)], writes=[("Us", us)])
            usv = Us[us][:, :].rearrange("p (h e) -> p h e", h=8)
            S.op("dve", lambda h: h.reciprocal(out=rden[us][:, :], in_=usv[:, :, 64]), reads=[("Us", us)], writes=[("rden", us)])
            S.op("dve", lambda h: h.tensor_tensor(out=obf[us][:, :].rearrange("p (h e) -> p h e", h=8), in0=usv[:, :, 0:64],
                                                  in1=rden[us][:, :].unsqueeze(2).to_broadcast([128, 8, 64]), op=ALU.mult),
                 reads=[("Us", us), ("rden", us)], writes=[("obf", us)])

        def s2(T):
            if layer == 0:
                ok = ("obf", T % 2)
                osrc = obf[T % 2]
            else:
                ok = ("obf", T % 3)
                osrc = obf[T % 3]
            bi = T % 2
            for kc in range(nkc):
                S.op("pe", lambda h, kc=kc: h.transpose(out=bank_bf16(banks[bi], 8)[:, kc, :], in_=osrc[:, kc * 128:(kc + 1) * 128], identity=k.identb[:]),
                     reads=[ok, "identb"], writes=[("bank", bi)])
            S.op("act", lambda h: h.copy(out=oT[T % 2][:, :, :], in_=bank_bf16(banks[bi], 8)[:, 0:nkc, :]),
                 writes=[("bank", bi), ("oT", T % 2)])

        def s3(T):
            pb = mbanks[T % 3]
            col = T % 64
            for half in range(2):
                for kc in range(nkc):
                    S.op("pe", lambda h, half=half, kc=kc: h.matmul(
                        banks[pb[half]][:, :], lhsT=oT[T % 2][:, kc, :], rhs=Wo[:, kc, half * 512:(half + 1) * 512],
                        start=(kc == 0), stop=(kc == nkc - 1)),
                        reads=[("oT", T % 2)] + wkeys, writes=[("bank", pb[half])])
            for half in range(2):
                S.op("act", lambda h, half=half: h.activation(out=k.junk[:, 0:512], in_=banks[pb[half]][:, :], func=AF.Square,
                                                              accum_out=ss[:, 2 * col + half:2 * col + half + 1]),
                     writes=[("bank", pb[half]), ("ss", 2 * col + half)])

        def s4(T):
            col = T % 64
            S.op("dve", lambda h: h.tensor_tensor(out=ms[:, 2 * col:2 * col + 1], in0=ss[:, 2 * col:2 * col + 1], in1=ss[:, 2 * col + 1:2 * col + 2], op=ALU.add),
                 reads=[("ss", 2 * col), ("ss", 2 * col + 1)], writes=[("ms", 2 * col)])
            rstd_from_ss(k, ms[:, 2 * col:2 * col + 1], ms[:, 2 * col + 1:2 * col + 2], rs[:, col:col + 1], k.nhalf[:, 0:1],
                         [("ms", 2 * col)], ("ms", 2 * col + 1), ("rs", col), 1.0 / D, 1e-6)

        def s5(T):
            pb = mbanks[T % 3]
            col = T % 64
            hs = T % NHB
            tb = tmp[T % 2]
            for half in range(2):
                S.op("dve", lambda h, half=half: h.scalar_tensor_tensor(
                    out=tb[:, half * 512:(half + 1) * 512], in0=banks[pb[half]][:, :], scalar=rs[:, col:col + 1],
                    in1=g1b[:, half * 512:(half + 1) * 512], op0=ALU.mult, op1=ALU.mult),
                    reads=[("rs", col), "g1b"], writes=[("bank", pb[half]), ("tmp", T % 2, half)])
            S.op("pool", lambda h: h.tensor_tensor(out=hres[hs][:, :], in0=hres[hs][:, :], in1=tb[:, :], op=ALU.add),
                 reads=[("tmp", T % 2, 0), ("tmp", T % 2, 1)], writes=[("hres", hs)])
            S.op("sp", lambda h: h.dma_start(out=hout[T * 128:(T + 1) * 128, :], in_=hres[hs][:]), reads=[("hres", hs)], dma=("hout", hs))

        stages = (s0, s1, s2, s3, s4, s5)
        for i in range(NT + len(stages) - 1):
            for si, fn in enumerate(stages):
                T = i - si
                if 0 <= T < NT:
                    fn(T)
        emit_block(k)


def phase_ffn(k, layer, hin, hout):
    nc, S = k.nc, k.S
    NS = SEQ // 256
    NCH = 2 * NFC
    NCB = 6
    with ExitStack() as es:
        tg = k.newtag()
        sb = lambda n, s, d: es.enter_context(nc.sbuf_tensor(n + tg, s, d))
        banks = [es.enter_context(nc.psum_tensor("bk%d" % i + tg, [128, 512], F32)) for i in range(8)]
        Wup = sb("Wup", [128, 8, 2 * DFF], BF16)
        Wdn = sb("Wdn", [128, NFC, 1024], BF16)
        g2b = sb("g2b", [128, 1024], F32)
        g3b = sb("g3b", [128, 1024], F32)
        cp = sb("cp", [128, NCH, 4], F32)
        hres = [sb("hres%d" % i, [128, 2, 1024], F32) for i in range(3)]
        xn = [sb("xn%d" % i, [128, 1024], BF16) for i in range(2)]
        xnT = [sb("xnT%d" % i, [128, 8, 258], BF16) for i in range(2)]
        Cb = [sb("Cb%d" % i, [128, 256], F32) for i in range(NCB)]
        Gb = [sb("Gb%d" % i, [128, 256], F32) for i in range(3)]
        gT = sb("gT", [128, NFC, 256], BF16)
        tmp = sb("tmp", [128, 1024], F32)
        ss = sb("ss", [128, 128], F32)
        ms = sb("ms", [128, 128], F32)
        rs = sb("rs", [128, 64], F32)
        ss2 = sb("ss2", [128, 64], F32)
        ms2 = sb("ms2", [128, 64], F32)
        rs2 = sb("rs2", [128, 64], F32)

        wup_keys = load_cast_w(k, Wup, "Wup", k.d["w_up"][layer], 8, 2 * DFF, "Wup")
        wdn_keys = load_cast_w(k, Wdn, "Wdn", k.d["w_down"][layer], NFC, 1024, "Wdn")
        S.op("sp", lambda h: h.dma_start(out=g2b[:], in_=k.d["norm_g"][layer, 2, :].partition_broadcast(128)), writes=["g2b"], dma="g2b")
        S.op("sp", lambda h: h.dma_start(out=g3b[:], in_=k.d["norm_g"][layer, 3, :].partition_broadcast(128)), writes=["g3b"], dma="g3b")
        S.op("sp", lambda h: h.dma_start(out=cp[:], in_=k.d["convp"][layer]), writes=["cp"], dma="cp")
        S.op("pool", lambda h: h.memset(xnT[0][:, :, 0:2], 0.0), writes=[("xnT", 0, "h")])

        def load(s):
            hs = s % 3
            for tt in range(2):
                T = s * 2 + tt
                S.op("sp", lambda h, tt=tt, T=T: h.dma_start(out=hres[hs][:, tt, :], in_=hin[T * 128:(T + 1) * 128, :]),
                     writes=[("hres", hs, tt)], dma=("hres", hs, tt))

        def prenorm_a(s):
            hs = s % 3
            for tt in range(2):
                T = s * 2 + tt
                col = T % 64
                S.op("act", lambda h, tt=tt, col=col: h.activation(out=k.junk[:], in_=hres[hs][:, tt, :], func=AF.Square,
                                                                   accum_out=ss2[:, col:col + 1]),
                     reads=[("hres", hs, tt)], writes=[("ss2", col)])
                rstd_from_ss(k, ss2[:, col:col + 1], ms2[:, col:col + 1], rs2[:, col:col + 1], k.nhalf[:, 0:1],
                             [("ss2", col)], ("ms2", col), ("rs2", col), 1.0 / D, 1e-6)
                S.op("dve", lambda h, tt=tt, col=col: h.scalar_tensor_tensor(
                    out=xn[tt][:], in0=hres[hs][:, tt, :], scalar=rs2[:, col:col + 1], in1=g2b[:], op0=ALU.mult, op1=ALU.mult),
                    reads=[("hres", hs, tt), ("rs2", col), "g2b"], writes=[("xn", tt)])

        def prenorm_b(s):
            xb = s % 2
            if s > 0:
                S.op("pool", lambda h: h.tensor_copy(out=xnT[xb][:, :, 0:2], in_=xnT[1 - xb][:, :, 256:258]),
                     reads=[("xnT", 1 - xb, 1)], writes=[("xnT", xb, "h")])
            for tt in range(2):
                for dc in range(8):
                    S.op("pe", lambda h, tt=tt, dc=dc: h.transpose(out=bank_bf16(banks[0], 8)[:, dc, :], in_=xn[tt][:, dc * 128:(dc + 1) * 128],
                                                                  identity=k.identb[:]),
                         reads=[("xn", tt), "identb"], writes=[("bank", 0)])
                S.op("act", lambda h, tt=tt: h.copy(out=xnT[xb][:, :, 2 + tt * 128:2 + (tt + 1) * 128], in_=bank_bf16(banks[0], 8)),
                     writes=[("bank", 0), ("xnT", xb, tt)])

        def stA(s, n):
            i, which = divmod(n, 2)
            fc = i + which * NFC
            bi = 1 + n % 3
            xb = s % 2
            for dc in range(8):
                S.op("pe", lambda h, fc=fc, dc=dc, bi=bi: h.matmul(
                    banks[bi][:, 0:258], lhsT=Wup[:, dc, fc * 128:(fc + 1) * 128], rhs=xnT[xb][:, dc, :], start=(dc == 0), stop=(dc == 7)),
                    reads=[("xnT", xb, 0), ("xnT", xb, 1), ("xnT", xb, "h")] + wup_keys, writes=[("bank", bi)])

        def stB(s, n):
            i, which = divmod(n, 2)
            fc = i + which * NFC
            bi = 1 + n % 3
            ci = n % NCB
            S.op("act", lambda h: h.activation(out=Cb[ci][:, :], in_=banks[bi][:, 2:258], func=AF.Identity,
                                               scale=cp[:, fc, 2:3], bias=cp[:, fc, 3:4]),
                 reads=["cp"], writes=[("bank", bi), ("Cb", ci)])

        def stC(s, n):
            i, which = divmod(n, 2)
            fc = i + which * NFC
            bi = 1 + n % 3
            ci = n % NCB
            S.op("dve", lambda h: h.scalar_tensor_tensor(
                out=Cb[ci][:, :], in0=banks[bi][:, 1:257], scalar=cp[:, fc, 1:2], in1=Cb[ci][:, :], op0=ALU.mult, op1=ALU.add),
                reads=["cp"], writes=[("bank", bi), ("Cb", ci)])
            S.op("dve", lambda h: h.scalar_tensor_tensor(
                out=Cb[ci][:, :], in0=banks[bi][:, 0:256], scalar=cp[:, fc, 0:1], in1=Cb[ci][:, :], op0=ALU.mult, op1=ALU.add),
                reads=["cp"], writes=[("bank", bi), ("Cb", ci)])

        def stD(s, n):
            i, which = divmod(n, 2)
            ci = n % NCB
            gi = i % 3
            if which == 0:
                S.op("act", lambda h: h.activation(out=Gb[gi][:, :], in_=Cb[ci][:, :], func=AF.Gelu),
                     reads=[("Cb", ci)], writes=[("Gb", gi)])
            else:
                S.op("pool", lambda h: h.tensor_tensor(out=gT[:, i, :], in0=Gb[gi][:, :], in1=Cb[ci][:, :], op=ALU.mult),
                     reads=[("Gb", gi), ("Cb", ci)], writes=[("gT", i)])

        def down_mm(s):
            for tt in range(2):
                pb = (4, 5) if tt == 0 else (6, 7)
                for half in range(2):
                    for i in range(NFC):
                        S.op("pe", lambda h, tt=tt, half=half, i=i, pb=pb: h.matmul(
                            banks[pb[half]][:, :], lhsT=gT[:, i, tt * 128:(tt + 1) * 128], rhs=Wdn[:, i, half * 512:(half + 1) * 512],
                            start=(i == 0), stop=(i == NFC - 1)),
                            reads=[("gT", i)] + wdn_keys, writes=[("bank", pb[half])])

        def down_post(s):
            hs = s % 3
            for tt in range(2):
                T = s * 2 + tt
                pb = (4, 5) if tt == 0 else (6, 7)
                postnorm_residual(k, [banks[pb[0]], banks[pb[1]]], [("bank", pb[0]), ("bank", pb[1])], g3b, "g3b",
                                  hres[hs][:, tt, :], ("hres", hs, tt), tmp, "tmp", (ss, ms, rs), T % 64)
                S.op("sp", lambda h, tt=tt, T=T: h.dma_start(out=hout[T * 128:(T + 1) * 128, :], in_=hres[hs][:, tt, :]),
                     reads=[("hres", hs, tt)], dma=("hout", hs, tt))

        load(0)
        load(1)
        prenorm_a(0)
        prenorm_b(0)
        for s in range(NS):
            for t in range(NCH + 3):
                if t < NCH:
                    stA(s, t)
                if 0 <= t - 1 < NCH:
                    stB(s, t - 1)
                if 0 <= t - 2 < NCH:
                    stC(s, t - 2)
                if 0 <= t - 3 < NCH:
                    stD(s, t - 3)
                if t == 8 and s > 0:
                    down_post(s - 1)
                if t == 9 and s + 2 < NS:
                    load(s + 2)
                if t == 18 and s + 1 < NS:
                    prenorm_a(s + 1)
                if t == 30 and s + 1 < NS:
                    prenorm_b(s + 1)
            down_mm(s)
        down_post(NS - 1)
        emit_block(k)


def phase3(k):
    nc, S = k.nc, k.S
    with ExitStack() as es:
        tg = k.newtag()
        sb = lambda n, s, d: es.enter_context(nc.sbuf_tensor(n + tg, s, d))
        banks = [es.enter_context(nc.psum_tensor("bk%d" % i + tg, [128, 512], F32)) for i in range(8)]
        Wq = sb("Wq", [128, 8, 1024], BF16)
        Wk = sb("Wk", [128, 8, 1024], BF16)
        Wv = sb("Wv", [128, 8, 1024], BF16)
        gqb = sb("gqb", [128, 1024], F32)
        gkb = sb("gkb", [128, 1024], F32)
        ht = [sb("ht%d" % i, [128, 1024], F32) for i in range(3)]
        xq = [sb("xq%d" % i, [128, 1024], BF16) for i in range(2)]
        xk = [sb("xk%d" % i, [128, 1024], BF16) for i in range(2)]
        xqT = [sb("xqT%d" % i, [128, 8, 512], BF16) for i in range(2)]
        xkT = [sb("xkT%d" % i, [128, 8, 512], BF16) for i in range(2)]
        Qs = [sb("Qs%d" % i, [128, 512], BF16) for i in range(4)]
        Vs = [sb("Vs%d" % i, [128, 1024], BF16) for i in range(2)]
        ss = sb("ss", [128, 128], F32)
        ms = sb("ms", [128, 128], F32)
        rs = sb("rs", [128, 128], F32)
        wq_keys = load_cast_w(k, Wq, "Wq", k.d["w_q_b"][0], 8, 1024, "Wq")
        wk_keys = load_cast_w(k, Wk, "Wk", k.d["w_k_shared"], 8, 1024, "Wk")
        wv_keys = load_cast_w(k, Wv, "Wv", k.d["w_v_shared"], 8, 1024, "Wv")
        S.op("sp", lambda h: h.dma_start(out=gqb[:], in_=k.d["norm_g"][1, 0, :].partition_broadcast(128)), writes=["gqb"], dma="gqb")
        S.op("sp", lambda h: h.dma_start(out=gkb[:], in_=k.d["kv_norm_g"].partition_broadcast(128)), writes=["gkb"], dma="gkb")
        hin = k.d["H2"]

        def front(st):
            for t4 in range(4):
                T = st * 4 + t4
                sl = T % 3
                col = T % 128
                S.op("sp", lambda h, T=T, sl=sl: h.dma_start(out=ht[sl][:], in_=hin[T * 128:(T + 1) * 128, :]), writes=[("ht", sl)], dma=("ht", sl))
                S.op("act", lambda h, sl=sl, col=col: h.activation(out=k.junk[:], in_=ht[sl][:], func=AF.Square, accum_out=ss[:, col:col + 1]),
                     reads=[("ht", sl)], writes=[("ss", col)])
                rstd_from_ss(k, ss[:, col:col + 1], ms[:, col:col + 1], rs[:, col:col + 1], k.nhalf[:, 0:1],
                             [("ss", col)], ("ms", col), ("rs", col), 1.0 / D, 1e-6)
                xs = T % 2
                for (xb, gb, gk, nm) in ((xq, gqb, "gqb", "xq"), (xk, gkb, "gkb", "xk")):
                    S.op("dve", lambda h, sl=sl, col=col, xb=xb, gb=gb, xs=xs: h.scalar_tensor_tensor(
                        out=xb[xs][:], in0=ht[sl][:], scalar=rs[:, col:col + 1], in1=gb[:], op0=ALU.mult, op1=ALU.mult),
                        reads=[("ht", sl), ("rs", col), gk], writes=[(nm, xs)])
                for (xb, xT, nm, bi) in ((xq, xqT, "xq", 0), (xk, xkT, "xk", 1)):
                    for dc in range(8):
                        S.op("pe", lambda h, dc=dc, xb=xb, bi=bi, xs=xs: h.transpose(out=bank_bf16(banks[bi], 8)[:, dc, :], in_=xb[xs][:, dc * 128:(dc + 1) * 128],
                                                                            identity=k.identb[:]),
                             reads=[(nm, xs), "identb"], writes=[("bank", bi)])
                    S.op("act", lambda h, t4=t4, xT=xT, bi=bi: h.copy(out=xT[st % 2][:, :, t4 * 128:(t4 + 1) * 128], in_=bank_bf16(banks[bi], 8)),
                         writes=[("bank", bi), (nm + "T", st % 2, t4)])

        pc = [0]

        def proj(st):
            for (W, wkeys, xT, nm, dst, scale) in ((Wq, wq_keys, xqT, "xq", k.d["QT1"], 0.125), (Wk, wk_keys, xkT, "xk", k.d["KT1"], None)):
                xkeys = [(nm + "T", st % 2, t4) for t4 in range(4)]
                for hd in range(8):
                    bi = 2 + pc[0] % 4
                    qi = pc[0] % 4
                    pc[0] += 1
                    for dc in range(8):
                        S.op("pe", lambda h, hd=hd, dc=dc, bi=bi, W=W, xT=xT: h.matmul(
                            banks[bi][:, :], lhsT=W[:, dc, hd * 128:(hd + 1) * 128], rhs=xT[st % 2][:, dc, :], start=(dc == 0), stop=(dc == 7)),
                            reads=xkeys + wkeys, writes=[("bank", bi)])
                    if scale is not None:
                        S.op("act", lambda h, bi=bi, qi=qi: h.activation(out=Qs[qi][:, :], in_=banks[bi][:, :], func=AF.Copy, scale=0.125),
                             writes=[("bank", bi), ("Qs", qi)])
                    else:
                        S.op("dve", lambda h, bi=bi, qi=qi: h.tensor_copy(out=Qs[qi][:, :], in_=banks[bi][:, :]),
                             writes=[("bank", bi), ("Qs", qi)])
                    S.op("sp", lambda h, hd=hd, qi=qi, dst=dst: h.dma_start(out=dst[hd, :, st * 512:(st + 1) * 512], in_=Qs[qi][:, :]),
                         reads=[("Qs", qi)], dma=("Qs", qi))
            for t4 in range(4):
                T = st * 4 + t4
                vs = T % 2
                for half in range(2):
                    bi = 2 + pc[0] % 4
                    pc[0] += 1
                    for dc in range(8):
                        S.op("pe", lambda h, t4=t4, half=half, dc=dc, bi=bi: h.matmul(
                            banks[bi][:, :], lhsT=xkT[st % 2][:, dc, t4 * 128:(t4 + 1) * 128], rhs=Wv[:, dc, half * 512:(half + 1) * 512],
                            start=(dc == 0), stop=(dc == 7)),
                            reads=[("xkT", st % 2, t4)] + wv_keys, writes=[("bank", bi)])
                    if half == 0:
                        S.op("act", lambda h, bi=bi, vs=vs: h.copy(out=Vs[vs][:, 0:512], in_=banks[bi][:, :]), writes=[("bank", bi), ("Vs", vs, 0)])
                    else:
                        S.op("dve", lambda h, bi=bi, vs=vs: h.tensor_copy(out=Vs[vs][:, 512:1024], in_=banks[bi][:, :]), writes=[("bank", bi), ("Vs", vs, 1)])
                S.op("sp", lambda h, T=T, vs=vs: h.dma_start(out=k.d["V1"][T * 128:(T + 1) * 128, :], in_=Vs[vs][:, :]),
                     reads=[("Vs", vs, 0), ("Vs", vs, 1)], dma=("Vs", vs))

        front(0)
        for st in range(8):
            if st + 1 < 8:
                front(st + 1)
            proj(st)
        emit_block(k)


def phase4(k, heads=range(8)):
    nc, S = k.nc, k.S
    NSB = 4
    SK = 3
    with ExitStack() as es:
        tg = k.newtag()
        sb = lambda n, s, d: es.enter_context(nc.sbuf_tensor(n + tg, s, d))
        banks = [es.enter_context(nc.psum_tensor("bk%d" % i + tg, [128, 512], F32)) for i in range(8)]
        KTh = [sb("KTh%d" % i, [128, SEQ], BF16) for i in range(2)]
        QZ = [[sb("QZ%d_%d" % (i, c), [128, SEQ], BF16) for c in range(2)] for i in range(2)]
        Vh = [sb("Vh%d" % i, [128, NT, 129], BF16) for i in range(2)]
        Tb = [sb("Tb%d" % i, [128, TBW], F32) for i in range(2)]
        mB = sb("mB", [128, 128], F32)
        Sb = [sb("Sb%d" % i, [128, 512], F32) for i in range(NSB)]
        PT = [sb("PT%d" % i, [128, 512], BF16) for i in range(NSB)]
        Ost = [[sb("Ost%d_%d" % (i, c), [128, 4, 129], F32) for c in range(2)] for i in range(2)]
        Oh = [sb("Oh%d" % i, [128, NT, 128], BF16) for i in range(2)]
        lv = [sb("lv%d" % i, [128, 64], F32) for i in range(4)]
        lsum = sb("lsum", [128, 4], F32)
        lam = sb("lam", [128, 1], F32)
        sgb = sb("sgb", [128, 128], F32)
        rd = sb("rd", [128, 64, 8], F32)
        av = [sb("av%d" % i, [128, 128], F32) for i in range(2)]
        tv = [sb("tv%d" % i, [128, 128], F32) for i in range(2)]
        st1 = sb("st1", [128, 128], F32)
        st2 = sb("st2", [128, 128], F32)
        st3 = sb("st3", [128, 128], F32)

        for i, nm in enumerate(("lam_q1", "lam_k1", "lam_q2", "lam_k2")):
            S.op("sp", lambda h, i=i, nm=nm: h.dma_start(out=lv[i][:], in_=k.d[nm][0, :].partition_broadcast(128)), writes=[("lv", i)], dma=("lv", i))
        S.op("sp", lambda h: h.dma_start(out=sgb[:], in_=k.d["subln_g"][0, :].partition_broadcast(128)), writes=["sgb"], dma="sgb")
        S.op("sp", lambda h: h.dma_start(out=mB[:], in_=k.d["maskB"]), writes=["mB"], dma="mB")
        S.op("dve", lambda h: h.tensor_scalar(out=sgb[:], in0=sgb[:], scalar1=1.0 - LAMBDA_INIT, scalar2=None, op0=ALU.mult),
             reads=["sgb"], writes=["sgb"])
        for i in range(2):
            S.op("dve", lambda h, i=i: h.scalar_tensor_tensor(out=k.junk[:, 0:64], in0=lv[2 * i][:], scalar=1.0, in1=lv[2 * i + 1][:],
                                                            op0=ALU.mult, op1=ALU.mult, accum_out=lsum[:, i:i + 1]),
                 reads=[("lv", 2 * i), ("lv", 2 * i + 1)], writes=[("lsum", i)])
            S.op("act", lambda h, i=i: h.activation(out=lsum[:, 2 + i:3 + i], in_=lsum[:, i:i + 1], func=AF.Exp),
                 reads=[("lsum", i)], writes=[("lsum", 2 + i)])
        S.op("dve", lambda h: h.tensor_tensor(out=lam[:], in0=lsum[:, 2:3], in1=lsum[:, 3:4], op=ALU.subtract),
             reads=[("lsum", 2), ("lsum", 3)], writes=["lam"])
        S.op("dve", lambda h: h.tensor_scalar(out=lam[:], in0=lam[:], scalar1=LAMBDA_INIT, scalar2=None, op0=ALU.add),
             reads=["lam"], writes=["lam"])
        for i in range(2):
            S.op("pool", lambda h, i=i: h.memset(Vh[i][:, :, 128:129], 1.0), writes=[("Vh1", i)])
            S.op("pool", lambda h, i=i: h.memset(QZ[i][0][64:128, :], 0.0), writes=[("QZz", i, 0)])
            S.op("pool", lambda h, i=i: h.memset(QZ[i][1][0:64, :], 0.0), writes=[("QZz", i, 1)])

        def load_head(hi, hd):
            b = hi % 2
            S.op("sp", lambda h: h.dma_start(out=KTh[b][:], in_=k.d["KT1"][hd]), writes=[("KTh", b)], dma=("KTh", b))
            S.op("sp", lambda h: h.dma_start(out=QZ[b][0][0:64, :], in_=k.d["QT1"][hd, 0:64, :]), writes=[("QZ", b, 0)], dma=("QZ", b, 0))
            S.op("sp", lambda h: h.dma_start(out=QZ[b][1][64:128, :], in_=k.d["QT1"][hd, 64:128, :]), writes=[("QZ", b, 1)], dma=("QZ", b, 1))
            S.op("sp", lambda h: h.dma_start(out=Vh[b][:, :, 0:128], in_=k.d["V1"][:, hd * 128:(hd + 1) * 128].rearrange("(t p) e -> p t e", p=128)),
                 writes=[("Vh", b)], dma=("Vh", b))
            S.op("sp", lambda h: h.dma_start(out=Tb[b][:], in_=k.d["biasB"][hd]), writes=[("Tb", b)], dma=("Tb", b))
            S.op("dve", lambda h: h.tensor_tensor(out=Tb[b][:, 0:128], in0=Tb[b][:, 0:128], in1=mB[:], op=ALU.add),
                 reads=["mB", ("Tb", b)], writes=[("Tb", b)])

        obank = (4, 5, 6, 7)
        heads = list(heads)
        items = []
        for hi, hd in enumerate(heads):
            for qs in range(8):
                for c in range(2):
                    for j in range(4 * qs + 4):
                        items.append((hi, hd, qs, c, j))
        N = len(items)
        deferred = {}

        def defer(t, fn):
            deferred.setdefault(t, []).append(fn)

        def geom(it):
            hi, hd, qs, c, j = it
            qt0 = max(0, j - 4 * qs)
            col0 = qt0 * 128
            ncols = 512 - col0
            delta = qs * 512 - j * 128 + col0
            off = min(delta, TB_CONST)
            assert 0 <= off and off + ncols <= TBW
            return qt0, col0, ncols, off

        def stA(n):
            hi, hd, qs, c, j = items[n]
            b = hi % 2
            qt0, col0, ncols, off = geom(items[n])
            bi = n % 4
            S.op("pe", lambda h: h.matmul(
                banks[bi][:, col0:512], lhsT=KTh[b][:, j * 128:(j + 1) * 128], rhs=QZ[b][c][:, qs * 512 + col0:(qs + 1) * 512],
                start=True, stop=True),
                reads=[("KTh", b), ("QZ", b, c), ("QZz", b, c)], writes=[("bank", bi)])

        def stB(n):
            hi, hd, qs, c, j = items[n]
            b = hi % 2
            qt0, col0, ncols, off = geom(items[n])
            bi = n % 4
            si = n % NSB
            S.op("dve", lambda h: h.tensor_tensor(
                out=Sb[si][:, col0:512], in0=banks[bi][:, col0:512], in1=Tb[b][:, off:off + ncols], op=ALU.add),
                reads=[("Tb", b)], writes=[("bank", bi), ("Sb", si)])

        def stC(n):
            qt0, col0, ncols, off = geom(items[n])
            si = n % NSB
            S.op("act", lambda h: h.activation(out=PT[si][:, col0:512], in_=Sb[si][:, col0:512], func=AF.Exp),
                 reads=[("Sb", si)], writes=[("PT", si)])

        def evac(hi, qs, c, qt):
            osl = qs % 2
            if qt % 2 == 0:
                S.op("act", lambda h: h.copy(out=Ost[osl][c][:, qt, :], in_=banks[obank[qt]][:, 0:129]),
                     writes=[("bank", obank[qt]), ("Ost", osl, c, qt)])
            else:
                S.op("dve", lambda h: h.tensor_copy(out=Ost[osl][c][:, qt, :], in_=banks[obank[qt]][:, 0:129]),
                     writes=[("bank", obank[qt]), ("Ost", osl, c, qt)])

        cc = [0]

        def combine(hi, hd, qs):
            b = hi % 2
            osl = qs % 2
            ci = cc[0] % 64
            cc[0] += 1
            okeys = [("Ost", osl, c_, qt) for c_ in range(2) for qt in range(4)]
            S.op("dve", lambda h: h.reciprocal(out=rd[:, ci, 0:4], in_=Ost[osl][0][:, :, 128]), reads=okeys, writes=[("rd", ci, 0)])
            S.op("dve", lambda h: h.reciprocal(out=rd[:, ci, 4:8], in_=Ost[osl][1][:, :, 128]), reads=okeys, writes=[("rd", ci, 1)])
            S.op("dve", lambda h: h.tensor_scalar(out=rd[:, ci, 4:8], in0=rd[:, ci, 4:8], scalar1=lam[:, 0:1], scalar2=None, op0=ALU.mult),
                 reads=[("rd", ci, 1), "lam"], writes=[("rd", ci, 1)])
            for qt in range(4):
                T = qs * 4 + qt
                col = T % 128
                a_ = av[qt % 2]
                t_ = tv[qt % 2]
                S.op("pool", lambda h, qt=qt, t_=t_: h.tensor_scalar(out=t_[:], in0=Ost[osl][1][:, qt, 0:128], scalar1=rd[:, ci, 4 + qt:5 + qt],
                                                                    scalar2=0.0, op0=ALU.mult, op1=ALU.add),
                     reads=okeys + [("rd", ci, 1)], writes=[("tv", qt % 2)])
                S.op("dve", lambda h, qt=qt, a_=a_, t_=t_: h.scalar_tensor_tensor(
                    out=a_[:], in0=Ost[osl][0][:, qt, 0:128], scalar=rd[:, ci, qt:qt + 1], in1=t_[:], op0=ALU.mult, op1=ALU.subtract),
                    reads=okeys + [("rd", ci, 0), ("tv", qt % 2)], writes=[("av", qt % 2)])
                S.op("dve", lambda h, a_=a_, col=col: h.scalar_tensor_tensor(
                    out=k.junk[:, 0:128], in0=a_[:], scalar=1.0, in1=a_[:], op0=ALU.mult, op1=ALU.mult, accum_out=st1[:, col:col + 1]),
                    reads=[("av", qt % 2)], writes=[("st1", col)])
                rstd_from_ss(k, st1[:, col:col + 1], st2[:, col:col + 1], st3[:, col:col + 1], k.nhalf[:, 0:1],
                             [("st1", col)], ("st2", col), ("st3", col), 1.0 / 128, 1e-5)
                S.op("pool", lambda h, a_=a_, col=col, T=T: h.tensor_scalar(
                    out=a_[:], in0=a_[:], scalar1=st3[:, col:col + 1], scalar2=0.0, op0=ALU.mult, op1=ALU.add),
                    reads=[("av", qt % 2), ("st3", col)], writes=[("av", qt % 2)])
                S.op("pool", lambda h, a_=a_, T=T: h.tensor_tensor(out=Oh[b][:, T, :], in0=a_[:], in1=sgb[:], op=ALU.mult),
                     reads=[("av", qt % 2), "sgb"], writes=[("Oh", b, T)])
            if qs == 7:
                S.op("sp", lambda h: h.dma_start(out=k.d["O1"][:, hd * 128:(hd + 1) * 128].rearrange("(t p) e -> p t e", p=128), in_=Oh[b][:]),
                     reads=[("Oh", b, T) for T in range(NT)], dma=("Oh", b))

        def stD(n, t):
            hi, hd, qs, c, j = items[n]
            b = hi % 2
            qt0, col0, ncols, off = geom(items[n])
            si = n % NSB
            for qt in range(qt0, 4):
                last = (j == 4 * qs + qt)
                S.op("pe", lambda h, qt=qt, last=last: h.matmul(
                    banks[obank[qt]][:, 0:129], lhsT=PT[si][:, qt * 128:(qt + 1) * 128], rhs=Vh[b][:, j, :],
                    start=(j == 0), stop=last),
                    reads=[("PT", si), ("Vh", b), ("Vh1", b)], writes=[("bank", obank[qt])])
                if last:
                    defer(t + 1, lambda hi=hi, qs=qs, c=c, qt=qt: evac(hi, qs, c, qt))
            if c == 1 and j == 4 * qs + 3:
                defer(t + 6, lambda hi=hi, hd=hd, qs=qs: combine(hi, hd, qs))

        load_head(0, heads[0])
        for t in range(N + SK + 8):
            for fn in deferred.pop(t, []):
                fn()
            if t < N:
                it = items[t]
                if it[2] == 0 and it[3] == 1 and it[4] == 0 and it[0] + 1 < len(heads):
                    load_head(it[0] + 1, heads[it[0] + 1])
                stA(t)
            if 0 <= t - 1 < N:
                stB(t - 1)
            if 0 <= t - 2 < N:
                stC(t - 2)
            if 0 <= t - SK < N:
                stD(t - SK, t)
        assert not deferred
        emit_block(k)


ALL_PHASES = ("p1", "p2a", "ffn0", "p3", "p4", "p5a", "ffn1")


def build(phases=ALL_PHASES, kinds=None):
    kinds = kinds or {}
    nc = bass.Bass("TRN2", target_bir_lowering=False)
    k = K()
    k.nc = nc
    d = {}

    def dt(name, shape, dtype, kind):
        d[name] = nc.dram_tensor(name, list(shape), dtype, kind=kinds.get(name, kind)).ap()

    EI = "ExternalInput"
    dt("x", [SEQ, D], F32, EI)
    dt("norm_g", [2, 4, D], F32, EI)
    dt("w_in_a", [1, D, 4608], F32, EI)
    dt("w_out_a", [1, 512, D], F32, EI)
    dt("kv_norm_g", [D], F32, EI)
    dt("w_k_shared", [D, D], F32, EI)
    dt("w_v_shared", [D, D], F32, EI)
    dt("w_q_b", [1, D, D], F32, EI)
    for nm in ("lam_q1", "lam_k1", "lam_q2", "lam_k2"):
        dt(nm, [1, 64], F32, EI)
    dt("subln_g", [1, 128], F32, EI)
    dt("w_out_b", [1, D, D], F32, EI)
    dt("w_up", [2, D, 2 * DFF], F32, EI)
    dt("w_down", [2, DFF, D], F32, EI)
    dt("convp", [2, 128, 2 * NFC, 4], F32, EI)
    dt("ident", [128, 128], F32, EI)
    dt("biasA", [3, NH, 128, 256], F32, EI)
    dt("maskA", [128, 256], F32, EI)
    dt("biasB", [NH, 128, TBW], F32, EI)
    dt("maskB", [128, 128], F32, EI)
    for g in range(3):
        dt("U%d" % g, [SEQ, 520], F32, "Internal")
    dt("H1", [SEQ, D], F32, "Internal")
    dt("H2", [SEQ, D], F32, "Internal")
    dt("QT1", [NH, 128, SEQ], BF16, "Internal")
    dt("KT1", [NH, 128, SEQ], BF16, "Internal")
    dt("V1", [SEQ, D], BF16, "Internal")
    dt("O1", [SEQ, D], BF16, "Internal")
    dt("out", [SEQ, D], F32, "ExternalOutput")
    k.d = d

    with ExitStack() as es:
        k.S = Sched(nc, es)
        S = k.S
        identf = es.enter_context(nc.sbuf_tensor("identf", [128, 128], F32))
        k.identb = es.enter_context(nc.sbuf_tensor("identb", [128, 128], BF16))
        k.nhalf = es.enter_context(nc.sbuf_tensor("nhalf", [128, 8], F32))
        k.junk = es.enter_context(nc.sbuf_tensor("junkg", [128, 1024], BF16))
        S.op("sp", lambda h: h.dma_start(out=identf[:], in_=d["ident"]), writes=["identf"], dma="identf")
        S.op("dve", lambda h: h.tensor_copy(out=k.identb[:], in_=identf[:]), reads=["identf"], writes=["identb"])
        S.op("dve", lambda h: h.memset(k.nhalf[:], -0.5), writes=["nhalf"])
        emit_block(k)
        for ph in phases:
            if ph == "p1":
                phase1(k)
            elif ph.startswith("p1g"):
                phase1(k, groups=(int(ph[3]),))
            elif ph == "p2a":
                phase_outproj(k, 0)
            elif ph == "ffn0":
                phase_ffn(k, 0, d["H1"], d["H2"])
            elif ph == "p3":
                phase3(k)
            elif ph == "p4":
                phase4(k)
            elif ph.startswith("p4h"):
                phase4(k, heads=[int(ph[3])])
            elif ph == "p5a":
                phase_outproj(k, 1)
            elif ph == "ffn1":
                phase_ffn(k, 1, d["H1"], d["out"])
    return nc


def _bucket(n):
    n = np.maximum(n, 0)
    nf = np.maximum(n, 1).astype(np.float32)
    large = 16 + (np.log(nf / np.float32(16)) / np.float32(math.log(2048 / 16)) * np.float32(16)).astype(np.int32)
    large = np.minimum(large, 31)
    return np.where(n < 16, n, large).astype(np.int64)


def host_consts(rel_bias_table, conv_w, conv_b):
    tab = np.asarray(rel_bias_table, dtype=np.float32)
    kl = np.arange(128)[:, None]
    qc = np.arange(256)[None, :]
    du = qc - kl
    biasA = np.stack([np.take(tab, _bucket(r * np.clip(du, 0, 128)), axis=1) for r in DIL]).astype(np.float32)
    maskA = np.where((du >= 0) & (du <= 128), np.float32(0.0), np.float32(NEG)).astype(np.float32)
    jj = np.arange(TBW)[None, :]
    pp = np.arange(128)[:, None]
    biasB = np.take(tab, _bucket(np.maximum(jj - pp, 0)), axis=1).astype(np.float32)
    maskB = np.where(jj[:, :128] - pp >= 0, np.float32(0.0), np.float32(NEG)).astype(np.float32)
    cw = np.asarray(conv_w, dtype=np.float32)
    cb = np.asarray(conv_b, dtype=np.float32)
    convp = np.concatenate([cw, cb[:, None, :]], axis=1)
    convp = np.ascontiguousarray(convp.reshape(2, 4, 2 * NFC, 128).transpose(0, 3, 2, 1))
    return dict(biasA=np.ascontiguousarray(biasA), maskA=maskA, biasB=np.ascontiguousarray(biasB), maskB=np.ascontiguousarray(maskB),
                convp=convp, ident=np.eye(128, dtype=np.float32))


_NC_CACHE = {}


def kernel(x, rel_bias_table, norm_g, w_in_a, w_out_a, kv_norm_g, w_k_shared, w_v_shared, w_q_b,
           lam_q1, lam_k1, lam_q2, lam_k2, subln_g, w_out_b, w_up, conv_w, conv_b, w_down):
    f = lambda a: np.ascontiguousarray(np.asarray(a, dtype=np.float32))
    hc = host_consts(rel_bias_table, conv_w, conv_b)
    shared = dict(norm_g=f(norm_g), w_in_a=f(w_in_a), w_out_a=f(w_out_a), kv_norm_g=f(kv_norm_g), w_k_shared=f(w_k_shared),
                  w_v_shared=f(w_v_shared), w_q_b=f(w_q_b), lam_q1=f(lam_q1), lam_k1=f(lam_k1), lam_q2=f(lam_q2), lam_k2=f(lam_k2),
                  subln_g=f(subln_g), w_out_b=f(w_out_b), w_up=f(w_up), w_down=f(w_down), **hc)
    x = f(x)
    if "nc" not in _NC_CACHE:
        _NC_CACHE["nc"] = build()
    nc = _NC_CACHE["nc"]
    in_maps = [dict(shared, x=x[b]) for b in range(8)]
    res = run_bass_kernel_spmd(nc, in_maps, core_ids=list(range(8)))
    return np.stack([np.asarray(res.results[b]["out"], dtype=np.float32) for b in range(8)], axis=0)
```

```python
import math
from contextlib import ExitStack

import numpy as np
import concourse.bass as bass
import concourse.mybir as mybir
from concourse.bass_utils import run_bass_kernel_spmd

F32 = mybir.dt.float32
BF16 = mybir.dt.bfloat16
AF = mybir.ActivationFunctionType
ALU = mybir.AluOpType

SEQ = 4096
D = 1024
NT = SEQ // 128
NH = 8
DIL = (1, 4, 16)
DFF = 2816
NFC = DFF // 128
NEG = -30000.0
LAMBDA_INIT = 0.8 - 0.6 * math.exp(-0.3 * 1)
TBW = 2176
TB_CONST = 1664

ENGS = ("pe", "act", "dve", "pool", "sp")


class _Op:
    __slots__ = ("eng", "fn", "deps", "idx", "dma", "sig", "sigval", "dsem", "dval")

    def __init__(self, eng, fn, dma):
        self.eng = eng
        self.fn = fn
        self.deps = []
        self.dma = dma
        self.sig = False
        self.sigval = 0
        self.dsem = None
        self.dval = 0


class Sched:
    def __init__(self, nc, es):
        self.nc = nc
        self.es = es
        self.sems = {e: es.enter_context(nc.semaphore("s_" + e)) for e in ENGS}
        self.q = {e: [] for e in ENGS}
        self.lastw = {}
        self.readers = {}
        self.sigcount = {e: 0 for e in ENGS}
        self.dma_slot = {}
        self.free_slots = {}
        self.nsem = 0

    def _slot(self, key, eng):
        if key not in self.dma_slot:
            fs = self.free_slots.setdefault(eng, [])
            if fs:
                self.dma_slot[key] = fs.pop()
            else:
                self.nsem += 1
                self.dma_slot[key] = [self.es.enter_context(self.nc.semaphore("d%d" % self.nsem)), 0, eng]
        assert self.dma_slot[key][2] == eng
        return self.dma_slot[key]

    def op(self, eng, fn, reads=(), writes=(), dma=None):
        o = _Op(eng, fn, dma is not None)
        o.idx = len(self.q[eng])
        if dma is not None:
            s = self._slot(dma, eng)
            s[1] += 16
            o.dsem, o.dval = s[0], s[1]
        dep = {}

        def add(d):
            if d is None or d is o:
                return
            k = ("d", id(d.dsem)) if d.dma else ("e", d.eng)
            cur = dep.get(k)
            if cur is None or (cur.dval < d.dval if d.dma else cur.idx < d.idx):
                dep[k] = d

        for b in reads:
            add(self.lastw.get(b))
        for b in writes:
            add(self.lastw.get(b))
            for r in self.readers.get(b, {}).values():
                add(r)
        o.deps = list(dep.values())
        me = ("d", id(o.dsem)) if o.dma else ("e", eng)
        for b in reads:
            self.readers.setdefault(b, {})[me] = o
        for b in writes:
            self.lastw[b] = o
            self.readers[b] = {}
        self.q[eng].append(o)
        return o

    def finalize(self):
        for e in ENGS:
            for o in self.q[e]:
                for d in o.deps:
                    if d.dma:
                        continue
                    if d.eng == "pe" and o.eng == "pe":
                        continue
                    d.sig = True
        for e in ENGS:
            c = self.sigcount[e]
            for o in self.q[e]:
                if o.sig and not o.dma:
                    c += 1
                    o.sigval = c
            self.sigcount[e] = c

    def emit_engine(self, e, h):
        known = {}
        for o in self.q[e]:
            need = {}
            for d in o.deps:
                if d.dma:
                    k = ("d", id(d.dsem))
                    if need.get(k, (None, 0))[1] < d.dval:
                        need[k] = (d.dsem, d.dval)
                else:
                    if d.eng == "pe" and e == "pe":
                        continue
                    k = ("e", d.eng)
                    if need.get(k, (None, 0))[1] < d.sigval:
                        need[k] = (self.sems[d.eng], d.sigval)
            for k, (sem, val) in need.items():
                if known.get(k, 0) >= val:
                    continue
                h.wait_ge(sem, val)
                known[k] = val
            ins = o.fn(h)
            if o.dma:
                ins.then_inc(o.dsem, 16)
            elif o.sig:
                ins.then_inc(self.sems[e], 1)

    def final_dma_waits(self, h):
        for key, sl in self.dma_slot.items():
            if sl[1]:
                h.wait_ge(sl[0], sl[1])

    def reset_phase(self):
        for sl in self.dma_slot.values():
            self.free_slots.setdefault(sl[2], []).append(sl)
        self.dma_slot = {}
        self.q = {e: [] for e in ENGS}
        self.lastw = {}
        self.readers = {}


class K:
    ntag = 0

    def newtag(self):
        self.ntag += 1
        return "_p%d" % self.ntag


def emit_block(k):
    S, nc = k.S, k.nc
    S.finalize()
    with nc.Block() as block:
        @block.sync
        def _(h):
            S.emit_engine("sp", h)
            S.final_dma_waits(h)

        @block.tensor
        def _(h):
            S.emit_engine("pe", h)

        @block.scalar
        def _(h):
            S.emit_engine("act", h)

        @block.vector
        def _(h):
            S.emit_engine("dve", h)

        @block.gpsimd
        def _(h):
            S.emit_engine("pool", h)
    S.reset_phase()


def bank_bf16(bank, c):
    return bank[:, :].bitcast(BF16).rearrange("p (c t) -> p c t", c=c)


def load_cast_w(k, dst, dst_key, w2d, nchunk, ncols, slot):
    S = k.S
    wv = w2d.rearrange("(c p) n -> p c n", p=128)
    keys = []
    i = 0
    for c in range(nchunk):
        for c0 in range(0, ncols, 2048):
            c1 = min(ncols, c0 + 2048)
            key = (dst_key, c, c0)
            keys.append(key)
            S.op("pool", lambda h, c=c, c0=c0, c1=c1: h.dma_start(out=dst[:, c, c0:c1], in_=wv[:, c, c0:c1]),
                 writes=[key], dma=(slot, i % 4))
            i += 1
    return keys


def rstd_from_ss(k, ss_ap, ms_ap, rstd_ap, nh_ap, keys_ss, key_ms, key_rstd, inv_n, eps):
    S = k.S
    S.op("dve", lambda h: h.tensor_scalar(out=ms_ap, in0=ss_ap, scalar1=inv_n, scalar2=eps, op0=ALU.mult, op1=ALU.add),
         reads=keys_ss, writes=[key_ms])
    S.op("pool", lambda h: h.tensor_tensor(out=rstd_ap, in0=ms_ap, in1=nh_ap, op=ALU.pow),
         reads=[key_ms], writes=[key_rstd])


def phase1(k, groups=(0, 1, 2)):
    nc, S = k.nc, k.S
    with ExitStack() as es:
        tg = k.newtag()
        sb = lambda n, s, d: es.enter_context(nc.sbuf_tensor(n + tg, s, d))
        banks = [es.enter_context(nc.psum_tensor("bk%d" % i + tg, [128, 512], F32)) for i in range(8)]
        g0b = sb("g0b", [128, 1024], F32)
        Wg = sb("Wg", [128, 8, 1536], BF16)
        xt = [sb("xt%d" % i, [128, 1024], F32) for i in range(5)]
        junk = k.junk
        xn = [sb("xn%d" % i, [128, 1024], BF16) for i in range(2)]
        xnT = [sb("xnT%d" % i, [128, 8, 512], BF16) for i in range(2)]
        QT = sb("QT", [128, 4, SEQ], BF16)
        KT = sb("KT", [128, 4, SEQ], BF16)
        Va = sb("Va", [128, NT, 8, 65], BF16)
        bA = sb("bA", [128, 8, 256], F32)
        mA = sb("mA", [128, 256], F32)
        Sb = [sb("Sb%d" % i, [128, 256], F32) for i in range(4)]
        PT = [sb("PT%d" % i, [128, 8, 256], BF16) for i in range(3)]
        Ost = [sb("Ost%d" % i, [128, 520], F32) for i in range(2)]
        st_ss = sb("st_ss", [128, 128], F32)
        st_ms = sb("st_ms", [128, 128], F32)
        st_rs = sb("st_rs", [128, 128], F32)

        S.op("sp", lambda h: h.dma_start(out=g0b[:], in_=k.d["norm_g"][0, 0, :].partition_broadcast(128)), writes=["g0b"], dma="g0b")
        S.op("sp", lambda h: h.dma_start(out=mA[:], in_=k.d["maskA"]), writes=["mA"], dma="mA")
        S.op("pool", lambda h: h.memset(Va[:, :, :, 64:65], 1.0), writes=["Va1"])

        x = k.d["x"]
        tcount = 0
        for g in groups:
            r = DIL[g]
            nb = NT // r
            xr = x.rearrange("(m r) d -> r m d", r=r)
            Ur = k.d["U%d" % g].rearrange("(m r) d -> r m d", r=r)
            wkeys = load_cast_w(k, Wg, "Wg", k.d["w_in_a"][0, :, g * 1536:(g + 1) * 1536], 8, 1536, "Wg")
            S.op("sp", lambda h, g=g: h.dma_start(out=bA[:], in_=k.d["biasA"][g].rearrange("h p q -> p h q")),
                 writes=["bA"], dma="bA")
            for hh in range(8):
                S.op("pool", lambda h, hh=hh: h.tensor_tensor(out=bA[:, hh, :], in0=bA[:, hh, :], in1=mA[:], op=ALU.add),
                     reads=["bA", "mA"], writes=["bA"])

            tbase = tcount
            tcount += NT

            def a_load(T):
                c, n = divmod(T, nb)
                sl = (tbase + T) % 5
                S.op("sp", lambda h, c=c, n=n, sl=sl, xr=xr: h.dma_start(out=xt[sl][:], in_=xr[c, n * 128:(n + 1) * 128, :]),
                     writes=[("xt", sl)], dma=("xt", sl))

            def a_sq(T):
                sl = (tbase + T) % 5
                col = (tbase + T) % 128
                S.op("act", lambda h, sl=sl, col=col: h.activation(out=junk[:], in_=xt[sl][:], func=AF.Square,
                                                                accum_out=st_ss[:, col:col + 1]),
                     reads=[("xt", sl)], writes=[("ss", col)])

            def a_rs(T):
                col = (tbase + T) % 128
                rstd_from_ss(k, st_ss[:, col:col + 1], st_ms[:, col:col + 1], st_rs[:, col:col + 1], k.nhalf[:, 0:1],
                             [("ss", col)], ("ms", col), ("rs", col), 1.0 / D, 1e-6)

            def a_xn(T):
                sl = (tbase + T) % 5
                col = (tbase + T) % 128
                xs = T % 2
                S.op("dve", lambda h, sl=sl, col=col, xs=xs: h.scalar_tensor_tensor(
                    out=xn[xs][:], in0=xt[sl][:], scalar=st_rs[:, col:col + 1], in1=g0b[:], op0=ALU.mult, op1=ALU.mult),
                    reads=[("xt", sl), ("rs", col), "g0b"], writes=[("xn", xs)])

            def a_tr(T):
                st, t4 = divmod(T, 4)
                xs = T % 2
                pb = banks[T % 2]
                for dc in range(8):
                    S.op("pe", lambda h, dc=dc, xs=xs, pb=pb: h.transpose(
                        out=bank_bf16(pb, 8)[:, dc, :], in_=xn[xs][:, dc * 128:(dc + 1) * 128], identity=k.identb[:]),
                        reads=[("xn", xs), "identb"], writes=[("bank", T % 2)])
                S.op("act", lambda h, st=st, t4=t4, pb=pb: h.copy(out=xnT[st % 2][:, :, t4 * 128:(t4 + 1) * 128], in_=bank_bf16(pb, 8)),
                     reads=[], writes=[("bank", T % 2), ("xnT", st % 2, t4)])

            pcount = [0]

            def a_proj(st, q):
                xk = [("xnT", st % 2, t4) for t4 in range(4)]
                for gi in range(3 * q, 3 * q + 3):
                    bi = 2 + pcount[0] % 3
                    pcount[0] += 1
                    if gi < 8:
                        fc = gi
                        for dc in range(8):
                            S.op("pe", lambda h, fc=fc, dc=dc, bi=bi: h.matmul(
                                banks[bi][:, :], lhsT=Wg[:, dc, fc * 128:(fc + 1) * 128], rhs=xnT[st % 2][:, dc, :],
                                start=(dc == 0), stop=(dc == 7)),
                                reads=xk + wkeys, writes=[("bank", bi)])
                        if fc < 4:
                            S.op("act", lambda h, fc=fc, bi=bi: h.activation(out=QT[:, fc, st * 512:(st + 1) * 512], in_=banks[bi][:, :],
                                                                       func=AF.Copy, scale=0.125),
                                 writes=[("bank", bi), ("QT", st)])
                        else:
                            S.op("dve", lambda h, fc=fc, bi=bi: h.tensor_copy(out=KT[:, fc - 4, st * 512:(st + 1) * 512], in_=banks[bi][:, :]),
                                 writes=[("bank", bi), ("KT", st)])
                    else:
                        t4 = gi - 8
                        T = st * 4 + t4
                        for dc in range(8):
                            S.op("pe", lambda h, t4=t4, dc=dc, bi=bi: h.matmul(
                                banks[bi][:, :], lhsT=xnT[st % 2][:, dc, t4 * 128:(t4 + 1) * 128], rhs=Wg[:, dc, 1024:1536],
                                start=(dc == 0), stop=(dc == 7)),
                                reads=[("xnT", st % 2, t4)] + wkeys, writes=[("bank", bi)])
                        if t4 % 2 == 0:
                            S.op("act", lambda h, T=T, bi=bi: h.copy(out=Va[:, T, :, 0:64], in_=banks[bi][:, :].rearrange("p (h e) -> p h e", h=8)),
                                 writes=[("bank", bi), ("Va", T)])
                        else:
                            S.op("dve", lambda h, T=T, bi=bi: h.tensor_copy(out=Va[:, T, :, 0:64], in_=banks[bi][:, :].rearrange("p (h e) -> p h e", h=8)),
                                 writes=[("bank", bi), ("Va", T)])

            for u in range(-4, NT + 4):
                if 0 <= u + 4 < NT:
                    a_load(u + 4)
                if 0 <= u + 3 < NT:
                    a_sq(u + 3)
                if 0 <= u + 2 < NT:
                    a_rs(u + 2)
                if 0 <= u + 1 < NT:
                    a_xn(u + 1)
                if 0 <= u < NT:
                    a_tr(u)
                if u >= 4:
                    a_proj((u - 4) // 4, (u - 4) % 4)

            scount = [0]

            def stageB_scores(T):
                c, j = divmod(T, nb)
                ncols = 256 if j < nb - 1 else 128
                st_list = sorted(set([(T * 128) // 512, (T * 128 + ncols - 1) // 512]))
                for hd in range(8):
                    hr = slice(64 * (hd % 2), 64 * (hd % 2) + 64)
                    hp = hd // 2
                    bi = 2 + scount[0] % 3
                    si = scount[0] % 4
                    scount[0] += 1
                    S.op("pe", lambda h, T=T, hr=hr, hp=hp, bi=bi, ncols=ncols: h.matmul(
                        banks[bi][:, 0:ncols], lhsT=KT[hr, hp, T * 128:(T + 1) * 128], rhs=QT[hr, hp, T * 128:T * 128 + ncols],
                        start=True, stop=True),
                        reads=[("KT", T // 4)] + [("QT", s_) for s_ in st_list], writes=[("bank", bi)])
                    S.op("dve", lambda h, hd=hd, bi=bi, si=si, ncols=ncols: h.tensor_tensor(
                        out=Sb[si][:, 0:ncols], in0=banks[bi][:, 0:ncols], in1=bA[:, hd, 0:ncols], op=ALU.add),
                        reads=["bA"], writes=[("bank", bi), ("Sb", si)])
                    S.op("act", lambda h, T=T, hd=hd, si=si, ncols=ncols: h.activation(
                        out=PT[T % 3][:, hd, 0:ncols], in_=Sb[si][:, 0:ncols], func=AF.Exp),
                        reads=[("Sb", si)], writes=[("PT", T % 3, hd)])

            def stageB_pv(T):
                c, j = divmod(T, nb)
                o0 = 0 if T % 2 == 0 else 1
                obanks = (0, 1) if o0 == 0 else (5, 6)
                for hd in range(8):
                    bi = obanks[hd // 4]
                    oap = banks[bi][:, :].rearrange("p (h e) -> p h e", h=4)[:, hd % 4, 0:65]
                    if j > 0:
                        S.op("pe", lambda h, T=T, hd=hd, oap=oap: h.matmul(
                            oap, lhsT=PT[(T - 1) % 3][:, hd, 128:256], rhs=Va[:, T - 1, hd, :], start=True, stop=False),
                            reads=[("PT", (T - 1) % 3, hd), ("Va", T - 1), "Va1"], writes=[("bank", bi)])
                    S.op("pe", lambda h, T=T, hd=hd, oap=oap, j=j: h.matmul(
                        oap, lhsT=PT[T % 3][:, hd, 0:128], rhs=Va[:, T, hd, :], start=(j == 0), stop=True),
                        reads=[("PT", T % 3, hd), ("Va", T), "Va1"], writes=[("bank", bi)])
                osl = T % 2
                for half in range(2):
                    bi = obanks[half]
                    S.op("act", lambda h, half=half, bi=bi, osl=osl: h.copy(
                        out=Ost[osl][:, half * 260:(half + 1) * 260].rearrange("p (h e) -> p h e", h=4),
                        in_=banks[bi][:, :].rearrange("p (h e) -> p h e", h=4)[:, :, 0:65]),
                        writes=[("bank", bi), ("Ost", osl, half)])
                S.op("sp", lambda h, c=c, j=j, osl=osl, Ur=Ur: h.dma_start(out=Ur[c, j * 128:(j + 1) * 128, :], in_=Ost[osl][:]),
                     reads=[("Ost", osl, 0), ("Ost", osl, 1)], dma=("Ost", osl))

            for T in range(NT):
                stageB_scores(T)
                if T >= 1:
                    stageB_pv(T - 1)
            stageB_pv(NT - 1)
        emit_block(k)


def postnorm_residual(k, pbanks, pkeys, gb, gkey, hres, hkey, tmp, tmpkey, st, col):
    S = k.S
    ss, ms, rs = st
    for half in range(2):
        S.op("act", lambda h, half=half: h.activation(out=k.junk[:, 0:512], in_=pbanks[half][:, :], func=AF.Square,
                                                      accum_out=ss[:, 2 * col + half:2 * col + half + 1]),
             writes=[pkeys[half], ("ss", 2 * col + half)])
    S.op("dve", lambda h: h.tensor_tensor(out=ms[:, 2 * col:2 * col + 1], in0=ss[:, 2 * col:2 * col + 1], in1=ss[:, 2 * col + 1:2 * col + 2], op=ALU.add),
         reads=[("ss", 2 * col), ("ss", 2 * col + 1)], writes=[("ms", 2 * col)])
    rstd_from_ss(k, ms[:, 2 * col:2 * col + 1], ms[:, 2 * col + 1:2 * col + 2], rs[:, col:col + 1], k.nhalf[:, 0:1],
                 [("ms", 2 * col)], ("ms", 2 * col + 1), ("rs", col), 1.0 / D, 1e-6)
    for half in range(2):
        S.op("dve", lambda h, half=half: h.scalar_tensor_tensor(
            out=tmp[:, half * 512:(half + 1) * 512], in0=pbanks[half][:, :], scalar=rs[:, col:col + 1],
            in1=gb[:, half * 512:(half + 1) * 512], op0=ALU.mult, op1=ALU.mult),
            reads=[("rs", col), gkey], writes=[pkeys[half], (tmpkey, half)])
    S.op("pool", lambda h: h.tensor_tensor(out=hres, in0=hres, in1=tmp[:, :], op=ALU.add),
         reads=[(tmpkey, 0), (tmpkey, 1)], writes=[hkey])


def phase_outproj(k, layer):
    nc, S = k.nc, k.S
    NHB = 7
    with ExitStack() as es:
        tg = k.newtag()
        sb = lambda n, s, d: es.enter_context(nc.sbuf_tensor(n + tg, s, d))
        banks = [es.enter_context(nc.psum_tensor("bk%d" % i + tg, [128, 512], F32)) for i in range(8)]
        nkc = 4 if layer == 0 else 8
        Wo = sb("Wo", [128, nkc, 1024], BF16)
        g1b = sb("g1b", [128, 1024], F32)
        hres = [sb("hres%d" % i, [128, 1024], F32) for i in range(NHB)]
        tmp = [sb("tmp%d" % i, [128, 1024], F32) for i in range(2)]
        oT = [sb("oT%d" % i, [128, nkc, 128], BF16) for i in range(2)]
        ss = sb("ss", [128, 128], F32)
        ms = sb("ms", [128, 128], F32)
        rs = sb("rs", [128, 64], F32)
        if layer == 0:
            Ul = [[sb("Ul%d_%d" % (i, g), [128, 520], F32) for g in range(3)] for i in range(3)]
            Us = [sb("Us%d" % i, [128, 520], F32) for i in range(2)]
            rden = [sb("rden%d" % i, [128, 8], F32) for i in range(2)]
            obf = [sb("obf%d" % i, [128, 512], BF16) for i in range(2)]
            wsrc = k.d["w_out_a"][0]
            hin = k.d["x"]
            hout = k.d["H1"]
        else:
            obf = [sb("obf%d" % i, [128, 1024], BF16) for i in range(3)]
            wsrc = k.d["w_out_b"][0]
            hin = k.d["H2"]
            hout = k.d["H1"]
        wkeys = load_cast_w(k, Wo, "Wo", wsrc, nkc, 1024, "Wo")
        S.op("sp", lambda h: h.dma_start(out=g1b[:], in_=k.d["norm_g"][layer, 1, :].partition_broadcast(128)), writes=["g1b"], dma="g1b")
        mbanks = ((2, 3), (4, 5), (6, 7))

        def s0(T):
            hs = T % NHB
            S.op("sp", lambda h: h.dma_start(out=hres[hs][:], in_=hin[T * 128:(T + 1) * 128, :]), writes=[("hres", hs)], dma=("hres", hs))
            if layer == 0:
                us = T % 3
                for g in range(3):
                    S.op("sp", lambda h, g=g: h.dma_start(out=Ul[us][g][:], in_=k.d["U%d" % g][T * 128:(T + 1) * 128, :]),
                         writes=[("Ul", us, g)], dma=("Ul", us, g))
            else:
                os_ = T % 3
                S.op("sp", lambda h: h.dma_start(out=obf[os_][:], in_=k.d["O1"][T * 128:(T + 1) * 128, :]), writes=[("obf", os_)], dma=("obf", os_))

        def s1(T):
            if layer != 0:
                return
            ul = T % 3
            us = T % 2
            S.op("dve", lambda h: h.tensor_tensor(out=Us[us][:], in0=Ul[ul][0][:], in1=Ul[ul][1][:], op=ALU.add),
                 reads=[("Ul", ul, 0), ("Ul", ul, 1)], writes=[("Us", us)])
            S.op("dve", lambda h: h.tensor_tensor(out=Us[us][:], in0=Us[us][:], in1=Ul[ul][2][:], op=ALU.add),
                 reads=[("Ul", ul, 2)], writes=[("Us", us)])
            usv = Us[us][:, :].rearrange("p (h e) -> p h e", h=8)
            S.op("dve", lambda h: h.reciprocal(out=rden[us][:, :], in_=usv[:, :, 64]), reads=[("Us", us)], writes=[("rden", us)])
            S.op("dve", lambda h: h.tensor_tensor(out=obf[us][:, :].rearrange("p (h e) -> p h e", h=8), in0=usv[:, :, 0:64],
                                                  in1=rden[us][:, :].unsqueeze(2).to_broadcast([128, 8, 64]), op=ALU.mult),
                 reads=[("Us", us), ("rden", us)], writes=[("obf", us)])

        def s2(T):
            if layer == 0:
                ok = ("obf", T % 2)
                osrc = obf[T % 2]
            else:
                ok = ("obf", T % 3)
                osrc = obf[T % 3]
            bi = T % 2
            for kc in range(nkc):
                S.op("pe", lambda h, kc=kc: h.transpose(out=bank_bf16(banks[bi], 8)[:, kc, :], in_=osrc[:, kc * 128:(kc + 1) * 128], identity=k.identb[:]),
                     reads=[ok, "identb"], writes=[("bank", bi)])
            S.op("act", lambda h: h.copy(out=oT[T % 2][:, :, :], in_=bank_bf16(banks[bi], 8)[:, 0:nkc, :]),
                 writes=[("bank", bi), ("oT", T % 2)])

        def s3(T):
            pb = mbanks[T % 3]
            col = T % 64
            for half in range(2):
                for kc in range(nkc):
                    S.op("pe", lambda h, half=half, kc=kc: h.matmul(
                        banks[pb[half]][:, :], lhsT=oT[T % 2][:, kc, :], rhs=Wo[:, kc, half * 512:(half + 1) * 512],
                        start=(kc == 0), stop=(kc == nkc - 1)),
                        reads=[("oT", T % 2)] + wkeys, writes=[("bank", pb[half])])
            for half in range(2):
                S.op("act", lambda h, half=half: h.activation(out=k.junk[:, 0:512], in_=banks[pb[half]][:, :], func=AF.Square,
                                                              accum_out=ss[:, 2 * col + half:2 * col + half + 1]),
                     writes=[("bank", pb[half]), ("ss", 2 * col + half)])

        def s4(T):
            col = T % 64
            S.op("dve", lambda h: h.tensor_tensor(out=ms[:, 2 * col:2 * col + 1], in0=ss[:, 2 * col:2 * col + 1], in1=ss[:, 2 * col + 1:2 * col + 2], op=ALU.add),
                 reads=[("ss", 2 * col), ("ss", 2 * col + 1)], writes=[("ms", 2 * col)])
            rstd_from_ss(k, ms[:, 2 * col:2 * col + 1], ms[:, 2 * col + 1:2 * col + 2], rs[:, col:col + 1], k.nhalf[:, 0:1],
                         [("ms", 2 * col)], ("ms", 2 * col + 1), ("rs", col), 1.0 / D, 1e-6)

        def s5(T):
            pb = mbanks[T % 3]
            col = T % 64
            hs = T % NHB
            tb = tmp[T % 2]
            for half in range(2):
                S.op("dve", lambda h, half=half: h.scalar_tensor_tensor(
                    out=tb[:, half * 512:(half + 1) * 512], in0=banks[pb[half]][:, :], scalar=rs[:, col:col + 1],
                    in1=g1b[:, half * 512:(half + 1) * 512], op0=ALU.mult, op1=ALU.mult),
                    reads=[("rs", col), "g1b"], writes=[("bank", pb[half]), ("tmp", T % 2, half)])
            S.op("pool", lambda h: h.tensor_tensor(out=hres[hs][:, :], in0=hres[hs][:, :], in1=tb[:, :], op=ALU.add),
                 reads=[("tmp", T % 2, 0), ("tmp", T % 2, 1)], writes=[("hres", hs)])
            S.op("sp", lambda h: h.dma_start(out=hout[T * 128:(T + 1) * 128, :], in_=hres[hs][:]), reads=[("hres", hs)], dma=("hout", hs))

        stages = (s0, s1, s2, s3, s4, s5)
        for i in range(NT + len(stages) - 1):
            for si, fn in enumerate(stages):
                T = i - si
                if 0 <= T < NT:
                    fn(T)
        emit_block(k)


def phase_ffn(k, layer, hin, hout):
    nc, S = k.nc, k.S
    NS = SEQ // 256
    NCH = 2 * NFC
    NCB = 6
    with ExitStack() as es:
        tg = k.newtag()
        sb = lambda n, s, d: es.enter_context(nc.sbuf_tensor(n + tg, s, d))
        banks = [es.enter_context(nc.psum_tensor("bk%d" % i + tg, [128, 512], F32)) for i in range(8)]
        Wup = sb("Wup", [128, 8, 2 * DFF], BF16)
        Wdn = sb("Wdn", [128, NFC, 1024], BF16)
        g2b = sb("g2b", [128, 1024], F32)
        g3b = sb("g3b", [128, 1024], F32)
        cp = sb("cp", [128, NCH, 4], F32)
        hres = [sb("hres%d" % i, [128, 2, 1024], F32) for i in range(3)]
        xn = [sb("xn%d" % i, [128, 1024], BF16) for i in range(2)]
        xnT = [sb("xnT%d" % i, [128, 8, 258], BF16) for i in range(2)]
        Cb = [sb("Cb%d" % i, [128, 256], F32) for i in range(NCB)]
        Gb = [sb("Gb%d" % i, [128, 256], F32) for i in range(3)]
        gT = sb("gT", [128, NFC, 256], BF16)
        tmp = sb("tmp", [128, 1024], F32)
        ss = sb("ss", [128, 128], F32)
        ms = sb("ms", [128, 128], F32)
        rs = sb("rs", [128, 64], F32)
        ss2 = sb("ss2", [128, 64], F32)
        ms2 = sb("ms2", [128, 64], F32)
        rs2 = sb("rs2", [128, 64], F32)

        wup_keys = load_cast_w(k, Wup, "Wup", k.d["w_up"][layer], 8, 2 * DFF, "Wup")
        wdn_keys = load_cast_w(k, Wdn, "Wdn", k.d["w_down"][layer], NFC, 1024, "Wdn")
        S.op("sp", lambda h: h.dma_start(out=g2b[:], in_=k.d["norm_g"][layer, 2, :].partition_broadcast(128)), writes=["g2b"], dma="g2b")
        S.op("sp", lambda h: h.dma_start(out=g3b[:], in_=k.d["norm_g"][layer, 3, :].partition_broadcast(128)), writes=["g3b"], dma="g3b")
        S.op("sp", lambda h: h.dma_start(out=cp[:], in_=k.d["convp"][layer]), writes=["cp"], dma="cp")
        S.op("pool", lambda h: h.memset(xnT[0][:, :, 0:2], 0.0), writes=[("xnT", 0, "h")])

        def load(s):
            hs = s % 3
            for tt in range(2):
                T = s * 2 + tt
                S.op("sp", lambda h, tt=tt, T=T: h.dma_start(out=hres[hs][:, tt, :], in_=hin[T * 128:(T + 1) * 128, :]),
                     writes=[("hres", hs, tt)], dma=("hres", hs, tt))

        def prenorm_a(s):
            hs = s % 3
            for tt in range(2):
                T = s * 2 + tt
                col = T % 64
                S.op("act", lambda h, tt=tt, col=col: h.activation(out=k.junk[:], in_=hres[hs][:, tt, :], func=AF.Square,
                                                                   accum_out=ss2[:, col:col + 1]),
                     reads=[("hres", hs, tt)], writes=[("ss2", col)])
                rstd_from_ss(k, ss2[:, col:col + 1], ms2[:, col:col + 1], rs2[:, col:col + 1], k.nhalf[:, 0:1],
                             [("ss2", col)], ("ms2", col), ("rs2", col), 1.0 / D, 1e-6)
                S.op("dve", lambda h, tt=tt, col=col: h.scalar_tensor_tensor(
                    out=xn[tt][:], in0=hres[hs][:, tt, :], scalar=rs2[:, col:col + 1], in1=g2b[:], op0=ALU.mult, op1=ALU.mult),
                    reads=[("hres", hs, tt), ("rs2", col), "g2b"], writes=[("xn", tt)])

        def prenorm_b(s):
            xb = s % 2
            if s > 0:
                S.op("pool", lambda h: h.tensor_copy(out=xnT[xb][:, :, 0:2], in_=xnT[1 - xb][:, :, 256:258]),
                     reads=[("xnT", 1 - xb, 1)], writes=[("xnT", xb, "h")])
            for tt in range(2):
                for dc in range(8):
                    S.op("pe", lambda h, tt=tt, dc=dc: h.transpose(out=bank_bf16(banks[0], 8)[:, dc, :], in_=xn[tt][:, dc * 128:(dc + 1) * 128],
                                                                  identity=k.identb[:]),
                         reads=[("xn", tt), "identb"], writes=[("bank", 0)])
                S.op("act", lambda h, tt=tt: h.copy(out=xnT[xb][:, :, 2 + tt * 128:2 + (tt + 1) * 128], in_=bank_bf16(banks[0], 8)),
                     writes=[("bank", 0), ("xnT", xb, tt)])

        def stA(s, n):
            i, which = divmod(n, 2)
            fc = i + which * NFC
            bi = 1 + n % 3
            xb = s % 2
            for dc in range(8):
                S.op("pe", lambda h, fc=fc, dc=dc, bi=bi: h.matmul(
                    banks[bi][:, 0:258], lhsT=Wup[:, dc, fc * 128:(fc + 1) * 128], rhs=xnT[xb][:, dc, :], start=(dc == 0), stop=(dc == 7)),
                    reads=[("xnT", xb, 0), ("xnT", xb, 1), ("xnT", xb, "h")] + wup_keys, writes=[("bank", bi)])

        def stB(s, n):
            i, which = divmod(n, 2)
            fc = i + which * NFC
            bi = 1 + n % 3
            ci = n % NCB
            S.op("act", lambda h: h.activation(out=Cb[ci][:, :], in_=banks[bi][:, 2:258], func=AF.Identity,
                                               scale=cp[:, fc, 2:3], bias=cp[:, fc, 3:4]),
                 reads=["cp"], writes=[("bank", bi), ("Cb", ci)])

        def stC(s, n):
            i, which = divmod(n, 2)
            fc = i + which * NFC
            bi = 1 + n % 3
            ci = n % NCB
            S.op("dve", lambda h: h.scalar_tensor_tensor(
                out=Cb[ci][:, :], in0=banks[bi][:, 1:257], scalar=cp[:, fc, 1:2], in1=Cb[ci][:, :], op0=ALU.mult, op1=ALU.add),
                reads=["cp"], writes=[("bank", bi), ("Cb", ci)])
            S.op("dve", lambda h: h.scalar_tensor_tensor(
                out=Cb[ci][:, :], in0=banks[bi][:, 0:256], scalar=cp[:, fc, 0:1], in1=Cb[ci][:, :], op0=ALU.mult, op1=ALU.add),
                reads=["cp"], writes=[("bank", bi), ("Cb", ci)])

        def stD(s, n):
            i, which = divmod(n, 2)
            ci = n % NCB
            gi = i % 3
            if which == 0:
                S.op("act", lambda h: h.activation(out=Gb[gi][:, :], in_=Cb[ci][:, :], func=AF.Gelu),
                     reads=[("Cb", ci)], writes=[("Gb", gi)])
            else:
                S.op("pool", lambda h: h.tensor_tensor(out=gT[:, i, :], in0=Gb[gi][:, :], in1=Cb[ci][:, :], op=ALU.mult),
                     reads=[("Gb", gi), ("Cb", ci)], writes=[("gT", i)])

        def down_mm(s, tt):
            if True:
                pb = (4, 5) if tt == 0 else (6, 7)
                for half in range(2):
                    for i in range(NFC):
                        S.op("pe", lambda h, tt=tt, half=half, i=i, pb=pb: h.matmul(
                            banks[pb[half]][:, :], lhsT=gT[:, i, tt * 128:(tt + 1) * 128], rhs=Wdn[:, i, half * 512:(half + 1) * 512],
                            start=(i == 0), stop=(i == NFC - 1)),
                            reads=[("gT", i)] + wdn_keys, writes=[("bank", pb[half])])

        def down_post(s, tt):
            hs = s % 3
            if True:
                T = s * 2 + tt
                pb = (4, 5) if tt == 0 else (6, 7)
                postnorm_residual(k, [banks[pb[0]], banks[pb[1]]], [("bank", pb[0]), ("bank", pb[1])], g3b, "g3b",
                                  hres[hs][:, tt, :], ("hres", hs, tt), tmp, "tmp", (ss, ms, rs), T % 64)
                S.op("sp", lambda h, tt=tt, T=T: h.dma_start(out=hout[T * 128:(T + 1) * 128, :], in_=hres[hs][:, tt, :]),
                     reads=[("hres", hs, tt)], dma=("hout", hs, tt))

        load(0)
        load(1)
        prenorm_a(0)
        prenorm_b(0)
        for s in range(NS):
            if s + 2 < NS:
                load(s + 2)
            for t in range(NCH + 3):
                if t < NCH:
                    stA(s, t)
                if 0 <= t - 1 < NCH:
                    stB(s, t - 1)
                if 0 <= t - 2 < NCH:
                    stC(s, t - 2)
                if 0 <= t - 3 < NCH:
                    stD(s, t - 3)
            if s + 1 < NS:
                prenorm_a(s + 1)
            down_mm(s, 0)
            if s + 1 < NS:
                prenorm_b(s + 1)
            down_post(s, 0)
            down_mm(s, 1)
            down_post(s, 1)
        emit_block(k)


def phase3(k):
    nc, S = k.nc, k.S
    with ExitStack() as es:
        tg = k.newtag()
        sb = lambda n, s, d: es.enter_context(nc.sbuf_tensor(n + tg, s, d))
        banks = [es.enter_context(nc.psum_tensor("bk%d" % i + tg, [128, 512], F32)) for i in range(8)]
        Wq = sb("Wq", [128, 8, 1024], BF16)
        Wk = sb("Wk", [128, 8, 1024], BF16)
        Wv = sb("Wv", [128, 8, 1024], BF16)
        gqb = sb("gqb", [128, 1024], F32)
        gkb = sb("gkb", [128, 1024], F32)
        ht = [sb("ht%d" % i, [128, 1024], F32) for i in range(5)]
        xq = [sb("xq%d" % i, [128, 1024], BF16) for i in range(2)]
        xk = [sb("xk%d" % i, [128, 1024], BF16) for i in range(2)]
        xqT = [sb("xqT%d" % i, [128, 8, 512], BF16) for i in range(2)]
        xkT = [sb("xkT%d" % i, [128, 8, 512], BF16) for i in range(2)]
        Qs = [sb("Qs%d" % i, [128, 512], BF16) for i in range(4)]
        Vs = [sb("Vs%d" % i, [128, 1024], BF16) for i in range(2)]
        ss = sb("ss", [128, 128], F32)
        ms = sb("ms", [128, 128], F32)
        rs = sb("rs", [128, 128], F32)
        wq_keys = load_cast_w(k, Wq, "Wq", k.d["w_q_b"][0], 8, 1024, "Wq")
        wk_keys = load_cast_w(k, Wk, "Wk", k.d["w_k_shared"], 8, 1024, "Wk")
        wv_keys = load_cast_w(k, Wv, "Wv", k.d["w_v_shared"], 8, 1024, "Wv")
        S.op("sp", lambda h: h.dma_start(out=gqb[:], in_=k.d["norm_g"][1, 0, :].partition_broadcast(128)), writes=["gqb"], dma="gqb")
        S.op("sp", lambda h: h.dma_start(out=gkb[:], in_=k.d["kv_norm_g"].partition_broadcast(128)), writes=["gkb"], dma="gkb")
        hin = k.d["H2"]

        def a_load(T):
            sl = T % 5
            S.op("sp", lambda h, T=T, sl=sl: h.dma_start(out=ht[sl][:], in_=hin[T * 128:(T + 1) * 128, :]), writes=[("ht", sl)], dma=("ht", sl))

        def a_sq(T):
            sl = T % 5
            col = T % 128
            S.op("act", lambda h, sl=sl, col=col: h.activation(out=k.junk[:], in_=ht[sl][:], func=AF.Square, accum_out=ss[:, col:col + 1]),
                 reads=[("ht", sl)], writes=[("ss", col)])

        def a_rs(T):
            col = T % 128
            rstd_from_ss(k, ss[:, col:col + 1], ms[:, col:col + 1], rs[:, col:col + 1], k.nhalf[:, 0:1],
                         [("ss", col)], ("ms", col), ("rs", col), 1.0 / D, 1e-6)

        def a_xn(T):
            sl = T % 5
            col = T % 128
            xs = T % 2
            for (xb, gb, gk, nm) in ((xq, gqb, "gqb", "xq"), (xk, gkb, "gkb", "xk")):
                S.op("dve", lambda h, sl=sl, col=col, xb=xb, gb=gb, xs=xs: h.scalar_tensor_tensor(
                    out=xb[xs][:], in0=ht[sl][:], scalar=rs[:, col:col + 1], in1=gb[:], op0=ALU.mult, op1=ALU.mult),
                    reads=[("ht", sl), ("rs", col), gk], writes=[(nm, xs)])

        def a_tr(T):
            st, t4 = divmod(T, 4)
            xs = T % 2
            for (xb, xT, nm, bi) in ((xq, xqT, "xq", 0), (xk, xkT, "xk", 1)):
                for dc in range(8):
                    S.op("pe", lambda h, dc=dc, xb=xb, bi=bi, xs=xs: h.transpose(out=bank_bf16(banks[bi], 8)[:, dc, :], in_=xb[xs][:, dc * 128:(dc + 1) * 128],
                                                                               identity=k.identb[:]),
                         reads=[(nm, xs), "identb"], writes=[("bank", bi)])
                S.op("act", lambda h, t4=t4, xT=xT, bi=bi, st=st: h.copy(out=xT[st % 2][:, :, t4 * 128:(t4 + 1) * 128], in_=bank_bf16(banks[bi], 8)),
                     writes=[("bank", bi), (nm + "T", st % 2, t4)])

        pc = [0]

        def a_proj(st, q):
            for gi in range(6 * q, 6 * q + 6):
                bi = 2 + pc[0] % 4
                qi = pc[0] % 4
                pc[0] += 1
                if gi < 16:
                    if gi < 8:
                        W, wkeys, xT, nm, dst, scale, hd = Wq, wq_keys, xqT, "xq", k.d["QT1"], 0.125, gi
                    else:
                        W, wkeys, xT, nm, dst, scale, hd = Wk, wk_keys, xkT, "xk", k.d["KT1"], None, gi - 8
                    xkeys = [(nm + "T", st % 2, t4) for t4 in range(4)]
                    for dc in range(8):
                        S.op("pe", lambda h, hd=hd, dc=dc, bi=bi, W=W, xT=xT: h.matmul(
                            banks[bi][:, :], lhsT=W[:, dc, hd * 128:(hd + 1) * 128], rhs=xT[st % 2][:, dc, :], start=(dc == 0), stop=(dc == 7)),
                            reads=xkeys + wkeys, writes=[("bank", bi)])
                    if scale is not None:
                        S.op("act", lambda h, bi=bi, qi=qi: h.activation(out=Qs[qi][:, :], in_=banks[bi][:, :], func=AF.Copy, scale=0.125),
                             writes=[("bank", bi), ("Qs", qi)])
                    else:
                        S.op("dve", lambda h, bi=bi, qi=qi: h.tensor_copy(out=Qs[qi][:, :], in_=banks[bi][:, :]),
                             writes=[("bank", bi), ("Qs", qi)])
                    S.op("sp", lambda h, hd=hd, qi=qi, dst=dst: h.dma_start(out=dst[hd, :, st * 512:(st + 1) * 512], in_=Qs[qi][:, :]),
                         reads=[("Qs", qi)], dma=("Qs", qi))
                else:
                    t4, half = divmod(gi - 16, 2)
                    T = st * 4 + t4
                    vs = T % 2
                    for dc in range(8):
                        S.op("pe", lambda h, t4=t4, half=half, dc=dc, bi=bi: h.matmul(
                            banks[bi][:, :], lhsT=xkT[st % 2][:, dc, t4 * 128:(t4 + 1) * 128], rhs=Wv[:, dc, half * 512:(half + 1) * 512],
                            start=(dc == 0), stop=(dc == 7)),
                            reads=[("xkT", st % 2, t4)] + wv_keys, writes=[("bank", bi)])
                    if half == 0:
                        S.op("act", lambda h, bi=bi, vs=vs: h.copy(out=Vs[vs][:, 0:512], in_=banks[bi][:, :]), writes=[("bank", bi), ("Vs", vs, 0)])
                    else:
                        S.op("dve", lambda h, bi=bi, vs=vs: h.tensor_copy(out=Vs[vs][:, 512:1024], in_=banks[bi][:, :]), writes=[("bank", bi), ("Vs", vs, 1)])
                        S.op("sp", lambda h, T=T, vs=vs: h.dma_start(out=k.d["V1"][T * 128:(T + 1) * 128, :], in_=Vs[vs][:, :]),
                             reads=[("Vs", vs, 0), ("Vs", vs, 1)], dma=("Vs", vs))

        for u in range(-4, NT + 4):
            if 0 <= u + 4 < NT:
                a_load(u + 4)
            if 0 <= u + 3 < NT:
                a_sq(u + 3)
            if 0 <= u + 2 < NT:
                a_rs(u + 2)
            if 0 <= u + 1 < NT:
                a_xn(u + 1)
            if 0 <= u < NT:
                a_tr(u)
            if u >= 4:
                a_proj((u - 4) // 4, (u - 4) % 4)
        emit_block(k)


def phase4(k, heads=range(8)):
    nc, S = k.nc, k.S
    NSB = 4
    SK = 3
    with ExitStack() as es:
        tg = k.newtag()
        sb = lambda n, s, d: es.enter_context(nc.sbuf_tensor(n + tg, s, d))
        banks = [es.enter_context(nc.psum_tensor("bk%d" % i + tg, [128, 512], F32)) for i in range(8)]
        KTh = [sb("KTh%d" % i, [128, SEQ], BF16) for i in range(2)]
        QZ = [[sb("QZ%d_%d" % (i, c), [128, SEQ], BF16) for c in range(2)] for i in range(2)]
        Vh = [sb("Vh%d" % i, [128, NT, 129], BF16) for i in range(2)]
        Tb = [sb("Tb%d" % i, [128, TBW], F32) for i in range(2)]
        mB = sb("mB", [128, 128], F32)
        Sb = [sb("Sb%d" % i, [128, 512], F32) for i in range(NSB)]
        PT = [sb("PT%d" % i, [128, 512], BF16) for i in range(NSB)]
        Ost = [[sb("Ost%d_%d" % (i, c), [128, 4, 129], F32) for c in range(2)] for i in range(2)]
        Oh = [sb("Oh%d" % i, [128, NT, 128], BF16) for i in range(2)]
        lv = [sb("lv%d" % i, [128, 64], F32) for i in range(4)]
        lsum = sb("lsum", [128, 4], F32)
        lam = sb("lam", [128, 1], F32)
        sgb = sb("sgb", [128, 128], F32)
        rd = sb("rd", [128, 64, 8], F32)
        av_all = [sb("av%d" % i, [128, 128], F32) for i in range(8)]
        tv_all = [sb("tv%d" % i, [128, 128], F32) for i in range(8)]
        st1 = sb("st1", [128, 128], F32)
        st2 = sb("st2", [128, 128], F32)
        st3 = sb("st3", [128, 128], F32)

        for i, nm in enumerate(("lam_q1", "lam_k1", "lam_q2", "lam_k2")):
            S.op("sp", lambda h, i=i, nm=nm: h.dma_start(out=lv[i][:], in_=k.d[nm][0, :].partition_broadcast(128)), writes=[("lv", i)], dma=("lv", i))
        S.op("sp", lambda h: h.dma_start(out=sgb[:], in_=k.d["subln_g"][0, :].partition_broadcast(128)), writes=["sgb"], dma="sgb")
        S.op("sp", lambda h: h.dma_start(out=mB[:], in_=k.d["maskB"]), writes=["mB"], dma="mB")
        S.op("dve", lambda h: h.tensor_scalar(out=sgb[:], in0=sgb[:], scalar1=1.0 - LAMBDA_INIT, scalar2=None, op0=ALU.mult),
             reads=["sgb"], writes=["sgb"])
        for i in range(2):
            S.op("dve", lambda h, i=i: h.scalar_tensor_tensor(out=k.junk[:, 0:64], in0=lv[2 * i][:], scalar=1.0, in1=lv[2 * i + 1][:],
                                                            op0=ALU.mult, op1=ALU.mult, accum_out=lsum[:, i:i + 1]),
                 reads=[("lv", 2 * i), ("lv", 2 * i + 1)], writes=[("lsum", i)])
            S.op("act", lambda h, i=i: h.activation(out=lsum[:, 2 + i:3 + i], in_=lsum[:, i:i + 1], func=AF.Exp),
                 reads=[("lsum", i)], writes=[("lsum", 2 + i)])
        S.op("dve", lambda h: h.tensor_tensor(out=lam[:], in0=lsum[:, 2:3], in1=lsum[:, 3:4], op=ALU.subtract),
             reads=[("lsum", 2), ("lsum", 3)], writes=["lam"])
        S.op("dve", lambda h: h.tensor_scalar(out=lam[:], in0=lam[:], scalar1=LAMBDA_INIT, scalar2=None, op0=ALU.add),
             reads=["lam"], writes=["lam"])
        for i in range(2):
            S.op("pool", lambda h, i=i: h.memset(Vh[i][:, :, 128:129], 1.0), writes=[("Vh1", i)])
            S.op("pool", lambda h, i=i: h.memset(QZ[i][0][64:128, :], 0.0), writes=[("QZz", i, 0)])
            S.op("pool", lambda h, i=i: h.memset(QZ[i][1][0:64, :], 0.0), writes=[("QZz", i, 1)])

        def load_head(hi, hd, t_now=None):
            b = hi % 2
            S.op("sp", lambda h: h.dma_start(out=Tb[b][:], in_=k.d["biasB"][hd]), writes=[("Tb", b)], dma=("Tb", b))

            def mask_add():
                S.op("dve", lambda h: h.tensor_tensor(out=Tb[b][:, 0:128], in0=Tb[b][:, 0:128], in1=mB[:], op=ALU.add),
                     reads=["mB", ("Tb", b)], writes=[("Tb", b)])
            if t_now is None:
                mask_add()
            else:
                defer(t_now + 80, mask_add)
            S.op("sp", lambda h: h.dma_start(out=KTh[b][:], in_=k.d["KT1"][hd]), writes=[("KTh", b)], dma=("KTh", b))
            S.op("sp", lambda h: h.dma_start(out=QZ[b][0][0:64, :], in_=k.d["QT1"][hd, 0:64, :]), writes=[("QZ", b, 0)], dma=("QZ", b, 0))
            S.op("sp", lambda h: h.dma_start(out=QZ[b][1][64:128, :], in_=k.d["QT1"][hd, 64:128, :]), writes=[("QZ", b, 1)], dma=("QZ", b, 1))
            S.op("sp", lambda h: h.dma_start(out=Vh[b][:, :, 0:128], in_=k.d["V1"][:, hd * 128:(hd + 1) * 128].rearrange("(t p) e -> p t e", p=128)),
                 writes=[("Vh", b)], dma=("Vh", b))

        obank = (4, 5, 6, 7)
        heads = list(heads)
        items = []
        for hi, hd in enumerate(heads):
            for qs in range(8):
                for c in range(2):
                    for j in range(4 * qs + 4):
                        items.append((hi, hd, qs, c, j))
        N = len(items)
        deferred = {}

        def defer(t, fn):
            deferred.setdefault(t, []).append(fn)

        def geom(it):
            hi, hd, qs, c, j = it
            qt0 = max(0, j - 4 * qs)
            col0 = qt0 * 128
            ncols = 512 - col0
            delta = qs * 512 - j * 128 + col0
            off = min(delta, TB_CONST)
            assert 0 <= off and off + ncols <= TBW
            return qt0, col0, ncols, off

        def stA(n):
            hi, hd, qs, c, j = items[n]
            b = hi % 2
            qt0, col0, ncols, off = geom(items[n])
            bi = n % 4
            S.op("pe", lambda h: h.matmul(
                banks[bi][:, col0:512], lhsT=KTh[b][:, j * 128:(j + 1) * 128], rhs=QZ[b][c][:, qs * 512 + col0:(qs + 1) * 512],
                start=True, stop=True),
                reads=[("KTh", b), ("QZ", b, c), ("QZz", b, c)], writes=[("bank", bi)])

        def stB(n):
            hi, hd, qs, c, j = items[n]
            b = hi % 2
            qt0, col0, ncols, off = geom(items[n])
            bi = n % 4
            si = n % NSB
            S.op("dve", lambda h: h.tensor_tensor(
                out=Sb[si][:, col0:512], in0=banks[bi][:, col0:512], in1=Tb[b][:, off:off + ncols], op=ALU.add),
                reads=[("Tb", b)], writes=[("bank", bi), ("Sb", si)])

        def stC(n):
            qt0, col0, ncols, off = geom(items[n])
            si = n % NSB
            S.op("act", lambda h: h.activation(out=PT[si][:, col0:512], in_=Sb[si][:, col0:512], func=AF.Exp),
                 reads=[("Sb", si)], writes=[("PT", si)])

        def evac(hi, qs, c, qt):
            osl = qs % 2
            if qt % 2 == 0:
                S.op("act", lambda h: h.copy(out=Ost[osl][c][:, qt, :], in_=banks[obank[qt]][:, 0:129]),
                     writes=[("bank", obank[qt]), ("Ost", osl, c, qt)])
            else:
                S.op("dve", lambda h: h.tensor_copy(out=Ost[osl][c][:, qt, :], in_=banks[obank[qt]][:, 0:129]),
                     writes=[("bank", obank[qt]), ("Ost", osl, c, qt)])

        cc = [0]

        def combine_parts(hi, hd, qs):
            b = hi % 2
            osl = qs % 2
            ci = cc[0] % 64
            cc[0] += 1
            cb = (ci % 32) * 4
            av = av_all[(ci % 2) * 4:(ci % 2) * 4 + 4]
            tv = tv_all[(ci % 2) * 4:(ci % 2) * 4 + 4]
            ab = (ci % 2) * 4
            okeys = [("Ost", osl, c_, qt) for c_ in range(2) for qt in range(4)]

            def c1():
                S.op("dve", lambda h: h.reciprocal(out=rd[:, ci, 0:4], in_=Ost[osl][0][:, :, 128]), reads=okeys, writes=[("rd", ci, 0)])
                S.op("dve", lambda h: h.reciprocal(out=rd[:, ci, 4:8], in_=Ost[osl][1][:, :, 128]), reads=okeys, writes=[("rd", ci, 1)])
                S.op("dve", lambda h: h.tensor_scalar(out=rd[:, ci, 4:8], in0=rd[:, ci, 4:8], scalar1=lam[:, 0:1], scalar2=None, op0=ALU.mult),
                     reads=[("rd", ci, 1), "lam"], writes=[("rd", ci, 1)])

            def c2():
                for qt in range(4):
                    S.op("pool", lambda h, qt=qt: h.tensor_scalar(out=tv[qt][:], in0=Ost[osl][1][:, qt, 0:128], scalar1=rd[:, ci, 4 + qt:5 + qt],
                                                                  scalar2=0.0, op0=ALU.mult, op1=ALU.add),
                         reads=okeys + [("rd", ci, 1)], writes=[("tv", ab + qt)])
                    S.op("pool", lambda h, qt=qt: h.tensor_scalar(out=av[qt][:], in0=Ost[osl][0][:, qt, 0:128], scalar1=rd[:, ci, qt:qt + 1],
                                                                  scalar2=0.0, op0=ALU.mult, op1=ALU.add),
                         reads=okeys + [("rd", ci, 0)], writes=[("av", ab + qt)])
                    S.op("pool", lambda h, qt=qt: h.tensor_tensor(out=av[qt][:], in0=av[qt][:], in1=tv[qt][:], op=ALU.subtract),
                         reads=[("tv", ab + qt)], writes=[("av", ab + qt)])

            def c3():
                for qt in range(4):
                    S.op("dve", lambda h, qt=qt: h.scalar_tensor_tensor(
                        out=k.junk[:, 0:128], in0=av[qt][:], scalar=1.0, in1=av[qt][:], op0=ALU.mult, op1=ALU.mult,
                        accum_out=st1[:, cb + qt:cb + qt + 1]),
                        reads=[("av", ab + qt)], writes=[("st1", cb + qt)])
                S.op("dve", lambda h: h.tensor_scalar(out=st2[:, cb:cb + 4], in0=st1[:, cb:cb + 4], scalar1=1.0 / 128, scalar2=1e-5,
                                                      op0=ALU.mult, op1=ALU.add),
                     reads=[("st1", cb + qt) for qt in range(4)], writes=[("st2", cb)])

            def c4():
                S.op("pool", lambda h: h.tensor_tensor(out=st3[:, cb:cb + 4], in0=st2[:, cb:cb + 4], in1=k.nhalf[:, 0:4], op=ALU.pow),
                     reads=[("st2", cb)], writes=[("st3", cb)])
                for qt in range(4):
                    T = qs * 4 + qt
                    S.op("pool", lambda h, qt=qt: h.tensor_scalar(out=av[qt][:], in0=av[qt][:], scalar1=st3[:, cb + qt:cb + qt + 1],
                                                                  scalar2=0.0, op0=ALU.mult, op1=ALU.add),
                         reads=[("st3", cb)], writes=[("av", ab + qt)])
                    S.op("pool", lambda h, qt=qt, T=T: h.tensor_tensor(out=Oh[b][:, T, :], in0=av[qt][:], in1=sgb[:], op=ALU.mult),
                         reads=[("av", ab + qt), "sgb"], writes=[("Oh", b, T)])
                if qs == 7:
                    S.op("sp", lambda h: h.dma_start(out=k.d["O1"][:, hd * 128:(hd + 1) * 128].rearrange("(t p) e -> p t e", p=128), in_=Oh[b][:]),
                         reads=[("Oh", b, T) for T in range(NT)], dma=("Oh", b))

            return c1, c2, c3, c4

        def stD(n, t):
            hi, hd, qs, c, j = items[n]
            b = hi % 2
            qt0, col0, ncols, off = geom(items[n])
            si = n % NSB
            for qt in range(qt0, 4):
                last = (j == 4 * qs + qt)
                S.op("pe", lambda h, qt=qt, last=last: h.matmul(
                    banks[obank[qt]][:, 0:129], lhsT=PT[si][:, qt * 128:(qt + 1) * 128], rhs=Vh[b][:, j, :],
                    start=(j == 0), stop=last),
                    reads=[("PT", si), ("Vh", b), ("Vh1", b)], writes=[("bank", obank[qt])])
                if last:
                    defer(t + 1, lambda hi=hi, qs=qs, c=c, qt=qt: evac(hi, qs, c, qt))
            if c == 1 and j == 4 * qs + 3:
                parts = combine_parts(hi, hd, qs)
                for pi, dt_ in enumerate((4, 6, 12, 15)):
                    defer(t + dt_, parts[pi])

        load_head(0, heads[0])
        for t in range(N + SK + 20):
            for fn in deferred.pop(t, []):
                fn()
            if t < N:
                it = items[t]
                if it[2] == 0 and it[3] == 1 and it[4] == 0 and it[0] + 1 < len(heads):
                    load_head(it[0] + 1, heads[it[0] + 1], t)
                stA(t)
            if 0 <= t - 1 < N:
                stB(t - 1)
            if 0 <= t - 2 < N:
                stC(t - 2)
            if 0 <= t - SK < N:
                stD(t - SK, t)
        assert not deferred
        emit_block(k)


ALL_PHASES = ("p1", "p2a", "ffn0", "p3", "p4", "p5a", "ffn1")


def build(phases=ALL_PHASES, kinds=None):
    kinds = kinds or {}
    nc = bass.Bass("TRN2", target_bir_lowering=False)
    k = K()
    k.nc = nc
    d = {}

    def dt(name, shape, dtype, kind):
        d[name] = nc.dram_tensor(name, list(shape), dtype, kind=kinds.get(name, kind)).ap()

    EI = "ExternalInput"
    dt("x", [SEQ, D], F32, EI)
    dt("norm_g", [2, 4, D], F32, EI)
    dt("w_in_a", [1, D, 4608], F32, EI)
    dt("w_out_a", [1, 512, D], F32, EI)
    dt("kv_norm_g", [D], F32, EI)
    dt("w_k_shared", [D, D], F32, EI)
    dt("w_v_shared", [D, D], F32, EI)
    dt("w_q_b", [1, D, D], F32, EI)
    for nm in ("lam_q1", "lam_k1", "lam_q2", "lam_k2"):
        dt(nm, [1, 64], F32, EI)
    dt("subln_g", [1, 128], F32, EI)
    dt("w_out_b", [1, D, D], F32, EI)
    dt("w_up", [2, D, 2 * DFF], F32, EI)
    dt("w_down", [2, DFF, D], F32, EI)
    dt("convp", [2, 128, 2 * NFC, 4], F32, EI)
    dt("ident", [128, 128], F32, EI)
    dt("biasA", [3, NH, 128, 256], F32, EI)
    dt("maskA", [128, 256], F32, EI)
    dt("biasB", [NH, 128, TBW], F32, EI)
    dt("maskB", [128, 128], F32, EI)
    for g in range(3):
        dt("U%d" % g, [SEQ, 520], F32, "Internal")
    dt("H1", [SEQ, D], F32, "Internal")
    dt("H2", [SEQ, D], F32, "Internal")
    dt("QT1", [NH, 128, SEQ], BF16, "Internal")
    dt("KT1", [NH, 128, SEQ], BF16, "Internal")
    dt("V1", [SEQ, D], BF16, "Internal")
    dt("O1", [SEQ, D], BF16, "Internal")
    dt("out", [SEQ, D], F32, "ExternalOutput")
    k.d = d

    with ExitStack() as es:
        k.S = Sched(nc, es)
        S = k.S
        identf = es.enter_context(nc.sbuf_tensor("identf", [128, 128], F32))
        k.identb = es.enter_context(nc.sbuf_tensor("identb", [128, 128], BF16))
        k.nhalf = es.enter_context(nc.sbuf_tensor("nhalf", [128, 8], F32))
        k.junk = es.enter_context(nc.sbuf_tensor("junkg", [128, 1024], BF16))
        S.op("sp", lambda h: h.dma_start(out=identf[:], in_=d["ident"]), writes=["identf"], dma="identf")
        S.op("dve", lambda h: h.tensor_copy(out=k.identb[:], in_=identf[:]), reads=["identf"], writes=["identb"])
        S.op("dve", lambda h: h.memset(k.nhalf[:], -0.5), writes=["nhalf"])
        emit_block(k)
        for ph in phases:
            if ph == "p1":
                phase1(k)
            elif ph.startswith("p1g"):
                phase1(k, groups=(int(ph[3]),))
            elif ph == "p2a":
                phase_outproj(k, 0)
            elif ph == "ffn0":
                phase_ffn(k, 0, d["H1"], d["H2"])
            elif ph == "p3":
                phase3(k)
            elif ph == "p4":
                phase4(k)
            elif ph.startswith("p4h"):
                phase4(k, heads=[int(ph[3])])
            elif ph == "p5a":
                phase_outproj(k, 1)
            elif ph == "ffn1":
                phase_ffn(k, 1, d["H1"], d["out"])
    return nc


def _bucket(n):
    n = np.maximum(n, 0)
    nf = np.maximum(n, 1).astype(np.float32)
    large = 16 + (np.log(nf / np.float32(16)) / np.float32(math.log(2048 / 16)) * np.float32(16)).astype(np.int32)
    large = np.minimum(large, 31)
    return np.where(n < 16, n, large).astype(np.int64)


def host_consts(rel_bias_table, conv_w, conv_b):
    tab = np.asarray(rel_bias_table, dtype=np.float32)
    kl = np.arange(128)[:, None]
    qc = np.arange(256)[None, :]
    du = qc - kl
    biasA = np.stack([np.take(tab, _bucket(r * np.clip(du, 0, 128)), axis=1) for r in DIL]).astype(np.float32)
    maskA = np.where((du >= 0) & (du <= 128), np.float32(0.0), np.float32(NEG)).astype(np.float32)
    jj = np.arange(TBW)[None, :]
    pp = np.arange(128)[:, None]
    biasB = np.take(tab, _bucket(np.maximum(jj - pp, 0)), axis=1).astype(np.float32)
    maskB = np.where(jj[:, :128] - pp >= 0, np.float32(0.0), np.float32(NEG)).astype(np.float32)
    cw = np.asarray(conv_w, dtype=np.float32)
    cb = np.asarray(conv_b, dtype=np.float32)
    convp = np.concatenate([cw, cb[:, None, :]], axis=1)
    convp = np.ascontiguousarray(convp.reshape(2, 4, 2 * NFC, 128).transpose(0, 3, 2, 1))
    return dict(biasA=np.ascontiguousarray(biasA), maskA=maskA, biasB=np.ascontiguousarray(biasB), maskB=np.ascontiguousarray(maskB),
                convp=convp, ident=np.eye(128, dtype=np.float32))


_NC_CACHE = {}


def kernel(x, rel_bias_table, norm_g, w_in_a, w_out_a, kv_norm_g, w_k_shared, w_v_shared, w_q_b,
           lam_q1, lam_k1, lam_q2, lam_k2, subln_g, w_out_b, w_up, conv_w, conv_b, w_down):
    f = lambda a: np.ascontiguousarray(np.asarray(a, dtype=np.float32))
    hc = host_consts(rel_bias_table, conv_w, conv_b)
    shared = dict(norm_g=f(norm_g), w_in_a=f(w_in_a), w_out_a=f(w_out_a), kv_norm_g=f(kv_norm_g), w_k_shared=f(w_k_shared),
                  w_v_shared=f(w_v_shared), w_q_b=f(w_q_b), lam_q1=f(lam_q1), lam_k1=f(lam_k1), lam_q2=f(lam_q2), lam_k2=f(lam_k2),
                  subln_g=f(subln_g), w_out_b=f(w_out_b), w_up=f(w_up), w_down=f(w_down), **hc)
    x = f(x)
    if "nc" not in _NC_CACHE:
        _NC_CACHE["nc"] = build()
    nc = _NC_CACHE["nc"]
    in_maps = [dict(shared, x=x[b]) for b in range(8)]
    res = run_bass_kernel_spmd(nc, in_maps, core_ids=list(range(8)))
    return np.stack([np.asarray(res.results[b]["out"], dtype=np.float32) for b in range(8)], axis=0)
```

```python
import math
from contextlib import ExitStack

import numpy as np
import concourse.bass as bass
import concourse.mybir as mybir
from concourse.bass_utils import run_bass_kernel_spmd

F32 = mybir.dt.float32
BF16 = mybir.dt.bfloat16
AF = mybir.ActivationFunctionType
ALU = mybir.AluOpType

SEQ = 4096
D = 1024
NT = SEQ // 128
NH = 8
DIL = (1, 4, 16)
DFF = 2816
NFC = DFF // 128
NEG = -30000.0
LAMBDA_INIT = 0.8 - 0.6 * math.exp(-0.3 * 1)
TBW = 2176
TB_CONST = 1664

ENGS = ("pe", "act", "dve", "pool", "sp")


class _Op:
    __slots__ = ("eng", "fn", "deps", "idx", "dma", "sig", "sigval", "dsem", "dval")

    def __init__(self, eng, fn, dma):
        self.eng = eng
        self.fn = fn
        self.deps = []
        self.dma = dma
        self.sig = False
        self.sigval = 0
        self.dsem = None
        self.dval = 0


class Sched:
    def __init__(self, nc, es):
        self.nc = nc
        self.es = es
        self.sems = {e: es.enter_context(nc.semaphore("s_" + e)) for e in ENGS}
        self.q = {e: [] for e in ENGS}
        self.lastw = {}
        self.readers = {}
        self.sigcount = {e: 0 for e in ENGS}
        self.dma_slot = {}
        self.free_slots = {}
        self.nsem = 0

    def _slot(self, key, eng):
        if key not in self.dma_slot:
            fs = self.free_slots.setdefault(eng, [])
            if fs:
                self.dma_slot[key] = fs.pop()
            else:
                self.nsem += 1
                self.dma_slot[key] = [self.es.enter_context(self.nc.semaphore("d%d" % self.nsem)), 0, eng]
        assert self.dma_slot[key][2] == eng
        return self.dma_slot[key]

    def op(self, eng, fn, reads=(), writes=(), dma=None):
        o = _Op(eng, fn, dma is not None)
        o.idx = len(self.q[eng])
        if dma is not None:
            s = self._slot(dma, eng)
            s[1] += 16
            o.dsem, o.dval = s[0], s[1]
        dep = {}

        def add(d):
            if d is None or d is o:
                return
            k = ("d", id(d.dsem)) if d.dma else ("e", d.eng)
            cur = dep.get(k)
            if cur is None or (cur.dval < d.dval if d.dma else cur.idx < d.idx):
                dep[k] = d

        for b in reads:
            add(self.lastw.get(b))
        for b in writes:
            add(self.lastw.get(b))
            for r in self.readers.get(b, {}).values():
                add(r)
        o.deps = list(dep.values())
        me = ("d", id(o.dsem)) if o.dma else ("e", eng)
        for b in reads:
            self.readers.setdefault(b, {})[me] = o
        for b in writes:
            self.lastw[b] = o
            self.readers[b] = {}
        self.q[eng].append(o)
        return o

    def finalize(self):
        for e in ENGS:
            for o in self.q[e]:
                for d in o.deps:
                    if d.dma:
                        continue
                    if d.eng == "pe" and o.eng == "pe":
                        continue
                    d.sig = True
        for e in ENGS:
            c = self.sigcount[e]
            for o in self.q[e]:
                if o.sig and not o.dma:
                    c += 1
                    o.sigval = c
            self.sigcount[e] = c

    def emit_engine(self, e, h):
        known = {}
        for o in self.q[e]:
            need = {}
            for d in o.deps:
                if d.dma:
                    k = ("d", id(d.dsem))
                    if need.get(k, (None, 0))[1] < d.dval:
                        need[k] = (d.dsem, d.dval)
                else:
                    if d.eng == "pe" and e == "pe":
                        continue
                    k = ("e", d.eng)
                    if need.get(k, (None, 0))[1] < d.sigval:
                        need[k] = (self.sems[d.eng], d.sigval)
            for k, (sem, val) in need.items():
                if known.get(k, 0) >= val:
                    continue
                h.wait_ge(sem, val)
                known[k] = val
            ins = o.fn(h)
            if o.dma:
                ins.then_inc(o.dsem, 16)
            elif o.sig:
                ins.then_inc(self.sems[e], 1)

    def final_dma_waits(self, h):
        for key, sl in self.dma_slot.items():
            if sl[1]:
                h.wait_ge(sl[0], sl[1])

    def reset_phase(self):
        for sl in self.dma_slot.values():
            self.free_slots.setdefault(sl[2], []).append(sl)
        self.dma_slot = {}
        self.q = {e: [] for e in ENGS}
        self.lastw = {}
        self.readers = {}


class K:
    ntag = 0

    def newtag(self):
        self.ntag += 1
        return "_p%d" % self.ntag


def emit_block(k):
    S, nc = k.S, k.nc
    S.finalize()
    with nc.Block() as block:
        @block.sync
        def _(h):
            S.emit_engine("sp", h)
            S.final_dma_waits(h)

        @block.tensor
        def _(h):
            S.emit_engine("pe", h)

        @block.scalar
        def _(h):
            S.emit_engine("act", h)

        @block.vector
        def _(h):
            S.emit_engine("dve", h)

        @block.gpsimd
        def _(h):
            S.emit_engine("pool", h)
    S.reset_phase()


def bank_bf16(bank, c):
    return bank[:, :].bitcast(BF16).rearrange("p (c t) -> p c t", c=c)


def load_cast_w(k, dst, dst_key, w2d, nchunk, ncols, slot):
    S = k.S
    wv = w2d.rearrange("(c p) n -> p c n", p=128)
    keys = []
    i = 0
    for c in range(nchunk):
        for c0 in range(0, ncols, 2048):
            c1 = min(ncols, c0 + 2048)
            key = (dst_key, c, c0)
            keys.append(key)
            S.op("pool", lambda h, c=c, c0=c0, c1=c1: h.dma_start(out=dst[:, c, c0:c1], in_=wv[:, c, c0:c1]),
                 writes=[key], dma=(slot, i % 4))
            i += 1
    return keys


def rstd_from_ss(k, ss_ap, ms_ap, rstd_ap, nh_ap, keys_ss, key_ms, key_rstd, inv_n, eps):
    S = k.S
    S.op("dve", lambda h: h.tensor_scalar(out=ms_ap, in0=ss_ap, scalar1=inv_n, scalar2=eps, op0=ALU.mult, op1=ALU.add),
         reads=keys_ss, writes=[key_ms])
    S.op("pool", lambda h: h.tensor_tensor(out=rstd_ap, in0=ms_ap, in1=nh_ap, op=ALU.pow),
         reads=[key_ms], writes=[key_rstd])


def phase1(k, groups=(0, 1, 2)):
    nc, S = k.nc, k.S
    with ExitStack() as es:
        tg = k.newtag()
        sb = lambda n, s, d: es.enter_context(nc.sbuf_tensor(n + tg, s, d))
        banks = [es.enter_context(nc.psum_tensor("bk%d" % i + tg, [128, 512], F32)) for i in range(8)]
        g0b = sb("g0b", [128, 1024], F32)
        Wg = sb("Wg", [128, 8, 1536], BF16)
        xt = [sb("xt%d" % i, [128, 1024], F32) for i in range(5)]
        junk = k.junk
        xn = [sb("xn%d" % i, [128, 1024], BF16) for i in range(2)]
        xnT = [sb("xnT%d" % i, [128, 8, 512], BF16) for i in range(2)]
        QT = sb("QT", [128, 4, SEQ], BF16)
        KT = sb("KT", [128, 4, SEQ], BF16)
        Va = sb("Va", [128, NT, 8, 65], BF16)
        bA = sb("bA", [128, 8, 256], F32)
        mA = sb("mA", [128, 256], F32)
        Sb = [sb("Sb%d" % i, [128, 256], F32) for i in range(4)]
        PT = [sb("PT%d" % i, [128, 8, 256], BF16) for i in range(3)]
        Ost = [sb("Ost%d" % i, [128, 520], F32) for i in range(2)]
        st_ss = sb("st_ss", [128, 128], F32)
        st_ms = sb("st_ms", [128, 128], F32)
        st_rs = sb("st_rs", [128, 128], F32)

        S.op("sp", lambda h: h.dma_start(out=g0b[:], in_=k.d["norm_g"][0, 0, :].partition_broadcast(128)), writes=["g0b"], dma="g0b")
        S.op("sp", lambda h: h.dma_start(out=mA[:], in_=k.d["maskA"]), writes=["mA"], dma="mA")
        S.op("pool", lambda h: h.memset(Va[:, :, :, 64:65], 1.0), writes=["Va1"])

        x = k.d["x"]
        tcount = 0
        for g in groups:
            r = DIL[g]
            nb = NT // r
            xr = x.rearrange("(m r) d -> r m d", r=r)
            Ur = k.d["U%d" % g].rearrange("(m r) d -> r m d", r=r)
            wkeys = load_cast_w(k, Wg, "Wg", k.d["w_in_a"][0, :, g * 1536:(g + 1) * 1536], 8, 1536, "Wg")
            S.op("sp", lambda h, g=g: h.dma_start(out=bA[:], in_=k.d["biasA"][g].rearrange("h p q -> p h q")),
                 writes=["bA"], dma="bA")
            for hh in range(8):
                S.op("pool", lambda h, hh=hh: h.tensor_tensor(out=bA[:, hh, :], in0=bA[:, hh, :], in1=mA[:], op=ALU.add),
                     reads=["bA", "mA"], writes=["bA"])

            tbase = tcount
            tcount += NT

            def a_load(T):
                c, n = divmod(T, nb)
                sl = (tbase + T) % 5
                S.op("sp", lambda h, c=c, n=n, sl=sl, xr=xr: h.dma_start(out=xt[sl][:], in_=xr[c, n * 128:(n + 1) * 128, :]),
                     writes=[("xt", sl)], dma=("xt", sl))

            def a_sq(T):
                sl = (tbase + T) % 5
                col = (tbase + T) % 128
                S.op("act", lambda h, sl=sl, col=col: h.activation(out=junk[:], in_=xt[sl][:], func=AF.Square,
                                                                accum_out=st_ss[:, col:col + 1]),
                     reads=[("xt", sl)], writes=[("ss", col)])

            def a_rs(T):
                col = (tbase + T) % 128
                rstd_from_ss(k, st_ss[:, col:col + 1], st_ms[:, col:col + 1], st_rs[:, col:col + 1], k.nhalf[:, 0:1],
                             [("ss", col)], ("ms", col), ("rs", col), 1.0 / D, 1e-6)

            def a_xn(T):
                sl = (tbase + T) % 5
                col = (tbase + T) % 128
                xs = T % 2
                S.op("dve", lambda h, sl=sl, col=col, xs=xs: h.scalar_tensor_tensor(
                    out=xn[xs][:], in0=xt[sl][:], scalar=st_rs[:, col:col + 1], in1=g0b[:], op0=ALU.mult, op1=ALU.mult),
                    reads=[("xt", sl), ("rs", col), "g0b"], writes=[("xn", xs)])

            def a_tr(T):
                st, t4 = divmod(T, 4)
                xs = T % 2
                pb = banks[T % 2]
                for dc in range(8):
                    S.op("pe", lambda h, dc=dc, xs=xs, pb=pb: h.transpose(
                        out=bank_bf16(pb, 8)[:, dc, :], in_=xn[xs][:, dc * 128:(dc + 1) * 128], identity=k.identb[:]),
                        reads=[("xn", xs), "identb"], writes=[("bank", T % 2)])
                S.op("act", lambda h, st=st, t4=t4, pb=pb: h.copy(out=xnT[st % 2][:, :, t4 * 128:(t4 + 1) * 128], in_=bank_bf16(pb, 8)),
                     reads=[], writes=[("bank", T % 2), ("xnT", st % 2, t4)])

            pcount = [0]

            def a_proj(st, q):
                xk = [("xnT", st % 2, t4) for t4 in range(4)]
                for gi in range(3 * q, 3 * q + 3):
                    bi = 2 + pcount[0] % 3
                    pcount[0] += 1
                    if gi < 8:
                        fc = gi
                        for dc in range(8):
                            S.op("pe", lambda h, fc=fc, dc=dc, bi=bi: h.matmul(
                                banks[bi][:, :], lhsT=Wg[:, dc, fc * 128:(fc + 1) * 128], rhs=xnT[st % 2][:, dc, :],
                                start=(dc == 0), stop=(dc == 7)),
                                reads=xk + wkeys, writes=[("bank", bi)])
                        if fc < 4:
                            S.op("act", lambda h, fc=fc, bi=bi: h.activation(out=QT[:, fc, st * 512:(st + 1) * 512], in_=banks[bi][:, :],
                                                                       func=AF.Copy, scale=0.125),
                                 writes=[("bank", bi), ("QT", st)])
                        else:
                            S.op("dve", lambda h, fc=fc, bi=bi: h.tensor_copy(out=KT[:, fc - 4, st * 512:(st + 1) * 512], in_=banks[bi][:, :]),
                                 writes=[("bank", bi), ("KT", st)])
                    else:
                        t4 = gi - 8
                        T = st * 4 + t4
                        for dc in range(8):
                            S.op("pe", lambda h, t4=t4, dc=dc, bi=bi: h.matmul(
                                banks[bi][:, :], lhsT=xnT[st % 2][:, dc, t4 * 128:(t4 + 1) * 128], rhs=Wg[:, dc, 1024:1536],
                                start=(dc == 0), stop=(dc == 7)),
                                reads=[("xnT", st % 2, t4)] + wkeys, writes=[("bank", bi)])
                        if t4 % 2 == 0:
                            S.op("act", lambda h, T=T, bi=bi: h.copy(out=Va[:, T, :, 0:64], in_=banks[bi][:, :].rearrange("p (h e) -> p h e", h=8)),
                                 writes=[("bank", bi), ("Va", T)])
                        else:
                            S.op("dve", lambda h, T=T, bi=bi: h.tensor_copy(out=Va[:, T, :, 0:64], in_=banks[bi][:, :].rearrange("p (h e) -> p h e", h=8)),
                                 writes=[("bank", bi), ("Va", T)])

            for u in range(-4, NT + 4):
                if 0 <= u + 4 < NT:
                    a_load(u + 4)
                if 0 <= u + 3 < NT:
                    a_sq(u + 3)
                if 0 <= u + 2 < NT:
                    a_rs(u + 2)
                if 0 <= u + 1 < NT:
                    a_xn(u + 1)
                if 0 <= u < NT:
                    a_tr(u)
                if u >= 4:
                    a_proj((u - 4) // 4, (u - 4) % 4)

            scount = [0]

            def stageB_scores(T):
                c, j = divmod(T, nb)
                ncols = 256 if j < nb - 1 else 128
                st_list = sorted(set([(T * 128) // 512, (T * 128 + ncols - 1) // 512]))
                for hd in range(8):
                    hr = slice(64 * (hd % 2), 64 * (hd % 2) + 64)
                    hp = hd // 2
                    bi = 2 + scount[0] % 3
                    si = scount[0] % 4
                    scount[0] += 1
                    S.op("pe", lambda h, T=T, hr=hr, hp=hp, bi=bi, ncols=ncols: h.matmul(
                        banks[bi][:, 0:ncols], lhsT=KT[hr, hp, T * 128:(T + 1) * 128], rhs=QT[hr, hp, T * 128:T * 128 + ncols],
                        start=True, stop=True),
                        reads=[("KT", T // 4)] + [("QT", s_) for s_ in st_list], writes=[("bank", bi)])
                    S.op("dve", lambda h, hd=hd, bi=bi, si=si, ncols=ncols: h.tensor_tensor(
                        out=Sb[si][:, 0:ncols], in0=banks[bi][:, 0:ncols], in1=bA[:, hd, 0:ncols], op=ALU.add),
                        reads=["bA"], writes=[("bank", bi), ("Sb", si)])
                    S.op("act", lambda h, T=T, hd=hd, si=si, ncols=ncols: h.activation(
                        out=PT[T % 3][:, hd, 0:ncols], in_=Sb[si][:, 0:ncols], func=AF.Exp),
                        reads=[("Sb", si)], writes=[("PT", T % 3, hd)])

            def stageB_pv(T):
                c, j = divmod(T, nb)
                o0 = 0 if T % 2 == 0 else 1
                obanks = (0, 1) if o0 == 0 else (5, 6)
                for hd in range(8):
                    bi = obanks[hd // 4]
                    oap = banks[bi][:, :].rearrange("p (h e) -> p h e", h=4)[:, hd % 4, 0:65]
                    if j > 0:
                        S.op("pe", lambda h, T=T, hd=hd, oap=oap: h.matmul(
                            oap, lhsT=PT[(T - 1) % 3][:, hd, 128:256], rhs=Va[:, T - 1, hd, :], start=True, stop=False),
                            reads=[("PT", (T - 1) % 3, hd), ("Va", T - 1), "Va1"], writes=[("bank", bi)])
                    S.op("pe", lambda h, T=T, hd=hd, oap=oap, j=j: h.matmul(
                        oap, lhsT=PT[T % 3][:, hd, 0:128], rhs=Va[:, T, hd, :], start=(j == 0), stop=True),
                        reads=[("PT", T % 3, hd), ("Va", T), "Va1"], writes=[("bank", bi)])
                osl = T % 2
                for half in range(2):
                    bi = obanks[half]
                    S.op("act", lambda h, half=half, bi=bi, osl=osl: h.copy(
                        out=Ost[osl][:, half * 260:(half + 1) * 260].rearrange("p (h e) -> p h e", h=4),
                        in_=banks[bi][:, :].rearrange("p (h e) -> p h e", h=4)[:, :, 0:65]),
                        writes=[("bank", bi), ("Ost", osl, half)])
                S.op("sp", lambda h, c=c, j=j, osl=osl, Ur=Ur: h.dma_start(out=Ur[c, j * 128:(j + 1) * 128, :], in_=Ost[osl][:]),
                     reads=[("Ost", osl, 0), ("Ost", osl, 1)], dma=("Ost", osl))

            for T in range(NT):
                stageB_scores(T)
                if T >= 1:
                    stageB_pv(T - 1)
            stageB_pv(NT - 1)
        emit_block(k)


def postnorm_residual(k, pbanks, pkeys, gb, gkey, hres, hkey, tmp, tmpkey, st, col):
    S = k.S
    ss, ms, rs = st
    for half in range(2):
        S.op("act", lambda h, half=half: h.activation(out=k.junk[:, 0:512], in_=pbanks[half][:, :], func=AF.Square,
                                                      accum_out=ss[:, 2 * col + half:2 * col + half + 1]),
             writes=[pkeys[half], ("ss", 2 * col + half)])
    S.op("dve", lambda h: h.tensor_tensor(out=ms[:, 2 * col:2 * col + 1], in0=ss[:, 2 * col:2 * col + 1], in1=ss[:, 2 * col + 1:2 * col + 2], op=ALU.add),
         reads=[("ss", 2 * col), ("ss", 2 * col + 1)], writes=[("ms", 2 * col)])
    rstd_from_ss(k, ms[:, 2 * col:2 * col + 1], ms[:, 2 * col + 1:2 * col + 2], rs[:, col:col + 1], k.nhalf[:, 0:1],
                 [("ms", 2 * col)], ("ms", 2 * col + 1), ("rs", col), 1.0 / D, 1e-6)
    for half in range(2):
        S.op("dve", lambda h, half=half: h.scalar_tensor_tensor(
            out=tmp[:, half * 512:(half + 1) * 512], in0=pbanks[half][:, :], scalar=rs[:, col:col + 1],
            in1=gb[:, half * 512:(half + 1) * 512], op0=ALU.mult, op1=ALU.mult),
            reads=[("rs", col), gkey], writes=[pkeys[half], (tmpkey, half)])
    S.op("pool", lambda h: h.tensor_tensor(out=hres, in0=hres, in1=tmp[:, :], op=ALU.add),
         reads=[(tmpkey, 0), (tmpkey, 1)], writes=[hkey])


def phase_outproj(k, layer, prefetch=None):
    nc, S = k.nc, k.S
    NHB = 7
    with ExitStack() as es:
        tg = k.newtag()
        sb = lambda n, s, d: es.enter_context(nc.sbuf_tensor(n + tg, s, d))
        banks = [es.enter_context(nc.psum_tensor("bk%d" % i + tg, [128, 512], F32)) for i in range(8)]
        nkc = 4 if layer == 0 else 8
        Wo = sb("Wo", [128, nkc, 1024], BF16)
        g1b = sb("g1b", [128, 1024], F32)
        hres = [sb("hres%d" % i, [128, 1024], F32) for i in range(NHB)]
        tmp = [sb("tmp%d" % i, [128, 1024], F32) for i in range(2)]
        oT = [sb("oT%d" % i, [128, nkc, 128], BF16) for i in range(2)]
        ss = sb("ss", [128, 128], F32)
        ms = sb("ms", [128, 128], F32)
        rs = sb("rs", [128, 64], F32)
        if layer == 0:
            Ul = [[sb("Ul%d_%d" % (i, g), [128, 520], F32) for g in range(3)] for i in range(3)]
            Us = [sb("Us%d" % i, [128, 520], F32) for i in range(2)]
            rden = [sb("rden%d" % i, [128, 8], F32) for i in range(2)]
            obf = [sb("obf%d" % i, [128, 512], BF16) for i in range(2)]
            wsrc = k.d["w_out_a"][0]
            hin = k.d["x"]
            hout = k.d["H1"]
        else:
            obf = [sb("obf%d" % i, [128, 1024], BF16) for i in range(3)]
            wsrc = k.d["w_out_b"][0]
            hin = k.d["H2"]
            hout = k.d["H1"]
        wkeys = load_cast_w(k, Wo, "Wo", wsrc, nkc, 1024, "Wo")
        S.op("sp", lambda h: h.dma_start(out=g1b[:], in_=k.d["norm_g"][layer, 1, :].partition_broadcast(128)), writes=["g1b"], dma="g1b")
        mbanks = ((2, 3), (4, 5), (6, 7))

        def s0(T):
            hs = T % NHB
            S.op("sp", lambda h: h.dma_start(out=hres[hs][:], in_=hin[T * 128:(T + 1) * 128, :]), writes=[("hres", hs)], dma=("hres", hs))
            if layer == 0:
                us = T % 3
                for g in range(3):
                    S.op("sp", lambda h, g=g: h.dma_start(out=Ul[us][g][:], in_=k.d["U%d" % g][T * 128:(T + 1) * 128, :]),
                         writes=[("Ul", us, g)], dma=("Ul", us, g))
            else:
                os_ = T % 3
                S.op("sp", lambda h: h.dma_start(out=obf[os_][:], in_=k.d["O1"][T * 128:(T + 1) * 128, :]), writes=[("obf", os_)], dma=("obf", os_))

        def s1(T):
            if layer != 0:
                return
            ul = T % 3
            us = T % 2
            S.op("dve", lambda h: h.tensor_tensor(out=Us[us][:], in0=Ul[ul][0][:], in1=Ul[ul][1][:], op=ALU.add),
                 reads=[("Ul", ul, 0), ("Ul", ul, 1)], writes=[("Us", us)])
            S.op("dve", lambda h: h.tensor_tensor(out=Us[us][:], in0=Us[us][:], in1=Ul[ul][2][:], op=ALU.add),
                 reads=[("Ul", ul, 2)], writes=[("Us", us)])
            usv = Us[us][:, :].rearrange("p (h e) -> p h e", h=8)
            S.op("dve", lambda h: h.reciprocal(out=rden[us][:, :], in_=usv[:, :, 64]), reads=[("Us", us)], writes=[("rden", us)])
            S.op("dve", lambda h: h.tensor_tensor(out=obf[us][:, :].rearrange("p (h e) -> p h e", h=8), in0=usv[:, :, 0:64],
                                                  in1=rden[us][:, :].unsqueeze(2).to_broadcast([128, 8, 64]), op=ALU.mult),
                 reads=[("Us", us), ("rden", us)], writes=[("obf", us)])

        def s2(T):
            if layer == 0:
                ok = ("obf", T % 2)
                osrc = obf[T % 2]
            else:
                ok = ("obf", T % 3)
                osrc = obf[T % 3]
            bi = T % 2
            for kc in range(nkc):
                S.op("pe", lambda h, kc=kc: h.transpose(out=bank_bf16(banks[bi], 8)[:, kc, :], in_=osrc[:, kc * 128:(kc + 1) * 128], identity=k.identb[:]),
                     reads=[ok, "identb"], writes=[("bank", bi)])
            S.op("act", lambda h: h.copy(out=oT[T % 2][:, :, :], in_=bank_bf16(banks[bi], 8)[:, 0:nkc, :]),
                 writes=[("bank", bi), ("oT", T % 2)])

        def s3(T):
            pb = mbanks[T % 3]
            col = T % 64
            for half in range(2):
                for kc in range(nkc):
                    S.op("pe", lambda h, half=half, kc=kc: h.matmul(
                        banks[pb[half]][:, :], lhsT=oT[T % 2][:, kc, :], rhs=Wo[:, kc, half * 512:(half + 1) * 512],
                        start=(kc == 0), stop=(kc == nkc - 1)),
                        reads=[("oT", T % 2)] + wkeys, writes=[("bank", pb[half])])
            for half in range(2):
                S.op("act", lambda h, half=half: h.activation(out=k.junk[:, 0:512], in_=banks[pb[half]][:, :], func=AF.Square,
                                                              accum_out=ss[:, 2 * col + half:2 * col + half + 1]),
                     writes=[("bank", pb[half]), ("ss", 2 * col + half)])

        def s4(T):
            col = T % 64
            S.op("dve", lambda h: h.tensor_tensor(out=ms[:, 2 * col:2 * col + 1], in0=ss[:, 2 * col:2 * col + 1], in1=ss[:, 2 * col + 1:2 * col + 2], op=ALU.add),
                 reads=[("ss", 2 * col), ("ss", 2 * col + 1)], writes=[("ms", 2 * col)])
            rstd_from_ss(k, ms[:, 2 * col:2 * col + 1], ms[:, 2 * col + 1:2 * col + 2], rs[:, col:col + 1], k.nhalf[:, 0:1],
                         [("ms", 2 * col)], ("ms", 2 * col + 1), ("rs", col), 1.0 / D, 1e-6)

        def s5(T):
            pb = mbanks[T % 3]
            col = T % 64
            hs = T % NHB
            tb = tmp[T % 2]
            for half in range(2):
                S.op("dve", lambda h, half=half: h.scalar_tensor_tensor(
                    out=tb[:, half * 512:(half + 1) * 512], in0=banks[pb[half]][:, :], scalar=rs[:, col:col + 1],
                    in1=g1b[:, half * 512:(half + 1) * 512], op0=ALU.mult, op1=ALU.mult),
                    reads=[("rs", col), "g1b"], writes=[("bank", pb[half]), ("tmp", T % 2, half)])
            S.op("pool", lambda h: h.tensor_tensor(out=hres[hs][:, :], in0=hres[hs][:, :], in1=tb[:, :], op=ALU.add),
                 reads=[("tmp", T % 2, 0), ("tmp", T % 2, 1)], writes=[("hres", hs)])
            S.op("sp", lambda h: h.dma_start(out=hout[T * 128:(T + 1) * 128, :], in_=hres[hs][:]), reads=[("hres", hs)], dma=("hout", hs))

        pre_ops = []
        if prefetch is not None:
            pdst, psrc = prefetch
            pwv = psrc.rearrange("(c p) n -> p c n", p=128)
            pi = 0
            for c in range(8):
                for c0 in range(0, 2 * DFF, 2048):
                    c1 = min(2 * DFF, c0 + 2048)
                    pre_ops.append((c, c0, c1, pi))
                    pi += 1
        stages = (s0, s1, s2, s3, s4, s5)
        for i in range(NT + len(stages) - 1):
            for si, fn in enumerate(stages):
                T = i - si
                if 0 <= T < NT:
                    fn(T)
            if i >= 4 and pre_ops:
                c, c0, c1, pi = pre_ops.pop(0)
                S.op("pool", lambda h, c=c, c0=c0, c1=c1: h.dma_start(out=pdst[:, c, c0:c1], in_=pwv[:, c, c0:c1]),
                     writes=[("WupP", c, c0)], dma=("WupP", pi % 4))
        assert not pre_ops
        emit_block(k)


def phase_ffn(k, layer, hin, hout, Wup_pre=None):
    nc, S = k.nc, k.S
    NS = SEQ // 256
    NCH = 2 * NFC
    NCB = 6
    with ExitStack() as es:
        tg = k.newtag()
        sb = lambda n, s, d: es.enter_context(nc.sbuf_tensor(n + tg, s, d))
        banks = [es.enter_context(nc.psum_tensor("bk%d" % i + tg, [128, 512], F32)) for i in range(8)]
        Wup = Wup_pre if Wup_pre is not None else sb("Wup", [128, 8, 2 * DFF], BF16)
        Wdn = sb("Wdn", [128, NFC, 1024], BF16)
        g2b = sb("g2b", [128, 1024], F32)
        g3b = sb("g3b", [128, 1024], F32)
        cp = sb("cp", [128, NCH, 4], F32)
        hres = [sb("hres%d" % i, [128, 2, 1024], F32) for i in range(3)]
        xn = [sb("xn%d" % i, [128, 1024], BF16) for i in range(2)]
        xnT = [sb("xnT%d" % i, [128, 8, 258], BF16) for i in range(2)]
        Cb = [sb("Cb%d" % i, [128, 256], F32) for i in range(NCB)]
        Gb = [sb("Gb%d" % i, [128, 256], F32) for i in range(3)]
        gT = sb("gT", [128, NFC, 256], BF16)
        tmp = sb("tmp", [128, 1024], F32)
        ss = sb("ss", [128, 128], F32)
        ms = sb("ms", [128, 128], F32)
        rs = sb("rs", [128, 64], F32)
        ss2 = sb("ss2", [128, 64], F32)
        ms2 = sb("ms2", [128, 64], F32)
        rs2 = sb("rs2", [128, 64], F32)

        wup_keys = [] if Wup_pre is not None else load_cast_w(k, Wup, "Wup", k.d["w_up"][layer], 8, 2 * DFF, "Wup")
        wdn_keys = load_cast_w(k, Wdn, "Wdn", k.d["w_down"][layer], NFC, 1024, "Wdn")
        S.op("sp", lambda h: h.dma_start(out=g2b[:], in_=k.d["norm_g"][layer, 2, :].partition_broadcast(128)), writes=["g2b"], dma="g2b")
        S.op("sp", lambda h: h.dma_start(out=g3b[:], in_=k.d["norm_g"][layer, 3, :].partition_broadcast(128)), writes=["g3b"], dma="g3b")
        S.op("sp", lambda h: h.dma_start(out=cp[:], in_=k.d["convp"][layer]), writes=["cp"], dma="cp")
        S.op("pool", lambda h: h.memset(xnT[0][:, :, 0:2], 0.0), writes=[("xnT", 0, "h")])

        def load(s):
            hs = s % 3
            for tt in range(2):
                T = s * 2 + tt
                S.op("sp", lambda h, tt=tt, T=T: h.dma_start(out=hres[hs][:, tt, :], in_=hin[T * 128:(T + 1) * 128, :]),
                     writes=[("hres", hs, tt)], dma=("hres", hs, tt))

        def prenorm_a(s):
            hs = s % 3
            for tt in range(2):
                T = s * 2 + tt
                col = T % 64
                S.op("act", lambda h, tt=tt, col=col: h.activation(out=k.junk[:], in_=hres[hs][:, tt, :], func=AF.Square,
                                                                   accum_out=ss2[:, col:col + 1]),
                     reads=[("hres", hs, tt)], writes=[("ss2", col)])
                rstd_from_ss(k, ss2[:, col:col + 1], ms2[:, col:col + 1], rs2[:, col:col + 1], k.nhalf[:, 0:1],
                             [("ss2", col)], ("ms2", col), ("rs2", col), 1.0 / D, 1e-6)
                S.op("dve", lambda h, tt=tt, col=col: h.scalar_tensor_tensor(
                    out=xn[tt][:], in0=hres[hs][:, tt, :], scalar=rs2[:, col:col + 1], in1=g2b[:], op0=ALU.mult, op1=ALU.mult),
                    reads=[("hres", hs, tt), ("rs2", col), "g2b"], writes=[("xn", tt)])

        def prenorm_b(s):
            xb = s % 2
            if s > 0:
                S.op("pool", lambda h: h.tensor_copy(out=xnT[xb][:, :, 0:2], in_=xnT[1 - xb][:, :, 256:258]),
                     reads=[("xnT", 1 - xb, 1)], writes=[("xnT", xb, "h")])
            for tt in range(2):
                for dc in range(8):
                    S.op("pe", lambda h, tt=tt, dc=dc: h.transpose(out=bank_bf16(banks[0], 8)[:, dc, :], in_=xn[tt][:, dc * 128:(dc + 1) * 128],
                                                                  identity=k.identb[:]),
                         reads=[("xn", tt), "identb"], writes=[("bank", 0)])
                S.op("act", lambda h, tt=tt: h.copy(out=xnT[xb][:, :, 2 + tt * 128:2 + (tt + 1) * 128], in_=bank_bf16(banks[0], 8)),
                     writes=[("bank", 0), ("xnT", xb, tt)])

        def stA(s, n):
            i, which = divmod(n, 2)
            fc = i + which * NFC
            bi = 1 + n % 3
            xb = s % 2
            for dc in range(8):
                S.op("pe", lambda h, fc=fc, dc=dc, bi=bi: h.matmul(
                    banks[bi][:, 0:258], lhsT=Wup[:, dc, fc * 128:(fc + 1) * 128], rhs=xnT[xb][:, dc, :], start=(dc == 0), stop=(dc == 7)),
                    reads=[("xnT", xb, 0), ("xnT", xb, 1), ("xnT", xb, "h")] + wup_keys, writes=[("bank", bi)])

        def stB(s, n):
            i, which = divmod(n, 2)
            fc = i + which * NFC
            bi = 1 + n % 3
            ci = n % NCB
            S.op("act", lambda h: h.activation(out=Cb[ci][:, :], in_=banks[bi][:, 2:258], func=AF.Identity,
                                               scale=cp[:, fc, 2:3], bias=cp[:, fc, 3:4]),
                 reads=["cp"], writes=[("bank", bi), ("Cb", ci)])

        def stC(s, n):
            i, which = divmod(n, 2)
            fc = i + which * NFC
            bi = 1 + n % 3
            ci = n % NCB
            S.op("dve", lambda h: h.scalar_tensor_tensor(
                out=Cb[ci][:, :], in0=banks[bi][:, 1:257], scalar=cp[:, fc, 1:2], in1=Cb[ci][:, :], op0=ALU.mult, op1=ALU.add),
                reads=["cp"], writes=[("bank", bi), ("Cb", ci)])
            S.op("dve", lambda h: h.scalar_tensor_tensor(
                out=Cb[ci][:, :], in0=banks[bi][:, 0:256], scalar=cp[:, fc, 0:1], in1=Cb[ci][:, :], op0=ALU.mult, op1=ALU.add),
                reads=["cp"], writes=[("bank", bi), ("Cb", ci)])

        def stD(s, n):
            i, which = divmod(n, 2)
            ci = n % NCB
            gi = i % 3
            if which == 0:
                S.op("act", lambda h: h.activation(out=Gb[gi][:, :], in_=Cb[ci][:, :], func=AF.Gelu),
                     reads=[("Cb", ci)], writes=[("Gb", gi)])
            else:
                S.op("pool", lambda h: h.tensor_tensor(out=gT[:, i, :], in0=Gb[gi][:, :], in1=Cb[ci][:, :], op=ALU.mult),
                     reads=[("Gb", gi), ("Cb", ci)], writes=[("gT", i)])

        def down_mm(s, tt):
            if True:
                pb = (4, 5) if tt == 0 else (6, 7)
                for half in range(2):
                    for i in range(NFC):
                        S.op("pe", lambda h, tt=tt, half=half, i=i, pb=pb: h.matmul(
                            banks[pb[half]][:, :], lhsT=gT[:, i, tt * 128:(tt + 1) * 128], rhs=Wdn[:, i, half * 512:(half + 1) * 512],
                            start=(i == 0), stop=(i == NFC - 1)),
                            reads=[("gT", i)] + wdn_keys, writes=[("bank", pb[half])])

        def down_post(s, tt):
            hs = s % 3
            if True:
                T = s * 2 + tt
                pb = (4, 5) if tt == 0 else (6, 7)
                postnorm_residual(k, [banks[pb[0]], banks[pb[1]]], [("bank", pb[0]), ("bank", pb[1])], g3b, "g3b",
                                  hres[hs][:, tt, :], ("hres", hs, tt), tmp, "tmp", (ss, ms, rs), T % 64)
                S.op("sp", lambda h, tt=tt, T=T: h.dma_start(out=hout[T * 128:(T + 1) * 128, :], in_=hres[hs][:, tt, :]),
                     reads=[("hres", hs, tt)], dma=("hout", hs, tt))

        load(0)
        load(1)
        prenorm_a(0)
        prenorm_b(0)
        for s in range(NS):
            if s + 2 < NS:
                load(s + 2)
            for t in range(NCH + 3):
                if t < NCH:
                    stA(s, t)
                if 0 <= t - 1 < NCH:
                    stB(s, t - 1)
                if 0 <= t - 2 < NCH:
                    stC(s, t - 2)
                if 0 <= t - 3 < NCH:
                    stD(s, t - 3)
            if s + 1 < NS:
                prenorm_a(s + 1)
            down_mm(s, 0)
            if s + 1 < NS:
                prenorm_b(s + 1)
            down_post(s, 0)
            down_mm(s, 1)
            down_post(s, 1)
        emit_block(k)


def phase3(k):
    nc, S = k.nc, k.S
    with ExitStack() as es:
        tg = k.newtag()
        sb = lambda n, s, d: es.enter_context(nc.sbuf_tensor(n + tg, s, d))
        banks = [es.enter_context(nc.psum_tensor("bk%d" % i + tg, [128, 512], F32)) for i in range(8)]
        Wq = sb("Wq", [128, 8, 1024], BF16)
        Wk = sb("Wk", [128, 8, 1024], BF16)
        Wv = sb("Wv", [128, 8, 1024], BF16)
        gqb = sb("gqb", [128, 1024], F32)
        gkb = sb("gkb", [128, 1024], F32)
        ht = [sb("ht%d" % i, [128, 1024], F32) for i in range(5)]
        xq = [sb("xq%d" % i, [128, 1024], BF16) for i in range(2)]
        xk = [sb("xk%d" % i, [128, 1024], BF16) for i in range(2)]
        xqT = [sb("xqT%d" % i, [128, 8, 512], BF16) for i in range(2)]
        xkT = [sb("xkT%d" % i, [128, 8, 512], BF16) for i in range(2)]
        Qs = [sb("Qs%d" % i, [128, 512], BF16) for i in range(4)]
        Vs = [sb("Vs%d" % i, [128, 1024], BF16) for i in range(2)]
        ss = sb("ss", [128, 128], F32)
        ms = sb("ms", [128, 128], F32)
        rs = sb("rs", [128, 128], F32)
        wq_keys = load_cast_w(k, Wq, "Wq", k.d["w_q_b"][0], 8, 1024, "Wq")
        wk_keys = load_cast_w(k, Wk, "Wk", k.d["w_k_shared"], 8, 1024, "Wk")
        wv_keys = load_cast_w(k, Wv, "Wv", k.d["w_v_shared"], 8, 1024, "Wv")
        S.op("sp", lambda h: h.dma_start(out=gqb[:], in_=k.d["norm_g"][1, 0, :].partition_broadcast(128)), writes=["gqb"], dma="gqb")
        S.op("sp", lambda h: h.dma_start(out=gkb[:], in_=k.d["kv_norm_g"].partition_broadcast(128)), writes=["gkb"], dma="gkb")
        hin = k.d["H2"]

        def a_load(T):
            sl = T % 5
            S.op("sp", lambda h, T=T, sl=sl: h.dma_start(out=ht[sl][:], in_=hin[T * 128:(T + 1) * 128, :]), writes=[("ht", sl)], dma=("ht", sl))

        def a_sq(T):
            sl = T % 5
            col = T % 128
            S.op("act", lambda h, sl=sl, col=col: h.activation(out=k.junk[:], in_=ht[sl][:], func=AF.Square, accum_out=ss[:, col:col + 1]),
                 reads=[("ht", sl)], writes=[("ss", col)])

        def a_rs(T):
            col = T % 128
            rstd_from_ss(k, ss[:, col:col + 1], ms[:, col:col + 1], rs[:, col:col + 1], k.nhalf[:, 0:1],
                         [("ss", col)], ("ms", col), ("rs", col), 1.0 / D, 1e-6)

        def a_xn(T):
            sl = T % 5
            col = T % 128
            xs = T % 2
            for (xb, gb, gk, nm) in ((xq, gqb, "gqb", "xq"), (xk, gkb, "gkb", "xk")):
                S.op("dve", lambda h, sl=sl, col=col, xb=xb, gb=gb, xs=xs: h.scalar_tensor_tensor(
                    out=xb[xs][:], in0=ht[sl][:], scalar=rs[:, col:col + 1], in1=gb[:], op0=ALU.mult, op1=ALU.mult),
                    reads=[("ht", sl), ("rs", col), gk], writes=[(nm, xs)])

        def a_tr(T):
            st, t4 = divmod(T, 4)
            xs = T % 2
            for (xb, xT, nm, bi) in ((xq, xqT, "xq", 0), (xk, xkT, "xk", 1)):
                for dc in range(8):
                    S.op("pe", lambda h, dc=dc, xb=xb, bi=bi, xs=xs: h.transpose(out=bank_bf16(banks[bi], 8)[:, dc, :], in_=xb[xs][:, dc * 128:(dc + 1) * 128],
                                                                               identity=k.identb[:]),
                         reads=[(nm, xs), "identb"], writes=[("bank", bi)])
                S.op("act", lambda h, t4=t4, xT=xT, bi=bi, st=st: h.copy(out=xT[st % 2][:, :, t4 * 128:(t4 + 1) * 128], in_=bank_bf16(banks[bi], 8)),
                     writes=[("bank", bi), (nm + "T", st % 2, t4)])

        pc = [0]

        def a_proj(st, q):
            for gi in range(6 * q, 6 * q + 6):
                bi = 2 + pc[0] % 4
                qi = pc[0] % 4
                pc[0] += 1
                if gi < 16:
                    if gi < 8:
                        W, wkeys, xT, nm, dst, scale, hd = Wq, wq_keys, xqT, "xq", k.d["QT1"], 0.125, gi
                    else:
                        W, wkeys, xT, nm, dst, scale, hd = Wk, wk_keys, xkT, "xk", k.d["KT1"], None, gi - 8
                    xkeys = [(nm + "T", st % 2, t4) for t4 in range(4)]
                    for dc in range(8):
                        S.op("pe", lambda h, hd=hd, dc=dc, bi=bi, W=W, xT=xT: h.matmul(
                            banks[bi][:, :], lhsT=W[:, dc, hd * 128:(hd + 1) * 128], rhs=xT[st % 2][:, dc, :], start=(dc == 0), stop=(dc == 7)),
                            reads=xkeys + wkeys, writes=[("bank", bi)])
                    if scale is not None:
                        S.op("act", lambda h, bi=bi, qi=qi: h.activation(out=Qs[qi][:, :], in_=banks[bi][:, :], func=AF.Copy, scale=0.125),
                             writes=[("bank", bi), ("Qs", qi)])
                    else:
                        S.op("dve", lambda h, bi=bi, qi=qi: h.tensor_copy(out=Qs[qi][:, :], in_=banks[bi][:, :]),
                             writes=[("bank", bi), ("Qs", qi)])
                    S.op("sp", lambda h, hd=hd, qi=qi, dst=dst: h.dma_start(out=dst[hd, :, st * 512:(st + 1) * 512], in_=Qs[qi][:, :]),
                         reads=[("Qs", qi)], dma=("Qs", qi))
                else:
                    t4, half = divmod(gi - 16, 2)
                    T = st * 4 + t4
                    vs = T % 2
                    for dc in range(8):
                        S.op("pe", lambda h, t4=t4, half=half, dc=dc, bi=bi: h.matmul(
                            banks[bi][:, :], lhsT=xkT[st % 2][:, dc, t4 * 128:(t4 + 1) * 128], rhs=Wv[:, dc, half * 512:(half + 1) * 512],
                            start=(dc == 0), stop=(dc == 7)),
                            reads=[("xkT", st % 2, t4)] + wv_keys, writes=[("bank", bi)])
                    if half == 0:
                        S.op("act", lambda h, bi=bi, vs=vs: h.copy(out=Vs[vs][:, 0:512], in_=banks[bi][:, :]), writes=[("bank", bi), ("Vs", vs, 0)])
                    else:
                        S.op("dve", lambda h, bi=bi, vs=vs: h.tensor_copy(out=Vs[vs][:, 512:1024], in_=banks[bi][:, :]), writes=[("bank", bi), ("Vs", vs, 1)])
                        S.op("sp", lambda h, T=T, vs=vs: h.dma_start(out=k.d["V1"][T * 128:(T + 1) * 128, :], in_=Vs[vs][:, :]),
                             reads=[("Vs", vs, 0), ("Vs", vs, 1)], dma=("Vs", vs))

        for u in range(-4, NT + 4):
            if 0 <= u + 4 < NT:
                a_load(u + 4)
            if 0 <= u + 3 < NT:
                a_sq(u + 3)
            if 0 <= u + 2 < NT:
                a_rs(u + 2)
            if 0 <= u + 1 < NT:
                a_xn(u + 1)
            if 0 <= u < NT:
                a_tr(u)
            if u >= 4:
                a_proj((u - 4) // 4, (u - 4) % 4)
        emit_block(k)


def phase4(k, heads=range(8)):
    nc, S = k.nc, k.S
    NSB = 4
    SK = 3
    with ExitStack() as es:
        tg = k.newtag()
        sb = lambda n, s, d: es.enter_context(nc.sbuf_tensor(n + tg, s, d))
        banks = [es.enter_context(nc.psum_tensor("bk%d" % i + tg, [128, 512], F32)) for i in range(8)]
        KTh = [sb("KTh%d" % i, [128, SEQ], BF16) for i in range(2)]
        QZ = [[sb("QZ%d_%d" % (i, c), [128, SEQ], BF16) for c in range(2)] for i in range(2)]
        Vh = [sb("Vh%d" % i, [128, NT, 129], BF16) for i in range(2)]
        Tb = [sb("Tb%d" % i, [128, TBW], F32) for i in range(2)]
        mB = sb("mB", [128, 128], F32)
        Sb = [sb("Sb%d" % i, [128, 512], F32) for i in range(NSB)]
        PT = [sb("PT%d" % i, [128, 512], BF16) for i in range(NSB)]
        Ost = [[sb("Ost%d_%d" % (i, c), [128, 4, 129], F32) for c in range(2)] for i in range(2)]
        Oh = [sb("Oh%d" % i, [128, NT, 128], BF16) for i in range(2)]
        lv = [sb("lv%d" % i, [128, 64], F32) for i in range(4)]
        lsum = sb("lsum", [128, 4], F32)
        lam = sb("lam", [128, 1], F32)
        sgb = sb("sgb", [128, 128], F32)
        rd = sb("rd", [128, 64, 8], F32)
        av_all = [sb("av%d" % i, [128, 128], F32) for i in range(8)]
        tv_all = [sb("tv%d" % i, [128, 128], F32) for i in range(8)]
        st1 = sb("st1", [128, 128], F32)
        st2 = sb("st2", [128, 128], F32)
        st3 = sb("st3", [128, 128], F32)

        for i, nm in enumerate(("lam_q1", "lam_k1", "lam_q2", "lam_k2")):
            S.op("sp", lambda h, i=i, nm=nm: h.dma_start(out=lv[i][:], in_=k.d[nm][0, :].partition_broadcast(128)), writes=[("lv", i)], dma=("lv", i))
        S.op("sp", lambda h: h.dma_start(out=sgb[:], in_=k.d["subln_g"][0, :].partition_broadcast(128)), writes=["sgb"], dma="sgb")
        S.op("sp", lambda h: h.dma_start(out=mB[:], in_=k.d["maskB"]), writes=["mB"], dma="mB")
        S.op("dve", lambda h: h.tensor_scalar(out=sgb[:], in0=sgb[:], scalar1=1.0 - LAMBDA_INIT, scalar2=None, op0=ALU.mult),
             reads=["sgb"], writes=["sgb"])
        for i in range(2):
            S.op("dve", lambda h, i=i: h.scalar_tensor_tensor(out=k.junk[:, 0:64], in0=lv[2 * i][:], scalar=1.0, in1=lv[2 * i + 1][:],
                                                            op0=ALU.mult, op1=ALU.mult, accum_out=lsum[:, i:i + 1]),
                 reads=[("lv", 2 * i), ("lv", 2 * i + 1)], writes=[("lsum", i)])
            S.op("act", lambda h, i=i: h.activation(out=lsum[:, 2 + i:3 + i], in_=lsum[:, i:i + 1], func=AF.Exp),
                 reads=[("lsum", i)], writes=[("lsum", 2 + i)])
        S.op("dve", lambda h: h.tensor_tensor(out=lam[:], in0=lsum[:, 2:3], in1=lsum[:, 3:4], op=ALU.subtract),
             reads=[("lsum", 2), ("lsum", 3)], writes=["lam"])
        S.op("dve", lambda h: h.tensor_scalar(out=lam[:], in0=lam[:], scalar1=LAMBDA_INIT, scalar2=None, op0=ALU.add),
             reads=["lam"], writes=["lam"])
        for i in range(2):
            S.op("pool", lambda h, i=i: h.memset(Vh[i][:, :, 128:129], 1.0), writes=[("Vh1", i)])
            S.op("pool", lambda h, i=i: h.memset(QZ[i][0][64:128, :], 0.0), writes=[("QZz", i, 0)])
            S.op("pool", lambda h, i=i: h.memset(QZ[i][1][0:64, :], 0.0), writes=[("QZz", i, 1)])

        def load_head(hi, hd, t_now=None):
            b = hi % 2
            S.op("sp", lambda h: h.dma_start(out=Tb[b][:], in_=k.d["biasB"][hd]), writes=[("Tb", b)], dma=("Tb", b))

            def mask_add():
                S.op("dve", lambda h: h.tensor_tensor(out=Tb[b][:, 0:128], in0=Tb[b][:, 0:128], in1=mB[:], op=ALU.add),
                     reads=["mB", ("Tb", b)], writes=[("Tb", b)])
            if t_now is None:
                mask_add()
            else:
                defer(t_now + 80, mask_add)
            S.op("sp", lambda h: h.dma_start(out=KTh[b][:], in_=k.d["KT1"][hd]), writes=[("KTh", b)], dma=("KTh", b))
            S.op("sp", lambda h: h.dma_start(out=QZ[b][0][0:64, :], in_=k.d["QT1"][hd, 0:64, :]), writes=[("QZ", b, 0)], dma=("QZ", b, 0))
            S.op("sp", lambda h: h.dma_start(out=QZ[b][1][64:128, :], in_=k.d["QT1"][hd, 64:128, :]), writes=[("QZ", b, 1)], dma=("QZ", b, 1))
            S.op("sp", lambda h: h.dma_start(out=Vh[b][:, :, 0:128], in_=k.d["V1"][:, hd * 128:(hd + 1) * 128].rearrange("(t p) e -> p t e", p=128)),
                 writes=[("Vh", b)], dma=("Vh", b))

        obank = (4, 5, 6, 7)
        heads = list(heads)
        items = []
        for hi, hd in enumerate(heads):
            for qs in range(8):
                for c in range(2):
                    for j in range(4 * qs + 4):
                        items.append((hi, hd, qs, c, j))
        N = len(items)
        deferred = {}

        def defer(t, fn):
            deferred.setdefault(t, []).append(fn)

        def geom(it):
            hi, hd, qs, c, j = it
            qt0 = max(0, j - 4 * qs)
            col0 = qt0 * 128
            ncols = 512 - col0
            delta = qs * 512 - j * 128 + col0
            off = min(delta, TB_CONST)
            assert 0 <= off and off + ncols <= TBW
            return qt0, col0, ncols, off

        def stA(n):
            hi, hd, qs, c, j = items[n]
            b = hi % 2
            qt0, col0, ncols, off = geom(items[n])
            bi = n % 4
            S.op("pe", lambda h: h.matmul(
                banks[bi][:, col0:512], lhsT=KTh[b][:, j * 128:(j + 1) * 128], rhs=QZ[b][c][:, qs * 512 + col0:(qs + 1) * 512],
                start=True, stop=True),
                reads=[("KTh", b), ("QZ", b, c), ("QZz", b, c)], writes=[("bank", bi)])

        def stB(n):
            hi, hd, qs, c, j = items[n]
            b = hi % 2
            qt0, col0, ncols, off = geom(items[n])
            bi = n % 4
            si = n % NSB
            if off == TB_CONST:
                return
            S.op("dve", lambda h: h.tensor_tensor(
                out=Sb[si][:, col0:512], in0=banks[bi][:, col0:512], in1=Tb[b][:, off:off + ncols], op=ALU.add),
                reads=[("Tb", b)], writes=[("bank", bi), ("Sb", si)])

        def stC(n):
            qt0, col0, ncols, off = geom(items[n])
            si = n % NSB
            if off == TB_CONST:
                b = items[n][0] % 2
                bi = n % 4
                S.op("act", lambda h: h.activation(out=PT[si][:, col0:512], in_=banks[bi][:, col0:512], func=AF.Exp,
                                                   bias=Tb[b][:, TBW - 1:TBW]),
                     reads=[("Tb", b)], writes=[("bank", bi), ("PT", si)])
                return
            S.op("act", lambda h: h.activation(out=PT[si][:, col0:512], in_=Sb[si][:, col0:512], func=AF.Exp),
                 reads=[("Sb", si)], writes=[("PT", si)])

        def evac(hi, qs, c, qt):
            osl = qs % 2
            if False:
                S.op("act", lambda h: h.copy(out=Ost[osl][c][:, qt, :], in_=banks[obank[qt]][:, 0:129]),
                     writes=[("bank", obank[qt]), ("Ost", osl, c, qt)])
            else:
                S.op("dve", lambda h: h.tensor_copy(out=Ost[osl][c][:, qt, :], in_=banks[obank[qt]][:, 0:129]),
                     writes=[("bank", obank[qt]), ("Ost", osl, c, qt)])

        cc = [0]

        def combine_parts(hi, hd, qs):
            b = hi % 2
            osl = qs % 2
            ci = cc[0] % 64
            cc[0] += 1
            cb = (ci % 32) * 4
            av = av_all[(ci % 2) * 4:(ci % 2) * 4 + 4]
            tv = tv_all[(ci % 2) * 4:(ci % 2) * 4 + 4]
            ab = (ci % 2) * 4
            okeys = [("Ost", osl, c_, qt) for c_ in range(2) for qt in range(4)]

            def c1():
                S.op("dve", lambda h: h.reciprocal(out=rd[:, ci, 0:4], in_=Ost[osl][0][:, :, 128]), reads=okeys, writes=[("rd", ci, 0)])
                S.op("dve", lambda h: h.reciprocal(out=rd[:, ci, 4:8], in_=Ost[osl][1][:, :, 128]), reads=okeys, writes=[("rd", ci, 1)])
                S.op("dve", lambda h: h.tensor_scalar(out=rd[:, ci, 4:8], in0=rd[:, ci, 4:8], scalar1=lam[:, 0:1], scalar2=None, op0=ALU.mult),
                     reads=[("rd", ci, 1), "lam"], writes=[("rd", ci, 1)])

            def c2():
                for qt in range(4):
                    S.op("pool", lambda h, qt=qt: h.tensor_scalar(out=tv[qt][:], in0=Ost[osl][1][:, qt, 0:128], scalar1=rd[:, ci, 4 + qt:5 + qt],
                                                                  scalar2=0.0, op0=ALU.mult, op1=ALU.add),
                         reads=okeys + [("rd", ci, 1)], writes=[("tv", ab + qt)])
                    S.op("pool", lambda h, qt=qt: h.tensor_scalar(out=av[qt][:], in0=Ost[osl][0][:, qt, 0:128], scalar1=rd[:, ci, qt:qt + 1],
                                                                  scalar2=0.0, op0=ALU.mult, op1=ALU.add),
                         reads=okeys + [("rd", ci, 0)], writes=[("av", ab + qt)])
                    S.op("pool", lambda h, qt=qt: h.tensor_tensor(out=av[qt][:], in0=av[qt][:], in1=tv[qt][:], op=ALU.subtract),
                         reads=[("tv", ab + qt)], writes=[("av", ab + qt)])

            def c3():
                for qt in range(4):
                    S.op("dve", lambda h, qt=qt: h.scalar_tensor_tensor(
                        out=k.junk[:, 0:128], in0=av[qt][:], scalar=1.0, in1=av[qt][:], op0=ALU.mult, op1=ALU.mult,
                        accum_out=st1[:, cb + qt:cb + qt + 1]),
                        reads=[("av", ab + qt)], writes=[("st1", cb + qt)])
                S.op("dve", lambda h: h.tensor_scalar(out=st2[:, cb:cb + 4], in0=st1[:, cb:cb + 4], scalar1=1.0 / 128, scalar2=1e-5,
                                                      op0=ALU.mult, op1=ALU.add),
                     reads=[("st1", cb + qt) for qt in range(4)], writes=[("st2", cb)])

            def c4():
                S.op("pool", lambda h: h.tensor_tensor(out=st3[:, cb:cb + 4], in0=st2[:, cb:cb + 4], in1=k.nhalf[:, 0:4], op=ALU.pow),
                     reads=[("st2", cb)], writes=[("st3", cb)])
                for qt in range(4):
                    T = qs * 4 + qt
                    S.op("pool", lambda h, qt=qt: h.tensor_scalar(out=av[qt][:], in0=av[qt][:], scalar1=st3[:, cb + qt:cb + qt + 1],
                                                                  scalar2=0.0, op0=ALU.mult, op1=ALU.add),
                         reads=[("st3", cb)], writes=[("av", ab + qt)])
                    S.op("pool", lambda h, qt=qt, T=T: h.tensor_tensor(out=Oh[b][:, T, :], in0=av[qt][:], in1=sgb[:], op=ALU.mult),
                         reads=[("av", ab + qt), "sgb"], writes=[("Oh", b, T)])
                if qs == 7:
                    S.op("sp", lambda h: h.dma_start(out=k.d["O1"][:, hd * 128:(hd + 1) * 128].rearrange("(t p) e -> p t e", p=128), in_=Oh[b][:]),
                         reads=[("Oh", b, T) for T in range(NT)], dma=("Oh", b))

            return c1, c2, c3, c4

        def stD(n, t):
            hi, hd, qs, c, j = items[n]
            b = hi % 2
            qt0, col0, ncols, off = geom(items[n])
            si = n % NSB
            for qt in range(qt0, 4):
                last = (j == 4 * qs + qt)
                S.op("pe", lambda h, qt=qt, last=last: h.matmul(
                    banks[obank[qt]][:, 0:129], lhsT=PT[si][:, qt * 128:(qt + 1) * 128], rhs=Vh[b][:, j, :],
                    start=(j == 0), stop=last),
                    reads=[("PT", si), ("Vh", b), ("Vh1", b)], writes=[("bank", obank[qt])])
                if last:
                    defer(t + 1, lambda hi=hi, qs=qs, c=c, qt=qt: evac(hi, qs, c, qt))
            if c == 1 and j == 4 * qs + 3:
                parts = combine_parts(hi, hd, qs)
                for pi, dt_ in enumerate((4, 6, 12, 15)):
                    defer(t + dt_, parts[pi])

        load_head(0, heads[0])
        for t in range(N + SK + 20):
            for fn in deferred.pop(t, []):
                fn()
            if t < N:
                it = items[t]
                if it[2] == 0 and it[3] == 1 and it[4] == 0 and it[0] + 1 < len(heads):
                    load_head(it[0] + 1, heads[it[0] + 1], t)
                stA(t)
            if 0 <= t - 1 < N:
                stB(t - 1)
            if 0 <= t - 2 < N:
                stC(t - 2)
            if 0 <= t - SK < N:
                stD(t - SK, t)
        assert not deferred
        emit_block(k)


ALL_PHASES = ("p1", "p2a", "ffn0", "p3", "p4", "p5a", "ffn1")


def build(phases=ALL_PHASES, kinds=None):
    kinds = kinds or {}
    nc = bass.Bass("TRN2", target_bir_lowering=False)
    k = K()
    k.nc = nc
    d = {}

    def dt(name, shape, dtype, kind):
        d[name] = nc.dram_tensor(name, list(shape), dtype, kind=kinds.get(name, kind)).ap()

    EI = "ExternalInput"
    dt("x", [SEQ, D], F32, EI)
    dt("norm_g", [2, 4, D], F32, EI)
    dt("w_in_a", [1, D, 4608], F32, EI)
    dt("w_out_a", [1, 512, D], F32, EI)
    dt("kv_norm_g", [D], F32, EI)
    dt("w_k_shared", [D, D], F32, EI)
    dt("w_v_shared", [D, D], F32, EI)
    dt("w_q_b", [1, D, D], F32, EI)
    for nm in ("lam_q1", "lam_k1", "lam_q2", "lam_k2"):
        dt(nm, [1, 64], F32, EI)
    dt("subln_g", [1, 128], F32, EI)
    dt("w_out_b", [1, D, D], F32, EI)
    dt("w_up", [2, D, 2 * DFF], F32, EI)
    dt("w_down", [2, DFF, D], F32, EI)
    dt("convp", [2, 128, 2 * NFC, 4], F32, EI)
    dt("ident", [128, 128], F32, EI)
    dt("biasA", [3, NH, 128, 256], F32, EI)
    dt("maskA", [128, 256], F32, EI)
    dt("biasB", [NH, 128, TBW], F32, EI)
    dt("maskB", [128, 128], F32, EI)
    for g in range(3):
        dt("U%d" % g, [SEQ, 520], F32, "Internal")
    dt("H1", [SEQ, D], F32, "Internal")
    dt("H2", [SEQ, D], F32, "Internal")
    dt("QT1", [NH, 128, SEQ], BF16, "Internal")
    dt("KT1", [NH, 128, SEQ], BF16, "Internal")
    dt("V1", [SEQ, D], BF16, "Internal")
    dt("O1", [SEQ, D], BF16, "Internal")
    dt("out", [SEQ, D], F32, "ExternalOutput")
    k.d = d

    with ExitStack() as es:
        k.S = Sched(nc, es)
        S = k.S
        identf = es.enter_context(nc.sbuf_tensor("identf", [128, 128], F32))
        k.identb = es.enter_context(nc.sbuf_tensor("identb", [128, 128], BF16))
        k.nhalf = es.enter_context(nc.sbuf_tensor("nhalf", [128, 8], F32))
        k.junk = es.enter_context(nc.sbuf_tensor("junkg", [128, 1024], BF16))
        S.op("sp", lambda h: h.dma_start(out=identf[:], in_=d["ident"]), writes=["identf"], dma="identf")
        S.op("dve", lambda h: h.tensor_copy(out=k.identb[:], in_=identf[:]), reads=["identf"], writes=["identb"])
        S.op("dve", lambda h: h.memset(k.nhalf[:], -0.5), writes=["nhalf"])
        emit_block(k)
        for ph in phases:
            if ph == "p1":
                phase1(k)
            elif ph.startswith("p1g"):
                phase1(k, groups=(int(ph[3]),))
            elif ph == "p2a":
                if "ffn0" in phases:
                    k.wes0 = ExitStack()
                    k.Wup0 = k.wes0.enter_context(nc.sbuf_tensor("WupP0", [128, 8, 2 * DFF], BF16))
                    phase_outproj(k, 0, prefetch=(k.Wup0, d["w_up"][0]))
                else:
                    phase_outproj(k, 0)
            elif ph == "ffn0":
                if getattr(k, "Wup0", None) is not None:
                    phase_ffn(k, 0, d["H1"], d["H2"], Wup_pre=k.Wup0)
                    k.wes0.close()
                else:
                    phase_ffn(k, 0, d["H1"], d["H2"])
            elif ph == "p3":
                phase3(k)
            elif ph == "p4":
                phase4(k)
            elif ph.startswith("p4h"):
                phase4(k, heads=[int(ph[3])])
            elif ph == "p5a":
                if "ffn1" in phases:
                    k.wes1 = ExitStack()
                    k.Wup1 = k.wes1.enter_context(nc.sbuf_tensor("WupP1", [128, 8, 2 * DFF], BF16))
                    phase_outproj(k, 1, prefetch=(k.Wup1, d["w_up"][1]))
                else:
                    phase_outproj(k, 1)
            elif ph == "ffn1":
                if getattr(k, "Wup1", None) is not None:
                    phase_ffn(k, 1, d["H1"], d["out"], Wup_pre=k.Wup1)
                    k.wes1.close()
                else:
                    phase_ffn(k, 1, d["H1"], d["out"])
    return nc


def _bucket(n):
    n = np.maximum(n, 0)
    nf = np.maximum(n, 1).astype(np.float32)
    large = 16 + (np.log(nf / np.float32(16)) / np.float32(math.log(2048 / 16)) * np.float32(16)).astype(np.int32)
    large = np.minimum(large, 31)
    return np.where(n < 16, n, large).astype(np.int64)


def host_consts(rel_bias_table, conv_w, conv_b):
    tab = np.asarray(rel_bias_table, dtype=np.float32)
    kl = np.arange(128)[:, None]
    qc = np.arange(256)[None, :]
    du = qc - kl
    biasA = np.stack([np.take(tab, _bucket(r * np.clip(du, 0, 128)), axis=1) for r in DIL]).astype(np.float32)
    maskA = np.where((du >= 0) & (du <= 128), np.float32(0.0), np.float32(NEG)).astype(np.float32)
    jj = np.arange(TBW)[None, :]
    pp = np.arange(128)[:, None]
    biasB = np.take(tab, _bucket(np.maximum(jj - pp, 0)), axis=1).astype(np.float32)
    maskB = np.where(jj[:, :128] - pp >= 0, np.float32(0.0), np.float32(NEG)).astype(np.float32)
    cw = np.asarray(conv_w, dtype=np.float32)
    cb = np.asarray(conv_b, dtype=np.float32)
    convp = np.concatenate([cw, cb[:, None, :]], axis=1)
    convp = np.ascontiguousarray(convp.reshape(2, 4, 2 * NFC, 128).transpose(0, 3, 2, 1))
    return dict(biasA=np.ascontiguousarray(biasA), maskA=maskA, biasB=np.ascontiguousarray(biasB), maskB=np.ascontiguousarray(maskB),
                convp=convp, ident=np.eye(128, dtype=np.float32))


_NC_CACHE = {}


def kernel(x, rel_bias_table, norm_g, w_in_a, w_out_a, kv_norm_g, w_k_shared, w_v_shared, w_q_b,
           lam_q1, lam_k1, lam_q2, lam_k2, subln_g, w_out_b, w_up, conv_w, conv_b, w_down):
    f = lambda a: np.ascontiguousarray(np.asarray(a, dtype=np.float32))
    hc = host_consts(rel_bias_table, conv_w, conv_b)
    shared = dict(norm_g=f(norm_g), w_in_a=f(w_in_a), w_out_a=f(w_out_a), kv_norm_g=f(kv_norm_g), w_k_shared=f(w_k_shared),
                  w_v_shared=f(w_v_shared), w_q_b=f(w_q_b), lam_q1=f(lam_q1), lam_k1=f(lam_k1), lam_q2=f(lam_q2), lam_k2=f(lam_k2),
                  subln_g=f(subln_g), w_out_b=f(w_out_b), w_up=f(w_up), w_down=f(w_down), **hc)
    x = f(x)
    if "nc" not in _NC_CACHE:
        _NC_CACHE["nc"] = build()
    nc = _NC_CACHE["nc"]
    in_maps = [dict(shared, x=x[b]) for b in range(8)]
    res = run_bass_kernel_spmd(nc, in_maps, core_ids=list(range(8)))
    return np.stack([np.asarray(res.results[b]["out"], dtype=np.float32) for b in range(8)], axis=0)
```

```python
import math
from contextlib import ExitStack

import numpy as np
import concourse.bass as bass
import concourse.mybir as mybir
from concourse.bass_utils import run_bass_kernel_spmd

F32 = mybir.dt.float32
BF16 = mybir.dt.bfloat16
AF = mybir.ActivationFunctionType
ALU = mybir.AluOpType

SEQ = 4096
D = 1024
NT = SEQ // 128
NH = 8
DIL = (1, 4, 16)
DFF = 2816
NFC = DFF // 128
NEG = -30000.0
LAMBDA_INIT = 0.8 - 0.6 * math.exp(-0.3 * 1)
TBW = 2176
TB_CONST = 1664

ENGS = ("pe", "act", "dve", "pool", "sp")


class _Op:
    __slots__ = ("eng", "fn", "deps", "idx", "dma", "sig", "sigval", "dsem", "dval")

    def __init__(self, eng, fn, dma):
        self.eng = eng
        self.fn = fn
        self.deps = []
        self.dma = dma
        self.sig = False
        self.sigval = 0
        self.dsem = None
        self.dval = 0


class Sched:
    def __init__(self, nc, es):
        self.nc = nc
        self.es = es
        self.sems = {e: es.enter_context(nc.semaphore("s_" + e)) for e in ENGS}
        self.q = {e: [] for e in ENGS}
        self.lastw = {}
        self.readers = {}
        self.sigcount = {e: 0 for e in ENGS}
        self.dma_slot = {}
        self.free_slots = {}
        self.nsem = 0

    def _slot(self, key, eng):
        if key not in self.dma_slot:
            fs = self.free_slots.setdefault(eng, [])
            if fs:
                self.dma_slot[key] = fs.pop()
            else:
                self.nsem += 1
                self.dma_slot[key] = [self.es.enter_context(self.nc.semaphore("d%d" % self.nsem)), 0, eng]
        assert self.dma_slot[key][2] == eng
        return self.dma_slot[key]

    def op(self, eng, fn, reads=(), writes=(), dma=None):
        o = _Op(eng, fn, dma is not None)
        o.idx = len(self.q[eng])
        if dma is not None:
            s = self._slot(dma, eng)
            s[1] += 16
            o.dsem, o.dval = s[0], s[1]
        dep = {}

        def add(d):
            if d is None or d is o:
                return
            k = ("d", id(d.dsem)) if d.dma else ("e", d.eng)
            cur = dep.get(k)
            if cur is None or (cur.dval < d.dval if d.dma else cur.idx < d.idx):
                dep[k] = d

        for b in reads:
            add(self.lastw.get(b))
        for b in writes:
            add(self.lastw.get(b))
            for r in self.readers.get(b, {}).values():
                add(r)
        o.deps = list(dep.values())
        me = ("d", id(o.dsem)) if o.dma else ("e", eng)
        for b in reads:
            self.readers.setdefault(b, {})[me] = o
        for b in writes:
            self.lastw[b] = o
            self.readers[b] = {}
        self.q[eng].append(o)
        return o

    def finalize(self):
        for e in ENGS:
            for o in self.q[e]:
                for d in o.deps:
                    if d.dma:
                        continue
                    if d.eng == "pe" and o.eng == "pe":
                        continue
                    d.sig = True
        for e in ENGS:
            c = self.sigcount[e]
            for o in self.q[e]:
                if o.sig and not o.dma:
                    c += 1
                    o.sigval = c
            self.sigcount[e] = c

    def emit_engine(self, e, h):
        known = {}
        for o in self.q[e]:
            need = {}
            for d in o.deps:
                if d.dma:
                    k = ("d", id(d.dsem))
                    if need.get(k, (None, 0))[1] < d.dval:
                        need[k] = (d.dsem, d.dval)
                else:
                    if d.eng == "pe" and e == "pe":
                        continue
                    k = ("e", d.eng)
                    if need.get(k, (None, 0))[1] < d.sigval:
                        need[k] = (self.sems[d.eng], d.sigval)
            for k, (sem, val) in need.items():
                if known.get(k, 0) >= val:
                    continue
                h.wait_ge(sem, val)
                known[k] = val
            ins = o.fn(h)
            if o.dma:
                ins.then_inc(o.dsem, 16)
            elif o.sig:
                ins.then_inc(self.sems[e], 1)

    def final_dma_waits(self, h):
        for key, sl in self.dma_slot.items():
            if sl[1]:
                h.wait_ge(sl[0], sl[1])

    def reset_phase(self):
        for sl in self.dma_slot.values():
            self.free_slots.setdefault(sl[2], []).append(sl)
        self.dma_slot = {}
        self.q = {e: [] for e in ENGS}
        self.lastw = {}
        self.readers = {}


class K:
    ntag = 0

    def newtag(self):
        self.ntag += 1
        return "_p%d" % self.ntag


def emit_block(k):
    S, nc = k.S, k.nc
    S.finalize()
    with nc.Block() as block:
        @block.sync
        def _(h):
            S.emit_engine("sp", h)
            S.final_dma_waits(h)

        @block.tensor
        def _(h):
            S.emit_engine("pe", h)

        @block.scalar
        def _(h):
            S.emit_engine("act", h)

        @block.vector
        def _(h):
            S.emit_engine("dve", h)

        @block.gpsimd
        def _(h):
            S.emit_engine("pool", h)
    S.reset_phase()


def bank_bf16(bank, c):
    return bank[:, :].bitcast(BF16).rearrange("p (c t) -> p c t", c=c)


def load_cast_w(k, dst, dst_key, w2d, nchunk, ncols, slot):
    S = k.S
    wv = w2d.rearrange("(c p) n -> p c n", p=128)
    keys = []
    i = 0
    for c in range(nchunk):
        for c0 in range(0, ncols, 2048):
            c1 = min(ncols, c0 + 2048)
            key = (dst_key, c, c0)
            keys.append(key)
            S.op("pool", lambda h, c=c, c0=c0, c1=c1: h.dma_start(out=dst[:, c, c0:c1], in_=wv[:, c, c0:c1]),
                 writes=[key], dma=(slot, i % 4))
            i += 1
    return keys


def rstd_from_ss(k, ss_ap, ms_ap, rstd_ap, nh_ap, keys_ss, key_ms, key_rstd, inv_n, eps):
    S = k.S
    S.op("dve", lambda h: h.tensor_scalar(out=ms_ap, in0=ss_ap, scalar1=inv_n, scalar2=eps, op0=ALU.mult, op1=ALU.add),
         reads=keys_ss, writes=[key_ms])
    S.op("pool", lambda h: h.tensor_tensor(out=rstd_ap, in0=ms_ap, in1=nh_ap, op=ALU.pow),
         reads=[key_ms], writes=[key_rstd])


def phase1(k, groups=(0, 1, 2)):
    nc, S = k.nc, k.S
    with ExitStack() as es:
        tg = k.newtag()
        sb = lambda n, s, d: es.enter_context(nc.sbuf_tensor(n + tg, s, d))
        banks = [es.enter_context(nc.psum_tensor("bk%d" % i + tg, [128, 512], F32)) for i in range(8)]
        g0b = sb("g0b", [128, 1024], F32)
        Wg = sb("Wg", [128, 8, 1536], BF16)
        xt = [sb("xt%d" % i, [128, 1024], F32) for i in range(5)]
        junk = k.junk
        xn = [sb("xn%d" % i, [128, 1024], BF16) for i in range(2)]
        xnT = [sb("xnT%d" % i, [128, 8, 512], BF16) for i in range(2)]
        QT = sb("QT", [128, 4, SEQ], BF16)
        KT = sb("KT", [128, 4, SEQ], BF16)
        Va = sb("Va", [128, NT, 8, 65], BF16)
        bA = sb("bA", [128, 8, 256], F32)
        mA = sb("mA", [128, 256], F32)
        Sb = [sb("Sb%d" % i, [128, 2, 256], F32) for i in range(4)]
        PT = [sb("PT%d" % i, [128, 8, 256], BF16) for i in range(3)]
        Ost = [sb("Ost%d" % i, [128, 520], F32) for i in range(2)]
        st_ss = sb("st_ss", [128, 128], F32)
        st_ms = sb("st_ms", [128, 128], F32)
        st_rs = sb("st_rs", [128, 128], F32)

        S.op("sp", lambda h: h.dma_start(out=g0b[:], in_=k.d["norm_g"][0, 0, :].partition_broadcast(128)), writes=["g0b"], dma="g0b")
        S.op("sp", lambda h: h.dma_start(out=mA[:], in_=k.d["maskA"]), writes=["mA"], dma="mA")
        S.op("pool", lambda h: h.memset(Va[:, :, :, 64:65], 1.0), writes=["Va1"])

        x = k.d["x"]
        tcount = 0
        for g in groups:
            r = DIL[g]
            nb = NT // r
            xr = x.rearrange("(m r) d -> r m d", r=r)
            Ur = k.d["U%d" % g].rearrange("(m r) d -> r m d", r=r)
            wkeys = load_cast_w(k, Wg, "Wg", k.d["w_in_a"][0, :, g * 1536:(g + 1) * 1536], 8, 1536, "Wg")
            S.op("sp", lambda h, g=g: h.dma_start(out=bA[:], in_=k.d["biasA"][g].rearrange("h p q -> p h q")),
                 writes=["bA"], dma="bA")
            for hh in range(8):
                S.op("pool", lambda h, hh=hh: h.tensor_tensor(out=bA[:, hh, :], in0=bA[:, hh, :], in1=mA[:], op=ALU.add),
                     reads=["bA", "mA"], writes=["bA"])

            tbase = tcount
            tcount += NT

            def a_load(T):
                c, n = divmod(T, nb)
                sl = (tbase + T) % 5
                S.op("sp", lambda h, c=c, n=n, sl=sl, xr=xr: h.dma_start(out=xt[sl][:], in_=xr[c, n * 128:(n + 1) * 128, :]),
                     writes=[("xt", sl)], dma=("xt", sl))

            def a_sq(T):
                sl = (tbase + T) % 5
                col = (tbase + T) % 128
                S.op("act", lambda h, sl=sl, col=col: h.activation(out=junk[:], in_=xt[sl][:], func=AF.Square,
                                                                accum_out=st_ss[:, col:col + 1]),
                     reads=[("xt", sl)], writes=[("ss", col)])

            def a_rs(T):
                col = (tbase + T) % 128
                rstd_from_ss(k, st_ss[:, col:col + 1], st_ms[:, col:col + 1], st_rs[:, col:col + 1], k.nhalf[:, 0:1],
                             [("ss", col)], ("ms", col), ("rs", col), 1.0 / D, 1e-6)

            def a_xn(T):
                sl = (tbase + T) % 5
                col = (tbase + T) % 128
                xs = T % 2
                S.op("dve", lambda h, sl=sl, col=col, xs=xs: h.scalar_tensor_tensor(
                    out=xn[xs][:], in0=xt[sl][:], scalar=st_rs[:, col:col + 1], in1=g0b[:], op0=ALU.mult, op1=ALU.mult),
                    reads=[("xt", sl), ("rs", col), "g0b"], writes=[("xn", xs)])

            def a_tr(T):
                st, t4 = divmod(T, 4)
                xs = T % 2
                pb = banks[T % 2]
                for dc in range(8):
                    S.op("pe", lambda h, dc=dc, xs=xs, pb=pb: h.transpose(
                        out=bank_bf16(pb, 8)[:, dc, :], in_=xn[xs][:, dc * 128:(dc + 1) * 128], identity=k.identb[:]),
                        reads=[("xn", xs), "identb"], writes=[("bank", T % 2)])
                S.op("act", lambda h, st=st, t4=t4, pb=pb: h.copy(out=xnT[st % 2][:, :, t4 * 128:(t4 + 1) * 128], in_=bank_bf16(pb, 8)),
                     reads=[], writes=[("bank", T % 2), ("xnT", st % 2, t4)])

            pcount = [0]

            def a_proj(st, q):
                xk = [("xnT", st % 2, t4) for t4 in range(4)]
                for gi in range(3 * q, 3 * q + 3):
                    bi = 2 + pcount[0] % 3
                    pcount[0] += 1
                    if gi < 8:
                        fc = gi
                        for dc in range(8):
                            S.op("pe", lambda h, fc=fc, dc=dc, bi=bi: h.matmul(
                                banks[bi][:, :], lhsT=Wg[:, dc, fc * 128:(fc + 1) * 128], rhs=xnT[st % 2][:, dc, :],
                                start=(dc == 0), stop=(dc == 7)),
                                reads=xk + wkeys, writes=[("bank", bi)])
                        if fc < 4:
                            S.op("act", lambda h, fc=fc, bi=bi: h.activation(out=QT[:, fc, st * 512:(st + 1) * 512], in_=banks[bi][:, :],
                                                                       func=AF.Copy, scale=0.125),
                                 writes=[("bank", bi), ("QT", st)])
                        else:
                            S.op("dve", lambda h, fc=fc, bi=bi: h.tensor_copy(out=KT[:, fc - 4, st * 512:(st + 1) * 512], in_=banks[bi][:, :]),
                                 writes=[("bank", bi), ("KT", st)])
                    else:
                        t4 = gi - 8
                        T = st * 4 + t4
                        for dc in range(8):
                            S.op("pe", lambda h, t4=t4, dc=dc, bi=bi: h.matmul(
                                banks[bi][:, :], lhsT=xnT[st % 2][:, dc, t4 * 128:(t4 + 1) * 128], rhs=Wg[:, dc, 1024:1536],
                                start=(dc == 0), stop=(dc == 7)),
                                reads=[("xnT", st % 2, t4)] + wkeys, writes=[("bank", bi)])
                        if t4 % 2 == 0:
                            S.op("act", lambda h, T=T, bi=bi: h.copy(out=Va[:, T, :, 0:64], in_=banks[bi][:, :].rearrange("p (h e) -> p h e", h=8)),
                                 writes=[("bank", bi), ("Va", T)])
                        else:
                            S.op("dve", lambda h, T=T, bi=bi: h.tensor_copy(out=Va[:, T, :, 0:64], in_=banks[bi][:, :].rearrange("p (h e) -> p h e", h=8)),
                                 writes=[("bank", bi), ("Va", T)])

            for u in range(-4, NT + 4):
                if 0 <= u + 4 < NT:
                    a_load(u + 4)
                if 0 <= u + 3 < NT:
                    a_sq(u + 3)
                if 0 <= u + 2 < NT:
                    a_rs(u + 2)
                if 0 <= u + 1 < NT:
                    a_xn(u + 1)
                if 0 <= u < NT:
                    a_tr(u)
                if u >= 4:
                    a_proj((u - 4) // 4, (u - 4) % 4)

            scount = [0]

            def stageB_scores(T):
                c, j = divmod(T, nb)
                ncols = 256 if j < nb - 1 else 128
                st_list = sorted(set([(T * 128) // 512, (T * 128 + ncols - 1) // 512]))
                for hd in range(8):
                    hr = slice(64 * (hd % 2), 64 * (hd % 2) + 64)
                    hp = hd // 2
                    bi = 2 + scount[0] % 3
                    si = (scount[0] // 2) % 4
                    scount[0] += 1
                    S.op("pe", lambda h, T=T, hr=hr, hp=hp, bi=bi, ncols=ncols: h.matmul(
                        banks[bi][:, 0:ncols], lhsT=KT[hr, hp, T * 128:(T + 1) * 128], rhs=QT[hr, hp, T * 128:T * 128 + ncols],
                        start=True, stop=True),
                        reads=[("KT", T // 4)] + [("QT", s_) for s_ in st_list], writes=[("bank", bi)])
                    S.op("dve", lambda h, hd=hd, bi=bi, si=si, ncols=ncols: h.tensor_tensor(
                        out=Sb[si][:, hd % 2, 0:ncols], in0=banks[bi][:, 0:ncols], in1=bA[:, hd, 0:ncols], op=ALU.add),
                        reads=["bA"], writes=[("bank", bi), ("Sb", si, hd % 2)])
                    if hd % 2 == 1:
                        S.op("act", lambda h, T=T, hd=hd, si=si, ncols=ncols: h.activation(
                            out=PT[T % 3][:, hd - 1:hd + 1, 0:ncols], in_=Sb[si][:, :, 0:ncols], func=AF.Exp),
                            reads=[("Sb", si, 0), ("Sb", si, 1)], writes=[("PT", T % 3, hd - 1), ("PT", T % 3, hd)])

            def stageB_pv(T):
                c, j = divmod(T, nb)
                o0 = 0 if T % 2 == 0 else 1
                obanks = (0, 1) if o0 == 0 else (5, 6)
                for hd in range(8):
                    bi = obanks[hd // 4]
                    oap = banks[bi][:, :].rearrange("p (h e) -> p h e", h=4)[:, hd % 4, 0:65]
                    if j > 0:
                        S.op("pe", lambda h, T=T, hd=hd, oap=oap: h.matmul(
                            oap, lhsT=PT[(T - 1) % 3][:, hd, 128:256], rhs=Va[:, T - 1, hd, :], start=True, stop=False),
                            reads=[("PT", (T - 1) % 3, hd), ("Va", T - 1), "Va1"], writes=[("bank", bi)])
                    S.op("pe", lambda h, T=T, hd=hd, oap=oap, j=j: h.matmul(
                        oap, lhsT=PT[T % 3][:, hd, 0:128], rhs=Va[:, T, hd, :], start=(j == 0), stop=True),
                        reads=[("PT", T % 3, hd), ("Va", T), "Va1"], writes=[("bank", bi)])
                osl = T % 2
                for half in range(2):
                    bi = obanks[half]
                    S.op("act", lambda h, half=half, bi=bi, osl=osl: h.copy(
                        out=Ost[osl][:, half * 260:(half + 1) * 260].rearrange("p (h e) -> p h e", h=4),
                        in_=banks[bi][:, :].rearrange("p (h e) -> p h e", h=4)[:, :, 0:65]),
                        writes=[("bank", bi), ("Ost", osl, half)])
                S.op("sp", lambda h, c=c, j=j, osl=osl, Ur=Ur: h.dma_start(out=Ur[c, j * 128:(j + 1) * 128, :], in_=Ost[osl][:]),
                     reads=[("Ost", osl, 0), ("Ost", osl, 1)], dma=("Ost", osl))

            for T in range(NT):
                stageB_scores(T)
                if T >= 1:
                    stageB_pv(T - 1)
            stageB_pv(NT - 1)
        emit_block(k)


def postnorm_residual(k, pbanks, pkeys, gb, gkey, hres, hkey, tmp, tmpkey, st, col):
    S = k.S
    ss, ms, rs = st
    for half in range(2):
        S.op("act", lambda h, half=half: h.activation(out=k.junk[:, 0:512], in_=pbanks[half][:, :], func=AF.Square,
                                                      accum_out=ss[:, 2 * col + half:2 * col + half + 1]),
             writes=[pkeys[half], ("ss", 2 * col + half)])
    S.op("dve", lambda h: h.tensor_tensor(out=ms[:, 2 * col:2 * col + 1], in0=ss[:, 2 * col:2 * col + 1], in1=ss[:, 2 * col + 1:2 * col + 2], op=ALU.add),
         reads=[("ss", 2 * col), ("ss", 2 * col + 1)], writes=[("ms", 2 * col)])
    rstd_from_ss(k, ms[:, 2 * col:2 * col + 1], ms[:, 2 * col + 1:2 * col + 2], rs[:, col:col + 1], k.nhalf[:, 0:1],
                 [("ms", 2 * col)], ("ms", 2 * col + 1), ("rs", col), 1.0 / D, 1e-6)
    for half in range(2):
        S.op("dve", lambda h, half=half: h.scalar_tensor_tensor(
            out=tmp[:, half * 512:(half + 1) * 512], in0=pbanks[half][:, :], scalar=rs[:, col:col + 1],
            in1=gb[:, half * 512:(half + 1) * 512], op0=ALU.mult, op1=ALU.mult),
            reads=[("rs", col), gkey], writes=[pkeys[half], (tmpkey, half)])
    S.op("pool", lambda h: h.tensor_tensor(out=hres, in0=hres, in1=tmp[:, :], op=ALU.add),
         reads=[(tmpkey, 0), (tmpkey, 1)], writes=[hkey])


def phase_outproj(k, layer, prefetch=None):
    nc, S = k.nc, k.S
    NHB = 7
    with ExitStack() as es:
        tg = k.newtag()
        sb = lambda n, s, d: es.enter_context(nc.sbuf_tensor(n + tg, s, d))
        banks = [es.enter_context(nc.psum_tensor("bk%d" % i + tg, [128, 512], F32)) for i in range(8)]
        nkc = 4 if layer == 0 else 8
        Wo = sb("Wo", [128, nkc, 1024], BF16)
        g1b = sb("g1b", [128, 1024], F32)
        hres = [sb("hres%d" % i, [128, 1024], F32) for i in range(NHB)]
        tmp = [sb("tmp%d" % i, [128, 1024], F32) for i in range(2)]
        oT = [sb("oT%d" % i, [128, nkc, 128], BF16) for i in range(2)]
        ss = sb("ss", [128, 128], F32)
        ms = sb("ms", [128, 128], F32)
        rs = sb("rs", [128, 64], F32)
        if layer == 0:
            Ul = [[sb("Ul%d_%d" % (i, g), [128, 520], F32) for g in range(3)] for i in range(3)]
            Us = [sb("Us%d" % i, [128, 520], F32) for i in range(2)]
            rden = [sb("rden%d" % i, [128, 8], F32) for i in range(2)]
            obf = [sb("obf%d" % i, [128, 512], BF16) for i in range(2)]
            wsrc = k.d["w_out_a"][0]
            hin = k.d["x"]
            hout = k.d["H1"]
        else:
            obf = [sb("obf%d" % i, [128, 1024], BF16) for i in range(3)]
            wsrc = k.d["w_out_b"][0]
            hin = k.d["H2"]
            hout = k.d["H1"]
        wkeys = load_cast_w(k, Wo, "Wo", wsrc, nkc, 1024, "Wo")
        S.op("sp", lambda h: h.dma_start(out=g1b[:], in_=k.d["norm_g"][layer, 1, :].partition_broadcast(128)), writes=["g1b"], dma="g1b")
        mbanks = ((2, 3), (4, 5), (6, 7))

        def s0(T):
            hs = T % NHB
            S.op("sp", lambda h: h.dma_start(out=hres[hs][:], in_=hin[T * 128:(T + 1) * 128, :]), writes=[("hres", hs)], dma=("hres", hs))
            if layer == 0:
                us = T % 3
                for g in range(3):
                    S.op("sp", lambda h, g=g: h.dma_start(out=Ul[us][g][:], in_=k.d["U%d" % g][T * 128:(T + 1) * 128, :]),
                         writes=[("Ul", us, g)], dma=("Ul", us, g))
            else:
                os_ = T % 3
                S.op("sp", lambda h: h.dma_start(out=obf[os_][:], in_=k.d["O1"][T * 128:(T + 1) * 128, :]), writes=[("obf", os_)], dma=("obf", os_))

        def s1(T):
            if layer != 0:
                return
            ul = T % 3
            us = T % 2
            S.op("dve", lambda h: h.tensor_tensor(out=Us[us][:], in0=Ul[ul][0][:], in1=Ul[ul][1][:], op=ALU.add),
                 reads=[("Ul", ul, 0), ("Ul", ul, 1)], writes=[("Us", us)])
            S.op("dve", lambda h: h.tensor_tensor(out=Us[us][:], in0=Us[us][:], in1=Ul[ul][2][:], op=ALU.add),
                 reads=[("Ul", ul, 2)], writes=[("Us", us)])
            usv = Us[us][:, :].rearrange("p (h e) -> p h e", h=8)
            S.op("dve", lambda h: h.reciprocal(out=rden[us][:, :], in_=usv[:, :, 64]), reads=[("Us", us)], writes=[("rden", us)])
            S.op("dve", lambda h: h.tensor_tensor(out=obf[us][:, :].rearrange("p (h e) -> p h e", h=8), in0=usv[:, :, 0:64],
                                                  in1=rden[us][:, :].unsqueeze(2).to_broadcast([128, 8, 64]), op=ALU.mult),
                 reads=[("Us", us), ("rden", us)], writes=[("obf", us)])

        def s2(T):
            if layer == 0:
                ok = ("obf", T % 2)
                osrc = obf[T % 2]
            else:
                ok = ("obf", T % 3)
                osrc = obf[T % 3]
            bi = T % 2
            for kc in range(nkc):
                S.op("pe", lambda h, kc=kc: h.transpose(out=bank_bf16(banks[bi], 8)[:, kc, :], in_=osrc[:, kc * 128:(kc + 1) * 128], identity=k.identb[:]),
                     reads=[ok, "identb"], writes=[("bank", bi)])
            S.op("act", lambda h: h.copy(out=oT[T % 2][:, :, :], in_=bank_bf16(banks[bi], 8)[:, 0:nkc, :]),
                 writes=[("bank", bi), ("oT", T % 2)])

        def s3(T):
            pb = mbanks[T % 3]
            col = T % 64
            for half in range(2):
                for kc in range(nkc):
                    S.op("pe", lambda h, half=half, kc=kc: h.matmul(
                        banks[pb[half]][:, :], lhsT=oT[T % 2][:, kc, :], rhs=Wo[:, kc, half * 512:(half + 1) * 512],
                        start=(kc == 0), stop=(kc == nkc - 1)),
                        reads=[("oT", T % 2)] + wkeys, writes=[("bank", pb[half])])
            for half in range(2):
                S.op("act", lambda h, half=half: h.activation(out=k.junk[:, 0:512], in_=banks[pb[half]][:, :], func=AF.Square,
                                                              accum_out=ss[:, 2 * col + half:2 * col + half + 1]),
                     writes=[("bank", pb[half]), ("ss", 2 * col + half)])

        def s4(T):
            col = T % 64
            S.op("dve", lambda h: h.tensor_tensor(out=ms[:, 2 * col:2 * col + 1], in0=ss[:, 2 * col:2 * col + 1], in1=ss[:, 2 * col + 1:2 * col + 2], op=ALU.add),
                 reads=[("ss", 2 * col), ("ss", 2 * col + 1)], writes=[("ms", 2 * col)])
            rstd_from_ss(k, ms[:, 2 * col:2 * col + 1], ms[:, 2 * col + 1:2 * col + 2], rs[:, col:col + 1], k.nhalf[:, 0:1],
                         [("ms", 2 * col)], ("ms", 2 * col + 1), ("rs", col), 1.0 / D, 1e-6)

        def s5(T):
            pb = mbanks[T % 3]
            col = T % 64
            hs = T % NHB
            tb = tmp[T % 2]
            for half in range(2):
                S.op("dve", lambda h, half=half: h.scalar_tensor_tensor(
                    out=tb[:, half * 512:(half + 1) * 512], in0=banks[pb[half]][:, :], scalar=rs[:, col:col + 1],
                    in1=g1b[:, half * 512:(half + 1) * 512], op0=ALU.mult, op1=ALU.mult),
                    reads=[("rs", col), "g1b"], writes=[("bank", pb[half]), ("tmp", T % 2, half)])
            S.op("pool", lambda h: h.tensor_tensor(out=hres[hs][:, :], in0=hres[hs][:, :], in1=tb[:, :], op=ALU.add),
                 reads=[("tmp", T % 2, 0), ("tmp", T % 2, 1)], writes=[("hres", hs)])
            S.op("sp", lambda h: h.dma_start(out=hout[T * 128:(T + 1) * 128, :], in_=hres[hs][:]), reads=[("hres", hs)], dma=("hout", hs))

        pre_ops = []
        if prefetch is not None:
            pdst, psrc = prefetch
            pwv = psrc.rearrange("(c p) n -> p c n", p=128)
            pi = 0
            for c in range(8):
                for c0 in range(0, 2 * DFF, 2048):
                    c1 = min(2 * DFF, c0 + 2048)
                    pre_ops.append((c, c0, c1, pi))
                    pi += 1
        stages = (s0, s1, s2, s3, s4, s5)
        for i in range(NT + len(stages) - 1):
            for si, fn in enumerate(stages):
                T = i - si
                if 0 <= T < NT:
                    fn(T)
            if i >= 4 and pre_ops:
                c, c0, c1, pi = pre_ops.pop(0)
                S.op("pool", lambda h, c=c, c0=c0, c1=c1: h.dma_start(out=pdst[:, c, c0:c1], in_=pwv[:, c, c0:c1]),
                     writes=[("WupP", c, c0)], dma=("WupP", pi % 4))
        assert not pre_ops
        emit_block(k)


def phase_ffn(k, layer, hin, hout, Wup_pre=None):
    nc, S = k.nc, k.S
    NS = SEQ // 256
    NCH = 2 * NFC
    NCB = 6
    with ExitStack() as es:
        tg = k.newtag()
        sb = lambda n, s, d: es.enter_context(nc.sbuf_tensor(n + tg, s, d))
        banks = [es.enter_context(nc.psum_tensor("bk%d" % i + tg, [128, 512], F32)) for i in range(8)]
        Wup = Wup_pre if Wup_pre is not None else sb("Wup", [128, 8, 2 * DFF], BF16)
        Wdn = sb("Wdn", [128, NFC, 1024], BF16)
        g2b = sb("g2b", [128, 1024], F32)
        g3b = sb("g3b", [128, 1024], F32)
        cp = sb("cp", [128, NCH, 4], F32)
        hres = [sb("hres%d" % i, [128, 2, 1024], F32) for i in range(3)]
        xn = [sb("xn%d" % i, [128, 1024], BF16) for i in range(2)]
        xnT = [sb("xnT%d" % i, [128, 8, 258], BF16) for i in range(2)]
        Cb = [sb("Cb%d" % i, [128, 256], F32) for i in range(NCB)]
        Gb = [sb("Gb%d" % i, [128, 256], F32) for i in range(3)]
        gT = sb("gT", [128, NFC, 256], BF16)
        tmp = sb("tmp", [128, 1024], F32)
        ss = sb("ss", [128, 128], F32)
        ms = sb("ms", [128, 128], F32)
        rs = sb("rs", [128, 64], F32)
        ss2 = sb("ss2", [128, 64], F32)
        ms2 = sb("ms2", [128, 64], F32)
        rs2 = sb("rs2", [128, 64], F32)

        wup_keys = [] if Wup_pre is not None else load_cast_w(k, Wup, "Wup", k.d["w_up"][layer], 8, 2 * DFF, "Wup")
        wdn_keys = load_cast_w(k, Wdn, "Wdn", k.d["w_down"][layer], NFC, 1024, "Wdn")
        S.op("sp", lambda h: h.dma_start(out=g2b[:], in_=k.d["norm_g"][layer, 2, :].partition_broadcast(128)), writes=["g2b"], dma="g2b")
        S.op("sp", lambda h: h.dma_start(out=g3b[:], in_=k.d["norm_g"][layer, 3, :].partition_broadcast(128)), writes=["g3b"], dma="g3b")
        S.op("sp", lambda h: h.dma_start(out=cp[:], in_=k.d["convp"][layer]), writes=["cp"], dma="cp")
        S.op("pool", lambda h: h.memset(xnT[0][:, :, 0:2], 0.0), writes=[("xnT", 0, "h")])

        def load(s):
            hs = s % 3
            for tt in range(2):
                T = s * 2 + tt
                S.op("sp", lambda h, tt=tt, T=T: h.dma_start(out=hres[hs][:, tt, :], in_=hin[T * 128:(T + 1) * 128, :]),
                     writes=[("hres", hs, tt)], dma=("hres", hs, tt))

        def prenorm_a(s):
            hs = s % 3
            for tt in range(2):
                T = s * 2 + tt
                col = T % 64
                S.op("act", lambda h, tt=tt, col=col: h.activation(out=k.junk[:], in_=hres[hs][:, tt, :], func=AF.Square,
                                                                   accum_out=ss2[:, col:col + 1]),
                     reads=[("hres", hs, tt)], writes=[("ss2", col)])
                rstd_from_ss(k, ss2[:, col:col + 1], ms2[:, col:col + 1], rs2[:, col:col + 1], k.nhalf[:, 0:1],
                             [("ss2", col)], ("ms2", col), ("rs2", col), 1.0 / D, 1e-6)
                S.op("dve", lambda h, tt=tt, col=col: h.scalar_tensor_tensor(
                    out=xn[tt][:], in0=hres[hs][:, tt, :], scalar=rs2[:, col:col + 1], in1=g2b[:], op0=ALU.mult, op1=ALU.mult),
                    reads=[("hres", hs, tt), ("rs2", col), "g2b"], writes=[("xn", tt)])

        def prenorm_b(s):
            xb = s % 2
            if s > 0:
                S.op("pool", lambda h: h.tensor_copy(out=xnT[xb][:, :, 0:2], in_=xnT[1 - xb][:, :, 256:258]),
                     reads=[("xnT", 1 - xb, 1)], writes=[("xnT", xb, "h")])
            for tt in range(2):
                for dc in range(8):
                    S.op("pe", lambda h, tt=tt, dc=dc: h.transpose(out=bank_bf16(banks[0], 8)[:, dc, :], in_=xn[tt][:, dc * 128:(dc + 1) * 128],
                                                                  identity=k.identb[:]),
                         reads=[("xn", tt), "identb"], writes=[("bank", 0)])
                S.op("act", lambda h, tt=tt: h.copy(out=xnT[xb][:, :, 2 + tt * 128:2 + (tt + 1) * 128], in_=bank_bf16(banks[0], 8)),
                     writes=[("bank", 0), ("xnT", xb, tt)])

        def stA(s, n):
            i, which = divmod(n, 2)
            fc = i + which * NFC
            bi = 1 + n % 3
            xb = s % 2
            for dc in range(8):
                S.op("pe", lambda h, fc=fc, dc=dc, bi=bi: h.matmul(
                    banks[bi][:, 0:258], lhsT=Wup[:, dc, fc * 128:(fc + 1) * 128], rhs=xnT[xb][:, dc, :], start=(dc == 0), stop=(dc == 7)),
                    reads=[("xnT", xb, 0), ("xnT", xb, 1), ("xnT", xb, "h")] + wup_keys, writes=[("bank", bi)])

        def stB(s, n):
            i, which = divmod(n, 2)
            fc = i + which * NFC
            bi = 1 + n % 3
            ci = n % NCB
            S.op("act", lambda h: h.activation(out=Cb[ci][:, :], in_=banks[bi][:, 2:258], func=AF.Identity,
                                               scale=cp[:, fc, 2:3], bias=cp[:, fc, 3:4]),
                 reads=["cp"], writes=[("bank", bi), ("Cb", ci)])

        def stC(s, n):
            i, which = divmod(n, 2)
            fc = i + which * NFC
            bi = 1 + n % 3
            ci = n % NCB
            S.op("dve", lambda h: h.scalar_tensor_tensor(
                out=Cb[ci][:, :], in0=banks[bi][:, 1:257], scalar=cp[:, fc, 1:2], in1=Cb[ci][:, :], op0=ALU.mult, op1=ALU.add),
                reads=["cp"], writes=[("bank", bi), ("Cb", ci)])
            S.op("dve", lambda h: h.scalar_tensor_tensor(
                out=Cb[ci][:, :], in0=banks[bi][:, 0:256], scalar=cp[:, fc, 0:1], in1=Cb[ci][:, :], op0=ALU.mult, op1=ALU.add),
                reads=["cp"], writes=[("bank", bi), ("Cb", ci)])

        def stD(s, n):
            i, which = divmod(n, 2)
            ci = n % NCB
            gi = i % 3
            if which == 0:
                S.op("act", lambda h: h.activation(out=Gb[gi][:, :], in_=Cb[ci][:, :], func=AF.Gelu),
                     reads=[("Cb", ci)], writes=[("Gb", gi)])
            else:
                S.op("pool", lambda h: h.tensor_tensor(out=gT[:, i, :], in0=Gb[gi][:, :], in1=Cb[ci][:, :], op=ALU.mult),
                     reads=[("Gb", gi), ("Cb", ci)], writes=[("gT", i)])

        def down_mm(s, tt):
            if True:
                pb = (4, 5) if tt == 0 else (6, 7)
                for half in range(2):
                    for i in range(NFC):
                        S.op("pe", lambda h, tt=tt, half=half, i=i, pb=pb: h.matmul(
                            banks[pb[half]][:, :], lhsT=gT[:, i, tt * 128:(tt + 1) * 128], rhs=Wdn[:, i, half * 512:(half + 1) * 512],
                            start=(i == 0), stop=(i == NFC - 1)),
                            reads=[("gT", i)] + wdn_keys, writes=[("bank", pb[half])])

        def down_post(s, tt):
            hs = s % 3
            if True:
                T = s * 2 + tt
                pb = (4, 5) if tt == 0 else (6, 7)
                postnorm_residual(k, [banks[pb[0]], banks[pb[1]]], [("bank", pb[0]), ("bank", pb[1])], g3b, "g3b",
                                  hres[hs][:, tt, :], ("hres", hs, tt), tmp, "tmp", (ss, ms, rs), T % 64)
                S.op("sp", lambda h, tt=tt, T=T: h.dma_start(out=hout[T * 128:(T + 1) * 128, :], in_=hres[hs][:, tt, :]),
                     reads=[("hres", hs, tt)], dma=("hout", hs, tt))

        load(0)
        load(1)
        prenorm_a(0)
        prenorm_b(0)
        for s in range(NS):
            if s + 2 < NS:
                load(s + 2)
            for t in range(NCH + 3):
                if t < NCH:
                    stA(s, t)
                if 0 <= t - 1 < NCH:
                    stB(s, t - 1)
                if 0 <= t - 2 < NCH:
                    stC(s, t - 2)
                if 0 <= t - 3 < NCH:
                    stD(s, t - 3)
            if s + 1 < NS:
                prenorm_a(s + 1)
            down_mm(s, 0)
            if s + 1 < NS:
                prenorm_b(s + 1)
            down_post(s, 0)
            down_mm(s, 1)
            down_post(s, 1)
        emit_block(k)


def phase3(k):
    nc, S = k.nc, k.S
    with ExitStack() as es:
        tg = k.newtag()
        sb = lambda n, s, d: es.enter_context(nc.sbuf_tensor(n + tg, s, d))
        banks = [es.enter_context(nc.psum_tensor("bk%d" % i + tg, [128, 512], F32)) for i in range(8)]
        Wq = sb("Wq", [128, 8, 1024], BF16)
        Wk = sb("Wk", [128, 8, 1024], BF16)
        Wv = sb("Wv", [128, 8, 1024], BF16)
        gqb = sb("gqb", [128, 1024], F32)
        gkb = sb("gkb", [128, 1024], F32)
        ht = [sb("ht%d" % i, [128, 1024], F32) for i in range(5)]
        xq = [sb("xq%d" % i, [128, 1024], BF16) for i in range(2)]
        xk = [sb("xk%d" % i, [128, 1024], BF16) for i in range(2)]
        xqT = [sb("xqT%d" % i, [128, 8, 512], BF16) for i in range(2)]
        xkT = [sb("xkT%d" % i, [128, 8, 512], BF16) for i in range(2)]
        Qs = [sb("Qs%d" % i, [128, 512], BF16) for i in range(4)]
        Vs = [sb("Vs%d" % i, [128, 1024], BF16) for i in range(2)]
        ss = sb("ss", [128, 128], F32)
        ms = sb("ms", [128, 128], F32)
        rs = sb("rs", [128, 128], F32)
        wq_keys = load_cast_w(k, Wq, "Wq", k.d["w_q_b"][0], 8, 1024, "Wq")
        wk_keys = load_cast_w(k, Wk, "Wk", k.d["w_k_shared"], 8, 1024, "Wk")
        wv_keys = load_cast_w(k, Wv, "Wv", k.d["w_v_shared"], 8, 1024, "Wv")
        S.op("sp", lambda h: h.dma_start(out=gqb[:], in_=k.d["norm_g"][1, 0, :].partition_broadcast(128)), writes=["gqb"], dma="gqb")
        S.op("sp", lambda h: h.dma_start(out=gkb[:], in_=k.d["kv_norm_g"].partition_broadcast(128)), writes=["gkb"], dma="gkb")
        hin = k.d["H2"]

        def a_load(T):
            sl = T % 5
            S.op("sp", lambda h, T=T, sl=sl: h.dma_start(out=ht[sl][:], in_=hin[T * 128:(T + 1) * 128, :]), writes=[("ht", sl)], dma=("ht", sl))

        def a_sq(T):
            sl = T % 5
            col = T % 128
            S.op("act", lambda h, sl=sl, col=col: h.activation(out=k.junk[:], in_=ht[sl][:], func=AF.Square, accum_out=ss[:, col:col + 1]),
                 reads=[("ht", sl)], writes=[("ss", col)])

        def a_rs(T):
            col = T % 128
            rstd_from_ss(k, ss[:, col:col + 1], ms[:, col:col + 1], rs[:, col:col + 1], k.nhalf[:, 0:1],
                         [("ss", col)], ("ms", col), ("rs", col), 1.0 / D, 1e-6)

        def a_xn(T):
            sl = T % 5
            col = T % 128
            xs = T % 2
            for (xb, gb, gk, nm) in ((xq, gqb, "gqb", "xq"), (xk, gkb, "gkb", "xk")):
                S.op("dve", lambda h, sl=sl, col=col, xb=xb, gb=gb, xs=xs: h.scalar_tensor_tensor(
                    out=xb[xs][:], in0=ht[sl][:], scalar=rs[:, col:col + 1], in1=gb[:], op0=ALU.mult, op1=ALU.mult),
                    reads=[("ht", sl), ("rs", col), gk], writes=[(nm, xs)])

        def a_tr(T):
            st, t4 = divmod(T, 4)
            xs = T % 2
            for (xb, xT, nm, bi) in ((xq, xqT, "xq", 0), (xk, xkT, "xk", 1)):
                for dc in range(8):
                    S.op("pe", lambda h, dc=dc, xb=xb, bi=bi, xs=xs: h.transpose(out=bank_bf16(banks[bi], 8)[:, dc, :], in_=xb[xs][:, dc * 128:(dc + 1) * 128],
                                                                               identity=k.identb[:]),
                         reads=[(nm, xs), "identb"], writes=[("bank", bi)])
                S.op("act", lambda h, t4=t4, xT=xT, bi=bi, st=st: h.copy(out=xT[st % 2][:, :, t4 * 128:(t4 + 1) * 128], in_=bank_bf16(banks[bi], 8)),
                     writes=[("bank", bi), (nm + "T", st % 2, t4)])

        pc = [0]

        def a_proj(st, q):
            for gi in range(6 * q, 6 * q + 6):
                bi = 2 + pc[0] % 4
                qi = pc[0] % 4
                pc[0] += 1
                if gi < 16:
                    if gi < 8:
                        W, wkeys, xT, nm, dst, scale, hd = Wq, wq_keys, xqT, "xq", k.d["QT1"], 0.125, gi
                    else:
                        W, wkeys, xT, nm, dst, scale, hd = Wk, wk_keys, xkT, "xk", k.d["KT1"], None, gi - 8
                    xkeys = [(nm + "T", st % 2, t4) for t4 in range(4)]
                    for dc in range(8):
                        S.op("pe", lambda h, hd=hd, dc=dc, bi=bi, W=W, xT=xT: h.matmul(
                            banks[bi][:, :], lhsT=W[:, dc, hd * 128:(hd + 1) * 128], rhs=xT[st % 2][:, dc, :], start=(dc == 0), stop=(dc == 7)),
                            reads=xkeys + wkeys, writes=[("bank", bi)])
                    if scale is not None:
                        S.op("act", lambda h, bi=bi, qi=qi: h.activation(out=Qs[qi][:, :], in_=banks[bi][:, :], func=AF.Copy, scale=0.125),
                             writes=[("bank", bi), ("Qs", qi)])
                    else:
                        S.op("dve", lambda h, bi=bi, qi=qi: h.tensor_copy(out=Qs[qi][:, :], in_=banks[bi][:, :]),
                             writes=[("bank", bi), ("Qs", qi)])
                    S.op("sp", lambda h, hd=hd, qi=qi, dst=dst: h.dma_start(out=dst[hd, :, st * 512:(st + 1) * 512], in_=Qs[qi][:, :]),
                         reads=[("Qs", qi)], dma=("Qs", qi))
                else:
                    t4, half = divmod(gi - 16, 2)
                    T = st * 4 + t4
                    vs = T % 2
                    for dc in range(8):
                        S.op("pe", lambda h, t4=t4, half=half, dc=dc, bi=bi: h.matmul(
                            banks[bi][:, :], lhsT=xkT[st % 2][:, dc, t4 * 128:(t4 + 1) * 128], rhs=Wv[:, dc, half * 512:(half + 1) * 512],
                            start=(dc == 0), stop=(dc == 7)),
                            reads=[("xkT", st % 2, t4)] + wv_keys, writes=[("bank", bi)])
                    if half == 0:
                        S.op("act", lambda h, bi=bi, vs=vs: h.copy(out=Vs[vs][:, 0:512], in_=banks[bi][:, :]), writes=[("bank", bi), ("Vs", vs, 0)])
                    else:
                        S.op("dve", lambda h, bi=bi, vs=vs: h.tensor_copy(out=Vs[vs][:, 512:1024], in_=banks[bi][:, :]), writes=[("bank", bi), ("Vs", vs, 1)])
                        S.op("sp", lambda h, T=T, vs=vs: h.dma_start(out=k.d["V1"][T * 128:(T + 1) * 128, :], in_=Vs[vs][:, :]),
                             reads=[("Vs", vs, 0), ("Vs", vs, 1)], dma=("Vs", vs))

        for u in range(-4, NT + 4):
            if 0 <= u + 4 < NT:
                a_load(u + 4)
            if 0 <= u + 3 < NT:
                a_sq(u + 3)
            if 0 <= u + 2 < NT:
                a_rs(u + 2)
            if 0 <= u + 1 < NT:
                a_xn(u + 1)
            if 0 <= u < NT:
                a_tr(u)
            if u >= 4:
                a_proj((u - 4) // 4, (u - 4) % 4)
        emit_block(k)


def phase4(k, heads=range(8)):
    nc, S = k.nc, k.S
    NSB = 4
    SK = 3
    with ExitStack() as es:
        tg = k.newtag()
        sb = lambda n, s, d: es.enter_context(nc.sbuf_tensor(n + tg, s, d))
        banks = [es.enter_context(nc.psum_tensor("bk%d" % i + tg, [128, 512], F32)) for i in range(8)]
        KTh = [sb("KTh%d" % i, [128, SEQ], BF16) for i in range(2)]
        QZ = [[sb("QZ%d_%d" % (i, c), [128, SEQ], BF16) for c in range(2)] for i in range(2)]
        Vh = [sb("Vh%d" % i, [128, NT, 129], BF16) for i in range(2)]
        Tb = [sb("Tb%d" % i, [128, TBW], F32) for i in range(2)]
        mB = sb("mB", [128, 128], F32)
        Sb = [sb("Sb%d" % i, [128, 512], F32) for i in range(NSB)]
        PT = [sb("PT%d" % i, [128, 512], BF16) for i in range(NSB)]
        Ost = [[sb("Ost%d_%d" % (i, c), [128, 4, 129], F32) for c in range(2)] for i in range(2)]
        Oh = [sb("Oh%d" % i, [128, NT, 128], BF16) for i in range(2)]
        lv = [sb("lv%d" % i, [128, 64], F32) for i in range(4)]
        lsum = sb("lsum", [128, 4], F32)
        lam = sb("lam", [128, 1], F32)
        sgb = sb("sgb", [128, 128], F32)
        rd = sb("rd", [128, 64, 8], F32)
        av_all = [sb("av%d" % i, [128, 128], F32) for i in range(8)]
        tv_all = [sb("tv%d" % i, [128, 128], F32) for i in range(8)]
        st1 = sb("st1", [128, 128], F32)
        st2 = sb("st2", [128, 128], F32)
        st3 = sb("st3", [128, 128], F32)

        for i, nm in enumerate(("lam_q1", "lam_k1", "lam_q2", "lam_k2")):
            S.op("sp", lambda h, i=i, nm=nm: h.dma_start(out=lv[i][:], in_=k.d[nm][0, :].partition_broadcast(128)), writes=[("lv", i)], dma=("lv", i))
        S.op("sp", lambda h: h.dma_start(out=sgb[:], in_=k.d["subln_g"][0, :].partition_broadcast(128)), writes=["sgb"], dma="sgb")
        S.op("sp", lambda h: h.dma_start(out=mB[:], in_=k.d["maskB"]), writes=["mB"], dma="mB")
        S.op("dve", lambda h: h.tensor_scalar(out=sgb[:], in0=sgb[:], scalar1=1.0 - LAMBDA_INIT, scalar2=None, op0=ALU.mult),
             reads=["sgb"], writes=["sgb"])
        for i in range(2):
            S.op("dve", lambda h, i=i: h.scalar_tensor_tensor(out=k.junk[:, 0:64], in0=lv[2 * i][:], scalar=1.0, in1=lv[2 * i + 1][:],
                                                            op0=ALU.mult, op1=ALU.mult, accum_out=lsum[:, i:i + 1]),
                 reads=[("lv", 2 * i), ("lv", 2 * i + 1)], writes=[("lsum", i)])
            S.op("act", lambda h, i=i: h.activation(out=lsum[:, 2 + i:3 + i], in_=lsum[:, i:i + 1], func=AF.Exp),
                 reads=[("lsum", i)], writes=[("lsum", 2 + i)])
        S.op("dve", lambda h: h.tensor_tensor(out=lam[:], in0=lsum[:, 2:3], in1=lsum[:, 3:4], op=ALU.subtract),
             reads=[("lsum", 2), ("lsum", 3)], writes=["lam"])
        S.op("dve", lambda h: h.tensor_scalar(out=lam[:], in0=lam[:], scalar1=LAMBDA_INIT, scalar2=None, op0=ALU.add),
             reads=["lam"], writes=["lam"])
        for i in range(2):
            S.op("pool", lambda h, i=i: h.memset(Vh[i][:, :, 128:129], 1.0), writes=[("Vh1", i)])
            S.op("pool", lambda h, i=i: h.memset(QZ[i][0][64:128, :], 0.0), writes=[("QZz", i, 0)])
            S.op("pool", lambda h, i=i: h.memset(QZ[i][1][0:64, :], 0.0), writes=[("QZz", i, 1)])

        def load_head(hi, hd, t_now=None):
            b = hi % 2
            S.op("sp", lambda h: h.dma_start(out=Tb[b][:], in_=k.d["biasB"][hd]), writes=[("Tb", b)], dma=("Tb", b))

            def mask_add():
                S.op("dve", lambda h: h.tensor_tensor(out=Tb[b][:, 0:128], in0=Tb[b][:, 0:128], in1=mB[:], op=ALU.add),
                     reads=["mB", ("Tb", b)], writes=[("Tb", b)])
            if t_now is None:
                mask_add()
            else:
                defer(t_now + 80, mask_add)
            S.op("sp", lambda h: h.dma_start(out=KTh[b][:], in_=k.d["KT1"][hd]), writes=[("KTh", b)], dma=("KTh", b))
            S.op("sp", lambda h: h.dma_start(out=QZ[b][0][0:64, :], in_=k.d["QT1"][hd, 0:64, :]), writes=[("QZ", b, 0)], dma=("QZ", b, 0))
            S.op("sp", lambda h: h.dma_start(out=QZ[b][1][64:128, :], in_=k.d["QT1"][hd, 64:128, :]), writes=[("QZ", b, 1)], dma=("QZ", b, 1))
            S.op("sp", lambda h: h.dma_start(out=Vh[b][:, :, 0:128], in_=k.d["V1"][:, hd * 128:(hd + 1) * 128].rearrange("(t p) e -> p t e", p=128)),
                 writes=[("Vh", b)], dma=("Vh", b))

        obank = (4, 5, 6, 7)
        heads = list(heads)
        items = []
        for hi, hd in enumerate(heads):
            for qs in range(8):
                for c in range(2):
                    for j in range(4 * qs + 4):
                        items.append((hi, hd, qs, c, j))
        N = len(items)
        deferred = {}

        def defer(t, fn):
            deferred.setdefault(t, []).append(fn)

        def geom(it):
            hi, hd, qs, c, j = it
            qt0 = max(0, j - 4 * qs)
            col0 = qt0 * 128
            ncols = 512 - col0
            delta = qs * 512 - j * 128 + col0
            off = min(delta, TB_CONST)
            assert 0 <= off and off + ncols <= TBW
            return qt0, col0, ncols, off

        def stA(n):
            hi, hd, qs, c, j = items[n]
            b = hi % 2
            qt0, col0, ncols, off = geom(items[n])
            bi = n % 4
            S.op("pe", lambda h: h.matmul(
                banks[bi][:, col0:512], lhsT=KTh[b][:, j * 128:(j + 1) * 128], rhs=QZ[b][c][:, qs * 512 + col0:(qs + 1) * 512],
                start=True, stop=True),
                reads=[("KTh", b), ("QZ", b, c), ("QZz", b, c)], writes=[("bank", bi)])

        def stB(n):
            hi, hd, qs, c, j = items[n]
            b = hi % 2
            qt0, col0, ncols, off = geom(items[n])
            bi = n % 4
            si = n % NSB
            if off == TB_CONST:
                return
            S.op("dve", lambda h: h.tensor_tensor(
                out=Sb[si][:, col0:512], in0=banks[bi][:, col0:512], in1=Tb[b][:, off:off + ncols], op=ALU.add),
                reads=[("Tb", b)], writes=[("bank", bi), ("Sb", si)])

        def stC(n):
            qt0, col0, ncols, off = geom(items[n])
            si = n % NSB
            if off == TB_CONST:
                b = items[n][0] % 2
                bi = n % 4
                S.op("act", lambda h: h.activation(out=PT[si][:, col0:512], in_=banks[bi][:, col0:512], func=AF.Exp,
                                                   bias=Tb[b][:, TBW - 1:TBW]),
                     reads=[("Tb", b)], writes=[("bank", bi), ("PT", si)])
                return
            S.op("act", lambda h: h.activation(out=PT[si][:, col0:512], in_=Sb[si][:, col0:512], func=AF.Exp),
                 reads=[("Sb", si)], writes=[("PT", si)])

        def evac(hi, qs, c, qt):
            osl = qs % 2
            if False:
                S.op("act", lambda h: h.copy(out=Ost[osl][c][:, qt, :], in_=banks[obank[qt]][:, 0:129]),
                     writes=[("bank", obank[qt]), ("Ost", osl, c, qt)])
            else:
                S.op("dve", lambda h: h.tensor_copy(out=Ost[osl][c][:, qt, :], in_=banks[obank[qt]][:, 0:129]),
                     writes=[("bank", obank[qt]), ("Ost", osl, c, qt)])

        cc = [0]

        def combine_parts(hi, hd, qs):
            b = hi % 2
            osl = qs % 2
            ci = cc[0] % 64
            cc[0] += 1
            cb = (ci % 32) * 4
            av = av_all[(ci % 2) * 4:(ci % 2) * 4 + 4]
            tv = tv_all[(ci % 2) * 4:(ci % 2) * 4 + 4]
            ab = (ci % 2) * 4
            okeys = [("Ost", osl, c_, qt) for c_ in range(2) for qt in range(4)]

            def c1():
                S.op("dve", lambda h: h.reciprocal(out=rd[:, ci, 0:4], in_=Ost[osl][0][:, :, 128]), reads=okeys, writes=[("rd", ci, 0)])
                S.op("dve", lambda h: h.reciprocal(out=rd[:, ci, 4:8], in_=Ost[osl][1][:, :, 128]), reads=okeys, writes=[("rd", ci, 1)])
                S.op("dve", lambda h: h.tensor_scalar(out=rd[:, ci, 4:8], in0=rd[:, ci, 4:8], scalar1=lam[:, 0:1], scalar2=None, op0=ALU.mult),
                     reads=[("rd", ci, 1), "lam"], writes=[("rd", ci, 1)])

            def c2():
                for qt in range(4):
                    S.op("pool", lambda h, qt=qt: h.tensor_scalar(out=tv[qt][:], in0=Ost[osl][1][:, qt, 0:128], scalar1=rd[:, ci, 4 + qt:5 + qt],
                                                                  scalar2=0.0, op0=ALU.mult, op1=ALU.add),
                         reads=okeys + [("rd", ci, 1)], writes=[("tv", ab + qt)])
                    S.op("pool", lambda h, qt=qt: h.tensor_scalar(out=av[qt][:], in0=Ost[osl][0][:, qt, 0:128], scalar1=rd[:, ci, qt:qt + 1],
                                                                  scalar2=0.0, op0=ALU.mult, op1=ALU.add),
                         reads=okeys + [("rd", ci, 0)], writes=[("av", ab + qt)])
                    S.op("pool", lambda h, qt=qt: h.tensor_tensor(out=av[qt][:], in0=av[qt][:], in1=tv[qt][:], op=ALU.subtract),
                         reads=[("tv", ab + qt)], writes=[("av", ab + qt)])

            def c3():
                for qt in range(4):
                    S.op("dve", lambda h, qt=qt: h.scalar_tensor_tensor(
                        out=k.junk[:, 0:128], in0=av[qt][:], scalar=1.0, in1=av[qt][:], op0=ALU.mult, op1=ALU.mult,
                        accum_out=st1[:, cb + qt:cb + qt + 1]),
                        reads=[("av", ab + qt)], writes=[("st1", cb + qt)])
                S.op("dve", lambda h: h.tensor_scalar(out=st2[:, cb:cb + 4], in0=st1[:, cb:cb + 4], scalar1=1.0 / 128, scalar2=1e-5,
                                                      op0=ALU.mult, op1=ALU.add),
                     reads=[("st1", cb + qt) for qt in range(4)], writes=[("st2", cb)])

            def c4():
                S.op("pool", lambda h: h.tensor_tensor(out=st3[:, cb:cb + 4], in0=st2[:, cb:cb + 4], in1=k.nhalf[:, 0:4], op=ALU.pow),
                     reads=[("st2", cb)], writes=[("st3", cb)])
                for qt in range(4):
                    T = qs * 4 + qt
                    S.op("pool", lambda h, qt=qt: h.tensor_scalar(out=av[qt][:], in0=av[qt][:], scalar1=st3[:, cb + qt:cb + qt + 1],
                                                                  scalar2=0.0, op0=ALU.mult, op1=ALU.add),
                         reads=[("st3", cb)], writes=[("av", ab + qt)])
                    S.op("pool", lambda h, qt=qt, T=T: h.tensor_tensor(out=Oh[b][:, T, :], in0=av[qt][:], in1=sgb[:], op=ALU.mult),
                         reads=[("av", ab + qt), "sgb"], writes=[("Oh", b, T)])
                if qs == 7:
                    S.op("sp", lambda h: h.dma_start(out=k.d["O1"][:, hd * 128:(hd + 1) * 128].rearrange("(t p) e -> p t e", p=128), in_=Oh[b][:]),
                         reads=[("Oh", b, T) for T in range(NT)], dma=("Oh", b))

            return c1, c2, c3, c4

        def stD(n, t):
            hi, hd, qs, c, j = items[n]
            b = hi % 2
            qt0, col0, ncols, off = geom(items[n])
            si = n % NSB
            for qt in range(qt0, 4):
                last = (j == 4 * qs + qt)
                S.op("pe", lambda h, qt=qt, last=last: h.matmul(
                    banks[obank[qt]][:, 0:129], lhsT=PT[si][:, qt * 128:(qt + 1) * 128], rhs=Vh[b][:, j, :],
                    start=(j == 0), stop=last),
                    reads=[("PT", si), ("Vh", b), ("Vh1", b)], writes=[("bank", obank[qt])])
                if last:
                    defer(t + 1, lambda hi=hi, qs=qs, c=c, qt=qt: evac(hi, qs, c, qt))
            if c == 1 and j == 4 * qs + 3:
                parts = combine_parts(hi, hd, qs)
                for pi, dt_ in enumerate((4, 6, 12, 15)):
                    defer(t + dt_, parts[pi])

        load_head(0, heads[0])
        for t in range(N + SK + 20):
            for fn in deferred.pop(t, []):
                fn()
            if t < N:
                it = items[t]
                if it[2] == 0 and it[3] == 1 and it[4] == 0 and it[0] + 1 < len(heads):
                    load_head(it[0] + 1, heads[it[0] + 1], t)
                stA(t)
            if 0 <= t - 1 < N:
                stB(t - 1)
            if 0 <= t - 2 < N:
                stC(t - 2)
            if 0 <= t - SK < N:
                stD(t - SK, t)
        assert not deferred
        emit_block(k)


ALL_PHASES = ("p1", "p2a", "ffn0", "p3", "p4", "p5a", "ffn1")


def build(phases=ALL_PHASES, kinds=None):
    kinds = kinds or {}
    nc = bass.Bass("TRN2", target_bir_lowering=False)
    k = K()
    k.nc = nc
    d = {}

    def dt(name, shape, dtype, kind):
        d[name] = nc.dram_tensor(name, list(shape), dtype, kind=kinds.get(name, kind)).ap()

    EI = "ExternalInput"
    dt("x", [SEQ, D], F32, EI)
    dt("norm_g", [2, 4, D], F32, EI)
    dt("w_in_a", [1, D, 4608], F32, EI)
    dt("w_out_a", [1, 512, D], F32, EI)
    dt("kv_norm_g", [D], F32, EI)
    dt("w_k_shared", [D, D], F32, EI)
    dt("w_v_shared", [D, D], F32, EI)
    dt("w_q_b", [1, D, D], F32, EI)
    for nm in ("lam_q1", "lam_k1", "lam_q2", "lam_k2"):
        dt(nm, [1, 64], F32, EI)
    dt("subln_g", [1, 128], F32, EI)
    dt("w_out_b", [1, D, D], F32, EI)
    dt("w_up", [2, D, 2 * DFF], F32, EI)
    dt("w_down", [2, DFF, D], F32, EI)
    dt("convp", [2, 128, 2 * NFC, 4], F32, EI)
    dt("ident", [128, 128], F32, EI)
    dt("biasA", [3, NH, 128, 256], F32, EI)
    dt("maskA", [128, 256], F32, EI)
    dt("biasB", [NH, 128, TBW], F32, EI)
    dt("maskB", [128, 128], F32, EI)
    for g in range(3):
        dt("U%d" % g, [SEQ, 520], F32, "Internal")
    dt("H1", [SEQ, D], F32, "Internal")
    dt("H2", [SEQ, D], F32, "Internal")
    dt("QT1", [NH, 128, SEQ], BF16, "Internal")
    dt("KT1", [NH, 128, SEQ], BF16, "Internal")
    dt("V1", [SEQ, D], BF16, "Internal")
    dt("O1", [SEQ, D], BF16, "Internal")
    dt("out", [SEQ, D], F32, "ExternalOutput")
    k.d = d

    with ExitStack() as es:
        k.S = Sched(nc, es)
        S = k.S
        identf = es.enter_context(nc.sbuf_tensor("identf", [128, 128], F32))
        k.identb = es.enter_context(nc.sbuf_tensor("identb", [128, 128], BF16))
        k.nhalf = es.enter_context(nc.sbuf_tensor("nhalf", [128, 8], F32))
        k.junk = es.enter_context(nc.sbuf_tensor("junkg", [128, 1024], BF16))
        S.op("sp", lambda h: h.dma_start(out=identf[:], in_=d["ident"]), writes=["identf"], dma="identf")
        S.op("dve", lambda h: h.tensor_copy(out=k.identb[:], in_=identf[:]), reads=["identf"], writes=["identb"])
        S.op("dve", lambda h: h.memset(k.nhalf[:], -0.5), writes=["nhalf"])
        emit_block(k)
        for ph in phases:
            if ph == "p1":
                phase1(k)
            elif ph.startswith("p1g"):
                phase1(k, groups=(int(ph[3]),))
            elif ph == "p2a":
                if "ffn0" in phases:
                    k.wes0 = ExitStack()
                    k.Wup0 = k.wes0.enter_context(nc.sbuf_tensor("WupP0", [128, 8, 2 * DFF], BF16))
                    phase_outproj(k, 0, prefetch=(k.Wup0, d["w_up"][0]))
                else:
                    phase_outproj(k, 0)
            elif ph == "ffn0":
                if getattr(k, "Wup0", None) is not None:
                    phase_ffn(k, 0, d["H1"], d["H2"], Wup_pre=k.Wup0)
                    k.wes0.close()
                else:
                    phase_ffn(k, 0, d["H1"], d["H2"])
            elif ph == "p3":
                phase3(k)
            elif ph == "p4":
                phase4(k)
            elif ph.startswith("p4h"):
                phase4(k, heads=[int(ph[3])])
            elif ph == "p5a":
                if "ffn1" in phases:
                    k.wes1 = ExitStack()
                    k.Wup1 = k.wes1.enter_context(nc.sbuf_tensor("WupP1", [128, 8, 2 * DFF], BF16))
                    phase_outproj(k, 1, prefetch=(k.Wup1, d["w_up"][1]))
                else:
                    phase_outproj(k, 1)
            elif ph == "ffn1":
                if getattr(k, "Wup1", None) is not None:
                    phase_ffn(k, 1, d["H1"], d["out"], Wup_pre=k.Wup1)
                    k.wes1.close()
                else:
                    phase_ffn(k, 1, d["H1"], d["out"])
    return nc


def _bucket(n):
    n = np.maximum(n, 0)
    nf = np.maximum(n, 1).astype(np.float32)
    large = 16 + (np.log(nf / np.float32(16)) / np.float32(math.log(2048 / 16)) * np.float32(16)).astype(np.int32)
    large = np.minimum(large, 31)
    return np.where(n < 16, n, large).astype(np.int64)


def host_consts(rel_bias_table, conv_w, conv_b):
    tab = np.asarray(rel_bias_table, dtype=np.float32)
    kl = np.arange(128)[:, None]
    qc = np.arange(256)[None, :]
    du = qc - kl
    biasA = np.stack([np.take(tab, _bucket(r * np.clip(du, 0, 128)), axis=1) for r in DIL]).astype(np.float32)
    maskA = np.where((du >= 0) & (du <= 128), np.float32(0.0), np.float32(NEG)).astype(np.float32)
    jj = np.arange(TBW)[None, :]
    pp = np.arange(128)[:, None]
    biasB = np.take(tab, _bucket(np.maximum(jj - pp, 0)), axis=1).astype(np.float32)
    maskB = np.where(jj[:, :128] - pp >= 0, np.float32(0.0), np.float32(NEG)).astype(np.float32)
    cw = np.asarray(conv_w, dtype=np.float32)
    cb = np.asarray(conv_b, dtype=np.float32)
    convp = np.concatenate([cw, cb[:, None, :]], axis=1)
    convp = np.ascontiguousarray(convp.reshape(2, 4, 2 * NFC, 128).transpose(0, 3, 2, 1))
    return dict(biasA=np.ascontiguousarray(biasA), maskA=maskA, biasB=np.ascontiguousarray(biasB), maskB=np.ascontiguousarray(maskB),
                convp=convp, ident=np.eye(128, dtype=np.float32))


_NC_CACHE = {}


def kernel(x, rel_bias_table, norm_g, w_in_a, w_out_a, kv_norm_g, w_k_shared, w_v_shared, w_q_b,
           lam_q1, lam_k1, lam_q2, lam_k2, subln_g, w_out_b, w_up, conv_w, conv_b, w_down):
    f = lambda a: np.ascontiguousarray(np.asarray(a, dtype=np.float32))
    hc = host_consts(rel_bias_table, conv_w, conv_b)
    shared = dict(norm_g=f(norm_g), w_in_a=f(w_in_a), w_out_a=f(w_out_a), kv_norm_g=f(kv_norm_g), w_k_shared=f(w_k_shared),
                  w_v_shared=f(w_v_shared), w_q_b=f(w_q_b), lam_q1=f(lam_q1), lam_k1=f(lam_k1), lam_q2=f(lam_q2), lam_k2=f(lam_k2),
                  subln_g=f(subln_g), w_out_b=f(w_out_b), w_up=f(w_up), w_down=f(w_down), **hc)
    x = f(x)
    if "nc" not in _NC_CACHE:
        _NC_CACHE["nc"] = build()
    nc = _NC_CACHE["nc"]
    in_maps = [dict(shared, x=x[b]) for b in range(8)]
    res = run_bass_kernel_spmd(nc, in_maps, core_ids=list(range(8)))
    return np.stack([np.asarray(res.results[b]["out"], dtype=np.float32) for b in range(8)], axis=0)
```
